# Optimizing a Trainium2 kernel written in Bass

```python
import math
import jax
import jax.numpy as jnp
from jax import lax
import numpy as np

D_MODEL = 1024
BATCH = 8
SEQ = 2048
DEPTH = 2

N_META = 16
MIX_WIDTH = 512
N_BRANCH = 3
N_Q_HEADS = 8
N_KV_HEADS = 2
HEAD_DIM = 64
Q_GROUP = N_Q_HEADS // N_KV_HEADS
WINDOW = 128
BLOCK = 128
ROPE_THETA = 10000.0
SSM_GROUP_SIZE = 16
SSM_GROUPS = MIX_WIDTH // SSM_GROUP_SIZE
SSM_STATE = 64
DT_MIN = 1e-3
DT_MAX = 1e-1
POOL_WINDOWS = (2, 4, 8, 16)
POOL_GROUP = MIX_WIDTH // len(POOL_WINDOWS)
D_FF = 4 * D_MODEL
EPS = 1e-6
NEG_INF = -1e30

Q_W = N_Q_HEADS * HEAD_DIM
KV_W = N_KV_HEADS * HEAD_DIM
OFF_Q = 0
OFF_K = OFF_Q + Q_W
OFF_V = OFF_K + KV_W
OFF_SSM = OFF_V + KV_W
OFF_POOL = OFF_SSM + MIX_WIDTH
OFF_GATE = OFF_POOL + MIX_WIDTH
D_IN = OFF_GATE + N_BRANCH * D_MODEL

kernel_name = "hybrid_gated_swa_s5_pool_encoder"


def rms_norm(x, gain):
    xf = x.astype(jnp.float32)
    y = xf * lax.rsqrt(jnp.mean(xf * xf, axis=-1, keepdims=True) + EPS)
    return (y * gain.astype(jnp.float32)).astype(x.dtype)


def apply_rope(x, pos):
    half = HEAD_DIM // 2
    inv_freq = ROPE_THETA ** (-jnp.arange(half, dtype=jnp.float32) * 2.0 / HEAD_DIM)
    ang = pos[:, None] * inv_freq[None, :]
    cos = jnp.cos(ang)[None, :, None, :]
    sin = jnp.sin(ang)[None, :, None, :]
    xf = x.astype(jnp.float32)
    x1, x2 = xf[..., :half], xf[..., half:]
    out = jnp.concatenate([x1 * cos - x2 * sin, x2 * cos + x1 * sin], axis=-1)
    return out.astype(x.dtype)


def windowed_gqa_attention(q, k, v, sink):
    b, L = q.shape[0], q.shape[1]
    front = BLOCK - N_META
    Lp = L + front
    nb = Lp // BLOCK
    qb = jnp.pad(q, ((0, 0), (front, 0), (0, 0), (0, 0)))
    qb = qb.reshape(b, nb, BLOCK, N_KV_HEADS, Q_GROUP, HEAD_DIM)

    def band(t):
        tp = jnp.pad(t, ((0, 0), (front + BLOCK, BLOCK), (0, 0), (0, 0)))
        tp = tp.reshape(b, nb + 2, BLOCK, N_KV_HEADS, HEAD_DIM)
        return jnp.concatenate([tp[:, :-2], tp[:, 1:-1], tp[:, 2:]], axis=2)

    k_band, v_band = band(k), band(v)
    k_meta, v_meta = k[:, :N_META], v[:, :N_META]
    q_pos = jnp.arange(nb)[:, None] * BLOCK + jnp.arange(BLOCK)[None, :]
    k_pos = (jnp.arange(nb)[:, None] - 1) * BLOCK + jnp.arange(3 * BLOCK)[None, :]
    valid = ((k_pos[:, None, :] >= BLOCK) & (k_pos[:, None, :] < Lp)
             & (jnp.abs(q_pos[:, :, None] - k_pos[:, None, :]) <= WINDOW))
    scale = HEAD_DIM ** -0.5
    s_band = jnp.einsum("bnqhgd,bnkhd->bnhgqk", qb, k_band).astype(jnp.float32) * scale
    s_band = jnp.where(valid[None, :, None, None], s_band, NEG_INF)
    s_meta = jnp.einsum("bnqhgd,bmhd->bnhgqm", qb, k_meta).astype(jnp.float32) * scale
    s_sink = jnp.broadcast_to(
        sink.astype(jnp.float32).reshape(1, 1, N_KV_HEADS, Q_GROUP, 1, 1),
        s_meta.shape[:-1] + (1,))
    p = jax.nn.softmax(jnp.concatenate([s_band, s_meta, s_sink], axis=-1), axis=-1)
    nk = 3 * BLOCK
    p_band = p[..., :nk].astype(v.dtype)
    p_meta = p[..., nk:nk + N_META].astype(v.dtype)
    out = (jnp.einsum("bnhgqk,bnkhd->bnqhgd", p_band, v_band)
           + jnp.einsum("bnhgqm,bmhd->bnqhgd", p_meta, v_meta))
    return out.reshape(b, Lp, Q_W)[:, front:]


def _ssm_combine(left, right):
    a1r, a1i, b1r, b1i = left
    a2r, a2i, b2r, b2i = right
    ar = a2r * a1r - a2i * a1i
    ai = a2r * a1i + a2i * a1r
    br = a2r * b1r - a2i * b1i + b2r
    bi = a2r * b1i + a2i * b1r + b2i
    return ar, ai, br, bi


def bidirectional_s5(u, lam_re, lam_im, log_dt, b_re, b_im, c_re, c_im, d_skip, glu_w, glu_b):
    f32 = jnp.float32
    b, L = u.shape[0], u.shape[1]
    uf = u.astype(f32).reshape(b, L, SSM_GROUPS, SSM_GROUP_SIZE)
    y = d_skip.astype(f32).reshape(SSM_GROUPS, SSM_GROUP_SIZE) * uf
    for direction, reverse in ((0, False), (1, True)):
        lr = lam_re[direction].astype(f32)
        li = lam_im[direction].astype(f32)
        dt = jnp.exp(log_dt[direction].astype(f32))[:, None]
        mag = jnp.exp(lr * dt)
        abar_re = mag * jnp.cos(li * dt)
        abar_im = mag * jnp.sin(li * dt)
        den = lr * lr + li * li
        num_re = abar_re - 1.0
        f_re = (num_re * lr + abar_im * li) / den
        f_im = (abar_im * lr - num_re * li) / den
        br = b_re[direction].astype(f32)
        bi = b_im[direction].astype(f32)
        bbar_re = f_re[..., None] * br - f_im[..., None] * bi
        bbar_im = f_re[..., None] * bi + f_im[..., None] * br
        bu_re = jnp.einsum("blgp,gnp->blgn", uf, bbar_re)
        bu_im = jnp.einsum("blgp,gnp->blgn", uf, bbar_im)
        a_re = jnp.broadcast_to(abar_re, bu_re.shape)
        a_im = jnp.broadcast_to(abar_im, bu_im.shape)
        _, _, x_re, x_im = lax.associative_scan(
            _ssm_combine, (a_re, a_im, bu_re, bu_im), reverse=reverse, axis=1)
        y = (y + jnp.einsum("blgn,gpn->blgp", x_re, c_re[direction].astype(f32))
             - jnp.einsum("blgn,gpn->blgp", x_im, c_im[direction].astype(f32)))
    z = jax.nn.gelu(y.reshape(b, L, MIX_WIDTH), approximate=False)
    gate = jax.nn.sigmoid(z @ glu_w.astype(f32) + glu_b.astype(f32))
    return (z * gate).astype(u.dtype)


def multiscale_pool(u, pool_w, pool_scale):
    f32 = jnp.float32
    b, L = u.shape[0], u.shape[1]
    uf = u.astype(f32)
    cs = jnp.concatenate([jnp.zeros((b, 1, MIX_WIDTH), f32), jnp.cumsum(uf, axis=1)], axis=1)
    idx = jnp.arange(L)
    outs = []
    for gi, w in enumerate(POOL_WINDOWS):
        sl = slice(gi * POOL_GROUP, (gi + 1) * POOL_GROUP)
        lo = jnp.clip(idx - w // 2, 0, L)
        hi = jnp.clip(idx + w // 2, 0, L)
        cnt = (hi - lo).astype(f32)[None, :, None]
        csg = cs[..., sl]
        mean = (jnp.take(csg, hi, axis=1) - jnp.take(csg, lo, axis=1)) / cnt
        outs.append((mean - uf[..., sl]) @ pool_w[gi].astype(f32))
    return (jnp.concatenate(outs, axis=-1) * pool_scale.astype(f32)).astype(u.dtype)


def setup_inputs(seed: int = 0) -> dict:
    key = jax.random.key(seed)
    ks = jax.random.split(key, 32)
    f32 = jnp.float32

    def nrm(k, shape, scale):
        return jax.random.normal(k, shape, f32) * scale

    G, N, P = SSM_GROUPS, SSM_STATE, SSM_GROUP_SIZE
    n_idx = jnp.arange(N, dtype=f32)
    return {
        "x": nrm(ks[0], (BATCH, SEQ, D_MODEL), 1.0),
        "meta_tokens": nrm(ks[1], (N_META, D_MODEL), 1.0),
        "norm_mix": 1.0 + nrm(ks[2], (DEPTH, D_MODEL), 0.02),
        "w_in": nrm(ks[3], (DEPTH, D_MODEL, D_IN), D_MODEL ** -0.5),
        "attn_sink": nrm(ks[4], (DEPTH, N_Q_HEADS), 0.5),
        "ssm_lam_re": -0.5 + nrm(ks[5], (DEPTH, 2, G, N), 0.01),
        "ssm_lam_im": jnp.pi * n_idx + nrm(ks[6], (DEPTH, 2, G, N), 0.01),
        "ssm_log_dt": jax.random.uniform(ks[7], (DEPTH, 2, G), f32, math.log(DT_MIN), math.log(DT_MAX)),
        "ssm_b_re": nrm(ks[8], (DEPTH, 2, G, N, P), (2 * P) ** -0.5),
        "ssm_b_im": nrm(ks[9], (DEPTH, 2, G, N, P), (2 * P) ** -0.5),
        "ssm_c_re": nrm(ks[10], (DEPTH, 2, G, P, N), N ** -0.5),
        "ssm_c_im": nrm(ks[11], (DEPTH, 2, G, P, N), N ** -0.5),
        "ssm_d": nrm(ks[12], (DEPTH, MIX_WIDTH), 0.5),
        "ssm_glu_w": nrm(ks[13], (DEPTH, MIX_WIDTH, MIX_WIDTH), MIX_WIDTH ** -0.5),
        "ssm_glu_b": nrm(ks[14], (DEPTH, MIX_WIDTH), 0.02),
        "pool_w": nrm(ks[15], (DEPTH, len(POOL_WINDOWS), POOL_GROUP, POOL_GROUP), POOL_GROUP ** -0.5),
        "pool_scale": 1.0 + nrm(ks[16], (DEPTH, MIX_WIDTH), 0.02),
        "w_branch": nrm(ks[17], (DEPTH, N_BRANCH, MIX_WIDTH, D_MODEL), MIX_WIDTH ** -0.5),
        "w_out": nrm(ks[18], (DEPTH, D_MODEL, D_MODEL), 0.5 * D_MODEL ** -0.5),
        "norm_mlp": 1.0 + nrm(ks[19], (DEPTH, D_MODEL), 0.02),
        "w_up": nrm(ks[20], (DEPTH, D_MODEL, D_FF), D_MODEL ** -0.5),
        "w_down": nrm(ks[21], (DEPTH, D_FF, D_MODEL), 0.5 * D_FF ** -0.5),
        "norm_final": 1.0 + nrm(ks[22], (D_MODEL,), 0.02),
    }


def reference(x, meta_tokens, norm_mix, w_in, attn_sink, ssm_lam_re, ssm_lam_im, ssm_log_dt,
              ssm_b_re, ssm_b_im, ssm_c_re, ssm_c_im, ssm_d, ssm_glu_w, ssm_glu_b,
              pool_w, pool_scale, w_branch, w_out, norm_mlp, w_up, w_down, norm_final):
    b = x.shape[0]
    meta = jnp.broadcast_to(meta_tokens[None].astype(x.dtype), (b, N_META, D_MODEL))
    h = jnp.concatenate([meta, x], axis=1)
    L = h.shape[1]
    pos = jnp.arange(L, dtype=jnp.float32)
    for layer in range(DEPTH):
        hn = rms_norm(h, norm_mix[layer])
        proj = hn @ w_in[layer]
        q = apply_rope(proj[..., OFF_Q:OFF_K].reshape(b, L, N_Q_HEADS, HEAD_DIM), pos)
        k = apply_rope(proj[..., OFF_K:OFF_V].reshape(b, L, N_KV_HEADS, HEAD_DIM), pos)
        v = proj[..., OFF_V:OFF_SSM].reshape(b, L, N_KV_HEADS, HEAD_DIM)
        y_attn = windowed_gqa_attention(q, k, v, attn_sink[layer])
        y_ssm = bidirectional_s5(proj[..., OFF_SSM:OFF_POOL], ssm_lam_re[layer], ssm_lam_im[layer],
                                 ssm_log_dt[layer], ssm_b_re[layer], ssm_b_im[layer],
                                 ssm_c_re[layer], ssm_c_im[layer], ssm_d[layer],
                                 ssm_glu_w[layer], ssm_glu_b[layer])
        y_pool = multiscale_pool(proj[..., OFF_POOL:OFF_GATE], pool_w[layer], pool_scale[layer])
        ys = jnp.stack([y_attn, y_ssm, y_pool], axis=2)
        branch = jnp.einsum("blcm,cmd->blcd", ys, w_branch[layer])
        gates = jax.nn.sigmoid(proj[..., OFF_GATE:].reshape(b, L, N_BRANCH, D_MODEL))
        merged = jnp.sum(gates * branch, axis=2)
        h = h + merged @ w_out[layer]
        hn = rms_norm(h, norm_mlp[layer])
        h = h + jnp.square(jax.nn.relu(hn @ w_up[layer])) @ w_down[layer]
    return rms_norm(h, norm_final)[:, N_META:]
```

```python
import math
from contextlib import ExitStack

import numpy as np
import concourse.bass as bass
import concourse.mybir as mybir
from concourse.bass_utils import run_bass_kernel_spmd
from concourse.alu_op_type import AluOpType as ALU

F32 = mybir.dt.float32
BF16 = mybir.dt.bfloat16
I32 = mybir.dt.int32
AF = mybir.ActivationFunctionType

D = 1024
SEQ = 2048
NMETA = 16
LP = 2176
NT = 17
FRONT = 112
DIN = 4864
DFF = 4096
NL = 2
EPS = 1e-6
OFF_K, OFF_V, OFF_SSM, OFF_POOL, OFF_GATE = 512, 640, 768, 1280, 1792
NCH = 272
TCH = [(0, 512), (512, 512), (1024, 512), (1536, 512), (2048, 128)]


class Sched:
    ENG = ("pe", "act", "dve", "pool", "sp")

    def __init__(self, nc, es, n_dma_sems=28):
        self.nc = nc
        self.lists = {e: [] for e in self.ENG}
        self.sem = {e: es.enter_context(nc.semaphore("s_" + e)) for e in self.ENG}
        self.count = {e: 0 for e in self.ENG}
        self.waited = {e: {} for e in self.ENG}
        self.dma_sems = [es.enter_context(nc.semaphore("s_dma%d" % i)) for i in range(n_dma_sems)]
        self.dma_tot = [0] * n_dma_sems
        self.dma_rr = 0
        self.dma_rr_sw = 0
        self.res = {}
        self.ninstr = 0

    def _semobj(self, k):
        return self.sem[k] if isinstance(k, str) else self.dma_sems[k[1]]

    def _wait(self, eng, tok, raw=False):
        k, v = tok
        if k == eng and (not raw or eng == "pe"):
            return
        cur = self.waited[eng].get(k, 0)
        if cur >= v:
            return
        self.waited[eng][k] = v
        so = self._semobj(k)
        self.lists[eng].append(lambda h, so=so, v=v: h.wait_ge(so, v))

    def _deps(self, eng, reads, writes):
        for r in reads:
            st = self.res.get(r)
            if st and st[0] is not None:
                self._wait(eng, st[0], raw=True)
        for w in writes:
            st = self.res.get(w)
            if st:
                if st[0] is not None:
                    self._wait(eng, st[0])
                for k, v in st[1].items():
                    self._wait(eng, (k, v))

    def _commit(self, tok, reads, writes):
        k, v = tok
        for r in reads:
            st = self.res.setdefault(r, [None, {}])
            if st[1].get(k, 0) < v:
                st[1][k] = v
        for w in writes:
            self.res[w] = [tok, {}]

    def op(self, eng, fn, reads=(), writes=()):
        self._deps(eng, reads, writes)
        self.count[eng] += 1
        v = self.count[eng]
        so = self.sem[eng]
        self.lists[eng].append(lambda h, fn=fn, so=so: fn(h).then_inc(so, 1))
        self._commit((eng, v), reads, writes)
        self.ninstr += 1

    def dma(self, eng, out, in_, reads=(), writes=(), **kw):
        self._deps(eng, reads, writes)
        if eng == "pool":
            i = 16 + self.dma_rr_sw
            self.dma_rr_sw = (self.dma_rr_sw + 1) % (len(self.dma_sems) - 16)
        else:
            i = self.dma_rr
            self.dma_rr = (self.dma_rr + 1) % 16
        if self.dma_tot[i] > 0:
            k = ("d", i)
            cur = self.waited[eng].get(k, 0)
            if cur < self.dma_tot[i]:
                self.waited[eng][k] = self.dma_tot[i]
                so0, v0 = self.dma_sems[i], self.dma_tot[i]
                self.lists[eng].append(lambda h, so0=so0, v0=v0: h.wait_ge(so0, v0))
        self.dma_tot[i] += 16
        v = self.dma_tot[i]
        so = self.dma_sems[i]
        self.lists[eng].append(
            lambda h, so=so, out=out, in_=in_, kw=kw: h.dma_start(out=out, in_=in_, **kw).then_inc(so, 16))
        self._commit((("d", i), v), reads, writes)
        self.ninstr += 1

    def barrier(self):
        for e in self.ENG:
            for f in self.ENG:
                if f != e and self.count[f] > 0:
                    self._wait(e, (f, self.count[f]))
            for i, t in enumerate(self.dma_tot):
                if t > 0:
                    self._wait(e, (("d", i), t))
        self.res = {}

    def final_wait(self, eng="sp"):
        for f in self.ENG:
            if f != eng and self.count[f] > 0:
                self._wait(eng, (f, self.count[f]))
        for i, t in enumerate(self.dma_tot):
            if t > 0:
                self._wait(eng, (("d", i), t))

    def emit(self, block):
        L = self.lists

        @block.sync
        def _(h):
            for f in L["sp"]:
                f(h)

        @block.scalar
        def _(h):
            for f in L["act"]:
                f(h)

        @block.vector
        def _(h):
            for f in L["dve"]:
                f(h)

        @block.gpsimd
        def _(h):
            for f in L["pool"]:
                f(h)

        @block.tensor
        def _(h):
            for f in L["pe"]:
                f(h)


def make_consts():
    c = {}
    c["c_ident"] = np.eye(128, dtype=np.float32)
    t = np.arange(LP, dtype=np.float32) - FRONT
    r = np.arange(128)
    j = r % 64
    invf = (10000.0 ** (-(np.arange(32, dtype=np.float32)) * 2.0 / 64)).astype(np.float32)
    ang = (t[None, :] * invf[j % 32][:, None]).astype(np.float32)
    c["c_cos"] = np.cos(ang).astype(np.float32)
    sgn = np.where(j < 32, -1.0, 1.0).astype(np.float32)
    c["c_sin"] = (np.sin(ang) * sgn[:, None]).astype(np.float32)
    kl = np.arange(128)[:, None]
    ql = np.arange(128)[None, :]
    mp = np.where(kl >= ql, 0.0, -30000.0).astype(np.float32)
    mn = np.where(kl <= ql, 0.0, -30000.0).astype(np.float32)
    c["c_maskp"] = np.tile(mp, (1, 4))
    c["c_maskn"] = np.tile(mn, (1, 4))
    s = np.arange(128)[:, None] // 16
    tt = np.arange(128)[None, :] // 16
    c["c_maskfb"] = np.concatenate([(tt >= s), (tt <= s)], axis=1).astype(np.float32)
    tau = np.zeros((128, 2, 16, 9), np.float32)
    tau[:, 0, :, 0:8] = np.arange(8, dtype=np.float32)[None, None, :]
    tau[:, 1, :, 0:8] = (7 - np.arange(8, dtype=np.float32))[None, None, :]
    tau[:, :, :, 8] = 8.0
    c["c_tau"] = tau.reshape(128, 288)
    rc = np.zeros((4, LP), np.float32)
    L = NMETA + SEQ
    idx = np.arange(L)
    for gi, w in enumerate((2, 4, 8, 16)):
        lo = np.clip(idx - w // 2, 0, L)
        hi = np.clip(idx + w // 2, 0, L)
        rc[gi, FRONT:] = 1.0 / (hi - lo).astype(np.float32)
    c["c_rcnt"] = rc
    return c


CONST_SHAPES = {"c_ident": [128, 128], "c_cos": [128, LP], "c_sin": [128, LP], "c_maskp": [128, 512],
                "c_maskn": [128, 512], "c_maskfb": [128, 256], "c_tau": [128, 288], "c_rcnt": [4, LP]}

IN_SHAPES = {
    "x": [SEQ, D], "meta_tokens": [NMETA, D], "norm_mix": [NL, D], "w_in": [NL, D, DIN], "attn_sink": [NL, 8],
    "ssm_lam_re": [NL, 2, 32, 64], "ssm_lam_im": [NL, 2, 32, 64], "ssm_log_dt": [NL, 2, 32],
    "ssm_b_re": [NL, 2, 32, 64, 16], "ssm_b_im": [NL, 2, 32, 64, 16], "ssm_c_re": [NL, 2, 32, 16, 64],
    "ssm_c_im": [NL, 2, 32, 16, 64], "ssm_d": [NL, 512], "ssm_glu_w": [NL, 512, 512], "ssm_glu_b": [NL, 512],
    "pool_w": [NL, 4, 128, 128], "pool_scale": [NL, 512], "w_branch": [NL, 3, 512, D], "w_out": [NL, D, D],
    "norm_mlp": [NL, D], "w_up": [NL, D, DFF], "w_down": [NL, DFF, D], "norm_final": [D],
}


def AP(t, offset, ap):
    return bass.AP(tensor=t, offset=offset, ap=[list(a) for a in ap])


def bc(ap, axis, n):
    a = ap.unsqueeze(axis)
    shp = list(a.shape)
    shp[axis] = n
    return a.to_broadcast(shp)


class K:
    def __init__(self, nlayers=NL, taps=(), stop_after=None):
        self.nlayers = nlayers
        self.stop_after = stop_after
        nc = self.nc = bass.Bass("TRN2", target_bir_lowering=False)
        self.T = {}
        for k, shp in IN_SHAPES.items():
            self.T[k] = nc.dram_tensor(k, shp, F32, kind="ExternalInput")
        for k, shp in CONST_SHAPES.items():
            self.T[k] = nc.dram_tensor(k, shp, F32, kind="ExternalInput")
        self.A = {k: v.ap() for k, v in self.T.items()}
        self.out = nc.dram_tensor("out", [SEQ, D], F32, kind="ExternalOutput").ap()
        self.h_d = nc.dram_tensor("h_scr", [LP, D], F32, kind="Internal").ap()
        self.u_t = nc.dram_tensor("u_scr", [LP, 512], BF16, kind="Internal")
        self.u_d = self.u_t.ap()
        self.z_t = nc.dram_tensor("z_scr", [LP, 512], BF16, kind="Internal")
        self.z_d = self.z_t.ap()
        self.tap_out = {}
        for name, shp in taps:
            self.tap_out[name] = nc.dram_tensor("tap_" + name, shp, F32, kind="ExternalOutput").ap()
        self.uid = 0
        self.rr = {"ps": 0, "pt": 0, "ev": 0, "ew": 0}

    def sbuf(self, st, name, shape, dt):
        self.uid += 1
        return st.enter_context(self.nc.sbuf_tensor("%s_%d" % (name, self.uid), shape, dt))

    def next_ps(self):
        i = self.rr["ps"]
        self.rr["ps"] = (i + 1) % len(self.ps)
        return self.ps[i], ("ps", i)

    def next_pt(self):
        i = self.rr["pt"]
        self.rr["pt"] = (i + 1) % len(self.pt)
        return self.pt[i], ("pt", i)

    def evac(self, out_ap, in_ap, reads, writes, eng=None):
        S = self.S
        if eng is None:
            self.rr["ev"] ^= 1
            eng = "act" if self.rr["ev"] else "dve"
        if eng == "act":
            S.op("act", lambda h: h.activation(out=out_ap, in_=in_ap, func=AF.Copy), reads=reads, writes=writes)
        else:
            S.op(eng, lambda h: h.tensor_copy(out=out_ap, in_=in_ap), reads=reads, writes=writes)

    def ew(self):
        self.rr["ew"] ^= 1
        return "dve" if self.rr["ew"] else "pool"

    def tap(self, name, src_ap, reads):
        if name in self.tap_out:
            self.S.dma("sp", self.tap_out[name], src_ap, reads=reads)

    def load_w(self, dst_ap, name, base, rows0, nk, ld, c0, ncols, writes, eng="pool"):
        src = AP(self.T[name], base + rows0 * ld + c0, [[ld, 128], [128 * ld, nk], [1, ncols]])
        self.S.dma(eng, dst_ap, src, writes=writes)

    def mm(self, out_ap, lhsT, rhs, start, stop, reads, writes):
        self.S.op("pe", lambda h: h.matmul(out_ap, lhsT=lhsT, rhs=rhs, start=start, stop=stop), reads=reads, writes=writes)

    def tr(self, out_ap, in_ap, idn, reads, writes):
        self.S.op("pe", lambda h: h.transpose(out=out_ap, in_=in_ap, identity=idn), reads=list(reads) + ["ident"], writes=writes)

    def build(self):
        nc = self.nc
        with ExitStack() as es:
            S = self.S = Sched(nc, es)
            self.ps = [es.enter_context(nc.psum_tensor("ps%d" % i, [128, 512], F32)) for i in range(6)]
            self.pt = [es.enter_context(nc.psum_tensor("pt%d" % i, [128, 1024], BF16)) for i in range(2)]
            self.identf = self.sbuf(es, "identf", [128, 128], F32)
            self.ident = self.sbuf(es, "ident", [128, 128], BF16)
            self.small = self.sbuf(es, "small", [128, 16], F32)
            self.zt = self.sbuf(es, "zt", [FRONT, 256], F32)
            S.dma("sp", self.identf[:], self.A["c_ident"], writes=["identf"])
            S.op("dve", lambda h: h.tensor_copy(out=self.ident[:], in_=self.identf[:]), reads=["identf"], writes=["ident"])
            S.op("pool", lambda h: h.memset(self.small[:, 0:1], EPS), writes=["eps"])
            self.init_h()
            for l in range(self.nlayers):
                self.layer(l)
            S.final_wait("sp")
            with nc.Block() as block:
                S.emit(block)
        return nc

    def init_h(self):
        S = self.S
        zt = self.zt
        S.op("pool", lambda h: h.memset(zt[:], 0.0), writes=["zt"])
        for c in range(4):
            S.dma("sp", self.h_d[0:FRONT, c * 256:(c + 1) * 256], zt[:], reads=["zt"], writes=[("h", 0)])
        S.dma("sp", self.h_d[FRONT:128, :], self.A["meta_tokens"], writes=[("hm", 0)])
        for b in range(1, NT):
            S.dma("sp", self.h_d[b * 128:(b + 1) * 128, :], self.A["x"][(b - 1) * 128:b * 128, :], writes=[("h", b)])

    def norm_transpose(self, st, gname, goff, hnT, src_tiles):
        S = self.S
        small = self.small
        gain = self.sbuf(st, "gain", [128, D], F32)
        S.dma("sp", gain[:], AP(self.T[gname], goff, [[0, 128], [1, D]]), writes=["gain"])
        junk = self.sbuf(st, "junk", [128, D], BF16)
        hnb = [self.sbuf(st, "hnb%d" % i, [128, D], BF16) for i in range(2)]
        ss = self.sbuf(st, "ss", [128, 3 * NT], F32)
        for b in range(NT):
            src, rd = src_tiles(b)
            S.op("act", lambda h, src=src, b=b: h.activation(out=junk[:], in_=src, func=AF.Square, accum_out=ss[:, b:b + 1]),
                 reads=rd, writes=["junk", ("ss", b)])
            S.op("act", lambda h, b=b: h.activation(out=ss[:, NT + b:NT + b + 1], in_=ss[:, b:b + 1], func=AF.Sqrt,
                                                    scale=1.0 / D, bias=small[:, 0:1]),
                 reads=[("ss", b), "eps"], writes=[("ss1", b)])
            S.op("dve", lambda h, b=b: h.reciprocal(out=ss[:, 2 * NT + b:2 * NT + b + 1], in_=ss[:, NT + b:NT + b + 1]),
                 reads=[("ss1", b)], writes=[("ss2", b)])
            i = b % 2
            S.op("dve", lambda h, src=src, b=b, i=i: h.scalar_tensor_tensor(
                out=hnb[i][:], in0=src, scalar=ss[:, 2 * NT + b:2 * NT + b + 1], in1=gain[:], op0=ALU.mult, op1=ALU.mult),
                reads=list(rd) + [("ss2", b), "gain"], writes=[("hnb", i)])
            bank, bk = self.next_pt()
            for k in range(8):
                self.tr(bank[:, k * 128:(k + 1) * 128], hnb[i][:, k * 128:(k + 1) * 128], self.ident[:, :], [("hnb", i)], [bk])
            self.evac(hnT[:, :, b * 128:(b + 1) * 128], bank[:, :].rearrange("p (k t) -> p k t", k=8), [bk], [("hnT", b)])

    def layer(self, l):
        S = self.S
        with ExitStack() as lst:
            hnT = self.sbuf(lst, "hnT", [128, 8, LP], BF16)
            if l == 0 and "hnT" in self.tap_out:
                with ExitStack() as st:
                    tmp = self.sbuf(st, "tmp", [128, 8 * LP], F32)
                    S.op("dve", lambda h: h.tensor_copy(out=tmp[:], in_=hnT[:].rearrange("p k t -> p (k t)")), reads=[("hnT", b) for b in range(NT)], writes=["tmp"])
                    self.tap("hnT", tmp[:], ["tmp"])
                    S.barrier()
            if self.stop_after == "n1":
                return
            self.phase_ssm_core(l, hnT)
            if self.stop_after in ("ssmA", "ssmB", "ssmC", "ssmD", "ssmD1", "ssmD2"):
                return
            yT = [self.sbuf(lst, "yT%d" % c, [128, 4, LP], BF16) for c in (1,)]
            self.phase_ssm_glu(l, yT[0])
            if self.stop_after == "ssm":
                return
            yTa = self.sbuf(lst, "yTa", [128, 4, LP], BF16)
            self.phase_attn(l, hnT, yTa)
            if self.stop_after == "attn":
                return
            yTp = self.sbuf(lst, "yTp", [128, 4, LP], BF16)
            self.phase_pool(l, hnT, yTp)
            if self.stop_after == "pool":
                return
            self.phase_merge(l, hnT, [yTa, yT[0], yTp])
        if self.stop_after == "merge":
            return
        self.phase_ffn(l)

    def tap_yT(self, name, yT, l):
        S = self.S
        if l == 0 and name in self.tap_out:
            with ExitStack() as st:
                tmp = self.sbuf(st, "tmp", [128, 4 * LP], F32)
                S.op("dve", lambda h: h.tensor_copy(out=tmp[:], in_=yT[:].rearrange("p k t -> p (k t)")), reads=["yT_all"], writes=["tmp"])
                self.tap(name, tmp[:], ["tmp"])
                S.barrier()

    def phase_pool(self, l, hnT, yTp):
        S = self.S
        WX, OFF = LP + 64, 32
        with ExitStack() as st:
            wp = [self.sbuf(st, "wp%d" % i, [128, 8, 128], BF16) for i in range(2)]
            pw = [self.sbuf(st, "pw%d" % i, [128, 128], BF16) for i in range(2)]
            ua = self.sbuf(st, "ua", [128, WX], F32)
            pb = [self.sbuf(st, "pb%d" % i, [128, WX], F32) for i in range(2)]
            rcb = self.sbuf(st, "rcb", [128, LP], F32)
            dd = self.sbuf(st, "dd", [128, LP], BF16)
            psc = self.sbuf(st, "psc", [128, 4], F32)
            S.dma("sp", psc[:], AP(self.T["pool_scale"], l * 512, [[1, 128], [128, 4]]), writes=["psc"], allow_slow_non_contiguous=True)
            S.op("pool", lambda h: h.memset(ua[:], 0.0), writes=["ua"])
            S.op("pool", lambda h: h.memset(pb[0][:], 0.0), writes=["pb0"])
            S.op("pool", lambda h: h.memset(pb[1][:], 0.0), writes=["pb1"])
            shifts = [(-1, 0), (-1, 1), (-2, 2), (-4, 4)]
            for g in range(4):
                i = g % 2
                self.load_w(wp[i][:], "w_in", l * D * DIN, 0, 8, DIN, OFF_POOL + 128 * g, 128, [("wp", i)])
                S.dma("pool", pw[i][:], AP(self.T["pool_w"], (l * 4 + g) * 16384, [[128, 128], [1, 128]]), writes=[("pw", i)])
                S.dma("sp", rcb[:], AP(self.T["c_rcnt"], g * LP, [[0, 128], [1, LP]]), writes=["rcb"])
                for (t0, n) in TCH:
                    bank, bk = self.next_ps()
                    for k in range(8):
                        self.mm(bank[:, 0:n], wp[i][:, k, :], hnT[:, k, t0:t0 + n], k == 0, k == 7, [("wp", i)], [bk])
                    self.evac(ua[:, OFF + t0:OFF + t0 + n], bank[:, 0:n], [bk], ["ua"])
                cur, curk = ua, "ua"
                lo, hi = OFF - 16, OFF + LP + 16
                for lev in range(g + 1):
                    s0, s1 = shifts[lev]
                    dst, dk = pb[lev % 2], "pb%d" % (lev % 2)
                    S.op(self.ew(), lambda h, dst=dst, cur=cur, s0=s0, s1=s1: h.tensor_tensor(
                        out=dst[:, lo:hi], in0=cur[:, lo + s0:hi + s0], in1=cur[:, lo + s1:hi + s1], op=ALU.add),
                        reads=[curk], writes=[dk])
                    cur, curk = dst, dk
                oth, ok = pb[(g + 1) % 2], "pb%d" % ((g + 1) % 2)
                S.op("dve", lambda h, cur=cur, oth=oth: h.tensor_tensor(out=oth[:, OFF:OFF + LP], in0=cur[:, OFF:OFF + LP], in1=rcb[:], op=ALU.mult),
                     reads=[curk, "rcb"], writes=[ok])
                S.op("pool", lambda h, oth=oth: h.tensor_tensor(out=dd[:], in0=oth[:, OFF:OFF + LP], in1=ua[:, OFF:OFF + LP], op=ALU.subtract),
                     reads=[ok, "ua"], writes=["dd"])
                for (t0, n) in TCH:
                    bank, bk = self.next_ps()
                    self.mm(bank[:, 0:n], pw[i][:], dd[:, t0:t0 + n], True, True, [("pw", i), "dd"], [bk])
                    S.op("act", lambda h, bank=bank, n=n, t0=t0, g=g: h.activation(out=yTp[:, g, t0:t0 + n], in_=bank[:, 0:n], func=AF.Copy,
                                                                                    scale=psc[:, g:g + 1]),
                         reads=[bk, "psc"], writes=["yT_all"])
            S.barrier()
        self.tap_yT("yTp", yTp, l)

    def phase_attn(self, l, hnT, yTa):
        S = self.S
        A = self.A
        with ExitStack() as st:
            qT = self.sbuf(st, "qT", [128, 4, LP], BF16)
            kT = self.sbuf(st, "kT", [128, 2, LP], BF16)
            Va = self.sbuf(st, "Va", [128, NT, 2, 65], BF16)
            Vm = self.sbuf(st, "Vm", [16, 2, 65], BF16)
            cosT = self.sbuf(st, "cosT", [128, LP], F32)
            sinT = self.sbuf(st, "sinT", [128, LP], F32)
            mkf = self.sbuf(st, "mkf", [128, 1024], F32)
            mk = self.sbuf(st, "mk", [128, 1024], BF16)
            ww = [self.sbuf(st, "ww%d" % i, [128, 8, 128], BF16) for i in range(2)]
            wr = [self.sbuf(st, "wr%d" % i, [128, 8, 128], BF16) for i in range(2)]
            t1 = [self.sbuf(st, "t1%d" % i, [128, 512], F32) for i in range(2)]
            t2 = [self.sbuf(st, "t2%d" % i, [128, 512], F32) for i in range(2)]
            PT = [self.sbuf(st, "PT%d" % i, [128, 512], BF16) for i in range(8)]
            den = self.sbuf(st, "den", [128, 16], F32)
            snk = self.sbuf(st, "snk", [128, 8], F32)
            S.dma("sp", cosT[:], A["c_cos"], writes=["cosT"])
            S.dma("sp", sinT[:], A["c_sin"], writes=["sinT"])
            S.dma("sp", mkf[:, 0:512], A["c_maskp"], writes=["mkf"])
            S.dma("sp", mkf[:, 512:1024], A["c_maskn"], writes=["mkf"])
            S.op("dve", lambda h: h.tensor_copy(out=mk[:], in_=mkf[:]), reads=["mkf"], writes=["mk"])
            S.dma("sp", snk[:], AP(self.T["attn_sink"], l * 8, [[0, 128], [1, 8]]), writes=["snk"])
            S.op("act", lambda h: h.activation(out=snk[:], in_=snk[:], func=AF.Exp), reads=["snk"], writes=["snk"])
            S.op("pool", lambda h: h.memset(Va[:, :, :, 64:65], 1.0), writes=["Va1"])
            S.op("pool", lambda h: h.memset(Vm[:, :, 64:65], 1.0), writes=["Vm1"])
            for ct in range(6):
                i = ct % 2
                if ct < 4:
                    self.load_w(ww[i][:], "w_in", l * D * DIN, 0, 8, DIN, 128 * ct, 128, [("ww", i)])
                    dst = lambda t0, n, ct=ct: qT[:, ct, t0:t0 + n]
                else:
                    kh = ct - 4
                    self.load_w(ww[i][:, :, 0:64], "w_in", l * D * DIN, 0, 8, DIN, OFF_K + 64 * kh, 64, [("ww", i)])
                    self.load_w(ww[i][:, :, 64:128], "w_in", l * D * DIN, 0, 8, DIN, OFF_K + 64 * kh, 64, [("ww", i)])
                    dst = lambda t0, n, kh=kh: kT[:, kh, t0:t0 + n]
                wv = ww[i][:].rearrange("p k (a two j) -> p k a two j", a=2, two=2, j=32)
                rv = wr[i][:].rearrange("p k (a two j) -> p k a two j", a=2, two=2, j=32)
                S.op("dve", lambda h, wv=wv, rv=rv: h.tensor_copy(out=rv[:, :, :, 0, :], in_=wv[:, :, :, 1, :]), reads=[("ww", i)], writes=[("wr", i)])
                S.op("act", lambda h, wv=wv, rv=rv: h.activation(out=rv[:, :, :, 1, :], in_=wv[:, :, :, 0, :], func=AF.Copy), reads=[("ww", i)], writes=[("wr", i)])
                for ti, (t0, n) in enumerate(TCH):
                    j = ti % 2
                    ba, bka = self.next_ps()
                    bb_, bkb = self.next_ps()
                    for k in range(8):
                        self.mm(ba[:, 0:n], ww[i][:, k, :], hnT[:, k, t0:t0 + n], k == 0, k == 7, [("ww", i)], [bka])
                    for k in range(8):
                        self.mm(bb_[:, 0:n], wr[i][:, k, :], hnT[:, k, t0:t0 + n], k == 0, k == 7, [("wr", i)], [bkb])
                    S.op("dve", lambda h, ba=ba, n=n, t0=t0, j=j: h.tensor_tensor(out=t1[j][:, 0:n], in0=ba[:, 0:n], in1=cosT[:, t0:t0 + n], op=ALU.mult),
                         reads=[bka, "cosT"], writes=[("t1", j)])
                    S.op("dve", lambda h, bb_=bb_, n=n, t0=t0, j=j: h.tensor_tensor(out=t2[j][:, 0:n], in0=bb_[:, 0:n], in1=sinT[:, t0:t0 + n], op=ALU.mult),
                         reads=[bkb, "sinT"], writes=[("t2", j)])
                    d_ap = dst(t0, n)
                    S.op("dve", lambda h, d_ap=d_ap, n=n, j=j: h.tensor_tensor(out=d_ap, in0=t1[j][:, 0:n], in1=t2[j][:, 0:n], op=ALU.add),
                         reads=[("t1", j), ("t2", j)], writes=["qk"])
            wvv = ww[0]
            self.load_w(wvv[:], "w_in", l * D * DIN, 0, 8, DIN, OFF_V, 128, [("ww", 0)])
            for b in range(NT):
                bank, bk = self.next_ps()
                for k in range(8):
                    self.mm(bank[:, 0:128], hnT[:, k, b * 128:(b + 1) * 128], wvv[:, k, :], k == 0, k == 7, [("ww", 0)], [bk])
                self.evac(Va[:, b, :, 0:64], bank[:, 0:128].rearrange("p (a d) -> p a d", a=2), [bk], [("Va", b)])
            bank, bk = self.next_ps()
            for k in range(8):
                self.mm(bank[0:16, 0:128], hnT[:, k, FRONT:128], wvv[:, k, :], k == 0, k == 7, [("ww", 0)], [bk])
            self.evac(Vm[:, :, 0:64], bank[0:16, 0:128].rearrange("p (a d) -> p a d", a=2), [bk], ["Vm"])
            if l == 0 and "qT" in self.tap_out:
                with ExitStack() as st2:
                    tmp = self.sbuf(st2, "tmp", [128, 6 * LP], F32)
                    S.op("dve", lambda h: h.tensor_copy(out=tmp[:, 0:4 * LP], in_=qT[:].rearrange("p k t -> p (k t)")), reads=["qk"], writes=["tmp"])
                    S.op("dve", lambda h: h.tensor_copy(out=tmp[:, 4 * LP:6 * LP], in_=kT[:].rearrange("p k t -> p (k t)")), reads=["qk"], writes=["tmp"])
                    self.tap("qT", tmp[:], ["tmp"])
                    S.barrier()
            ytm = self.sbuf(st, "ytmall", [128, NT, 512], BF16)
            for n_ in range(NT):
                for kh in range(2):
                    kbs = []
                    if n_ - 1 >= 1:
                        kbs.append((n_ - 1, 0))
                    if n_ >= 1:
                        kbs.append((n_, None))
                    if n_ + 1 <= NT - 1:
                        kbs.append((n_ + 1, 1))
                    kbs.append((-1, None))
                    slots = []
                    for idx, (kb, mi) in enumerate(kbs):
                        nk = 128 if kb >= 0 else 16
                        kc0 = kb * 128 if kb >= 0 else FRONT
                        sl = (kh * 4 + idx)
                        for half in range(2):
                            bank, bk = self.next_ps()
                            if mi is not None:
                                self.mm(bank[:, 0:256], self.ident[:], mk[:, mi * 512:mi * 512 + 256], True, False, ["ident", "mk"], [bk])
                            for ii in range(2):
                                i = 2 * ii + half
                                hq = 4 * kh + i
                                tile_ = hq // 2
                                r0 = 64 * half
                                self.mm(bank[0:nk, ii * 128:(ii + 1) * 128], kT[r0:r0 + 64, kh, kc0:kc0 + nk],
                                        qT[r0:r0 + 64, tile_, n_ * 128:(n_ + 1) * 128], mi is None, True, ["qk"], [bk])
                            S.op("act", lambda h, bank=bank, nk=nk, sl=sl, half=half: h.activation(out=PT[sl][0:nk, half * 256:(half + 1) * 256], in_=bank[0:nk, 0:256], func=AF.Exp, scale=0.125),
                                 reads=[bk], writes=[("PT", sl)])
                        slots.append((sl, kb, nk))
                    ob, obk = self.next_ps()
                    for i in range(4):
                        pc0 = (i % 2) * 256 + (i // 2) * 128
                        for idx, (sl, kb, nk) in enumerate(slots):
                            vsrc = Va[:, kb, kh, :] if kb >= 0 else Vm[:, kh, :]
                            rds = [("PT", sl)] + ([("Va", kb), "Va1"] if kb >= 0 else ["Vm", "Vm1"])
                            self.mm(ob[:, i * 65:(i + 1) * 65], PT[sl][0:nk, pc0:pc0 + 128], vsrc, idx == 0, idx == len(slots) - 1, rds, [obk])
                    ov = ob[:, 0:260].rearrange("p (i c) -> p i c", i=4)
                    dsl = den[:, kh * 8:kh * 8 + 4]
                    rsl = den[:, kh * 8 + 4:kh * 8 + 8]
                    S.op("dve", lambda h, ov=ov, dsl=dsl, kh=kh: h.tensor_tensor(out=dsl.unsqueeze(2), in0=ov[:, :, 64:65], in1=snk[:, 4 * kh:4 * kh + 4].unsqueeze(2), op=ALU.add),
                         reads=[obk, "snk"], writes=[("den", kh)])
                    S.op("dve", lambda h, dsl=dsl, rsl=rsl: h.reciprocal(out=rsl, in_=dsl), reads=[("den", kh)], writes=[("rden", kh)])
                    for i in range(4):
                        hq = 4 * kh + i
                        S.op("act", lambda h, ob=ob, i=i, hq=hq, n_=n_, rsl=rsl: h.activation(out=ytm[:, n_, hq * 64:(hq + 1) * 64], in_=ob[:, i * 65:i * 65 + 64],
                                                                                           func=AF.Copy, scale=rsl[:, i:i + 1]),
                             reads=[obk, ("rden", kh)], writes=["ytm"])
            S.barrier()
            for n_ in range(NT):
                bank, bk = self.next_pt()
                for j in range(4):
                    self.tr(bank[:, j * 128:(j + 1) * 128], ytm[:, n_, j * 128:(j + 1) * 128], self.ident[:, :], ["ytm"], [bk])
                self.evac(yTa[:, :, n_ * 128:(n_ + 1) * 128], bank[:, 0:512].rearrange("p (j t) -> p j t", j=4), [bk], ["yT_all"])
            S.barrier()
        self.tap_yT("yTa", yTa, l)

    def phase_merge(self, l, hnT, yTs):
        S = self.S
        with ExitStack() as st:
            wg = [self.sbuf(st, "wg%d" % i, [128, 8, 3, 128], BF16) for i in range(2)]
            wb = [self.sbuf(st, "wb%d" % i, [128, 12, 128], BF16) for i in range(2)]
            wo = self.sbuf(st, "wo", [128, 4, D], BF16)
            mT = self.sbuf(st, "mT", [128, 4, LP], BF16)
            sg = [self.sbuf(st, "sg%d" % i, [128, 512], F32) for i in range(2)]
            pa = [self.sbuf(st, "pa%d" % i, [128, 512], F32) for i in range(3)]
            ht = [self.sbuf(st, "ht%d" % i, [128, D], F32) for i in range(3)]
            cnt = 0
            for half in range(2):
                for dt in range(4):
                    dti = 4 * half + dt
                    i = dti % 2
                    for c in range(3):
                        self.load_w(wg[i][:, :, c, :], "w_in", l * D * DIN, 0, 8, DIN, OFF_GATE + c * D + dti * 128, 128, [("wg", i)])
                    self.load_w(wb[i][:], "w_branch", l * 3 * 512 * D, 0, 12, D, dti * 128, 128, [("wb", i)])
                    for (t0, n) in TCH:
                        for c in range(3):
                            bB, kB = self.next_ps()
                            bG, kG = self.next_ps()
                            for k in range(4):
                                self.mm(bB[:, 0:n], wb[i][:, 4 * c + k, :], yTs[c][:, k, t0:t0 + n], k == 0, k == 3, [("wb", i), "yT_all"], [kB])
                            for k in range(8):
                                self.mm(bG[:, 0:n], wg[i][:, k, c, :], hnT[:, k, t0:t0 + n], k == 0, k == 7, [("wg", i)], [kG])
                            j = cnt % 2
                            cnt += 1
                            S.op("act", lambda h, bG=bG, n=n, j=j: h.activation(out=sg[j][:, 0:n], in_=bG[:, 0:n], func=AF.Sigmoid), reads=[kG], writes=[("sg", j)])
                            S.op("dve", lambda h, bB=bB, n=n, j=j, c=c: h.tensor_tensor(out=pa[c][:, 0:n], in0=bB[:, 0:n], in1=sg[j][:, 0:n], op=ALU.mult),
                                 reads=[kB, ("sg", j)], writes=[("pa", c)])
                        S.op("dve", lambda h, n=n: h.tensor_tensor(out=pa[0][:, 0:n], in0=pa[0][:, 0:n], in1=pa[1][:, 0:n], op=ALU.add),
                             reads=[("pa", 0), ("pa", 1)], writes=[("pa", 0)])
                        S.op("dve", lambda h, n=n, t0=t0, dt=dt: h.tensor_tensor(out=mT[:, dt, t0:t0 + n], in0=pa[0][:, 0:n], in1=pa[2][:, 0:n], op=ALU.add),
                             reads=[("pa", 0), ("pa", 2)], writes=["mT"])
                self.load_w(wo[:], "w_out", l * D * D, half * 512, 4, D, 0, D, ["wo"])
                for b in range(NT):
                    i = b % 3
                    rd = [("h", b)] + ([("hm", 0)] if b == 0 else [])
                    S.dma("sp", ht[i][:], self.h_d[b * 128:(b + 1) * 128, :], reads=rd, writes=[("ht", i)])
                    for ch in range(2):
                        bank, bk = self.next_ps()
                        for dt in range(4):
                            self.mm(bank[:, :], mT[:, dt, b * 128:(b + 1) * 128], wo[:, dt, ch * 512:(ch + 1) * 512], dt == 0, dt == 3, ["mT", "wo"], [bk])
                        S.op("dve", lambda h, bank=bank, i=i, ch=ch: h.tensor_tensor(out=ht[i][:, ch * 512:(ch + 1) * 512], in0=bank[:, :], in1=ht[i][:, ch * 512:(ch + 1) * 512], op=ALU.add),
                             reads=[bk, ("ht", i)], writes=[("ht", i)])
                    S.dma("sp", self.h_d[b * 128:(b + 1) * 128, :], ht[i][:], reads=[("ht", i)], writes=[("h", b), ("hm", 0)] if b == 0 else [("h", b)])
            S.barrier()
        if l == 0 and "h_mix" in self.tap_out:
            S.dma("sp", self.tap_out["h_mix"], self.h_d, reads=[("h", b) for b in range(NT)])
            S.barrier()

    def phase_ffn(self, l):
        S = self.S
        last = (l == NL - 1)
        with ExitStack() as st:
            hs = self.sbuf(st, "hs", [128, NT, D], F32)
            hn2T = self.sbuf(st, "hn2T", [128, 8, LP], BF16)
            for b in range(NT):
                rd = [("h", b)] + ([("hm", 0)] if b == 0 else [])
                S.dma("sp", hs[:, b, :], self.h_d[b * 128:(b + 1) * 128, :], reads=rd, writes=[("hs", b, 0), ("hs", b, 1)])
            with ExitStack() as st2:
                self.norm_transpose(st2, "norm_mlp", l * D, hn2T, lambda b: (hs[:, b, :], [("hs", b, 0), ("hs", b, 1)]))
                wu = [self.sbuf(st2, "wu%d" % i, [128, 8, 512], BF16) for i in range(2)]
                wd = [self.sbuf(st2, "wd%d" % i, [128, 4, D], BF16) for i in range(2)]
                aT = self.sbuf(st2, "aT", [128, 4, LP], BF16)
                rl = [self.sbuf(st2, "rl%d" % i, [128, 512], F32) for i in range(2)]
                cnt = 0
                for fc in range(8):
                    i = fc % 2
                    self.load_w(wu[i][:], "w_up", l * D * DFF, 0, 8, DFF, fc * 512, 512, [("wu", i)])
                    self.load_w(wd[i][:], "w_down", l * DFF * D, fc * 512, 4, D, 0, D, [("wd", i)])
                    for ft in range(4):
                        for (t0, n) in TCH:
                            bank, bk = self.next_ps()
                            for k in range(8):
                                self.mm(bank[:, 0:n], wu[i][:, k, ft * 128:(ft + 1) * 128], hn2T[:, k, t0:t0 + n], k == 0, k == 7,
                                        [("wu", i)] + [("hnT", bb2) for bb2 in range(t0 // 128, (t0 + n) // 128)], [bk])
                            j = cnt % 2
                            cnt += 1
                            S.op("act", lambda h, bank=bank, n=n, j=j: h.activation(out=rl[j][:, 0:n], in_=bank[:, 0:n], func=AF.Relu), reads=[bk], writes=[("rl", j)])
                            S.op("dve", lambda h, n=n, j=j, ft=ft, t0=t0: h.tensor_tensor(out=aT[:, ft, t0:t0 + n], in0=rl[j][:, 0:n], in1=rl[j][:, 0:n], op=ALU.mult),
                                 reads=[("rl", j)], writes=["aT"])
                    for b in range(NT):
                        for ch in range(2):
                            bank, bk = self.next_ps()
                            for ft in range(4):
                                self.mm(bank[:, :], aT[:, ft, b * 128:(b + 1) * 128], wd[i][:, ft, ch * 512:(ch + 1) * 512], ft == 0, ft == 3, ["aT", ("wd", i)], [bk])
                            S.op("dve", lambda h, bank=bank, b=b, ch=ch: h.tensor_tensor(out=hs[:, b, ch * 512:(ch + 1) * 512], in0=bank[:, :], in1=hs[:, b, ch * 512:(ch + 1) * 512], op=ALU.add),
                                 reads=[bk, ("hs", b, ch)], writes=[("hs", b, ch)])
                S.barrier()
            if l == 0 and "h_new" in self.tap_out:
                for b in range(NT):
                    S.dma("sp", self.tap_out["h_new"][b * 128:(b + 1) * 128, :], hs[:, b, :], reads=[("hs", b, 0), ("hs", b, 1)])
            if not last or self.nlayers < NL:
                for b in range(NT):
                    S.dma("sp", self.h_d[b * 128:(b + 1) * 128, :], hs[:, b, :], reads=[("hs", b, 0), ("hs", b, 1)], writes=[("h", b), ("hm", 0)] if b == 0 else [("h", b)])
            if l == self.nlayers - 1:
                with ExitStack() as st2:
                    small = self.small
                    gain = self.sbuf(st2, "gainf", [128, D], F32)
                    S.dma("sp", gain[:], AP(self.T["norm_final"], 0, [[0, 128], [1, D]]), writes=["gainf"])
                    junk = self.sbuf(st2, "junkf", [128, D], BF16)
                    ss = self.sbuf(st2, "ssf", [128, 3 * NT], F32)
                    ob = [self.sbuf(st2, "ob%d" % i, [128, D], F32) for i in range(2)]
                    for b in range(1, NT):
                        i = b % 2
                        S.op("act", lambda h, b=b: h.activation(out=junk[:], in_=hs[:, b, :], func=AF.Square, accum_out=ss[:, b:b + 1]),
                             reads=[("hs", b, 0), ("hs", b, 1)], writes=["junkf", ("ssf", b)])
                        S.op("act", lambda h, b=b: h.activation(out=ss[:, NT + b:NT + b + 1], in_=ss[:, b:b + 1], func=AF.Sqrt, scale=1.0 / D, bias=small[:, 0:1]),
                             reads=[("ssf", b), "eps"], writes=[("ssf1", b)])
                        S.op("dve", lambda h, b=b: h.reciprocal(out=ss[:, 2 * NT + b:2 * NT + b + 1], in_=ss[:, NT + b:NT + b + 1]), reads=[("ssf1", b)], writes=[("ssf2", b)])
                        S.op("dve", lambda h, b=b, i=i: h.scalar_tensor_tensor(out=ob[i][:], in0=hs[:, b, :], scalar=ss[:, 2 * NT + b:2 * NT + b + 1], in1=gain[:], op0=ALU.mult, op1=ALU.mult),
                             reads=[("hs", b, 0), ("hs", b, 1), ("ssf2", b), "gainf"], writes=[("ob", i)])
                        S.dma("sp", self.out[(b - 1) * 128:b * 128, :], ob[i][:], reads=[("ob", i)])
            S.barrier()

    def phase_ssm_core(self, l, hnT):
        S = self.S
        T = self.T
        TWO_PI = 2.0 * math.pi
        with ExitStack() as st:
            UT = self.sbuf(st, "UT", [128, 32, NCH], BF16)
            Kb = self.sbuf(st, "Kb", [128, 32, 128], BF16)
            Cp = self.sbuf(st, "Cp", [128, 2, 16, 2, 128], BF16)
            BpT = self.sbuf(st, "BpT", [128, 2, 16, 2, 128], BF16)
            a8 = self.sbuf(st, "a8", [128, 2, 16, 2, 2], F32)
            with ExitStack() as sa:
                ht = [self.sbuf(sa, "ht%d" % i, [128, D], F32) for i in range(3)]

                def src_tiles(b, l=l):
                    i = b % 3
                    if l == 0:
                        if b == 0:
                            S.op("pool", lambda h, i=i: h.memset(ht[i][0:FRONT, :], 0.0), writes=[("ht", i)])
                            S.dma("sp", ht[i][FRONT:128, :], self.A["meta_tokens"], writes=[("htm", i)])
                            return ht[i][:], [("ht", i), ("htm", i)]
                        S.dma("sp", ht[i][:], self.A["x"][(b - 1) * 128:b * 128, :], writes=[("ht", i), ("htm", i)])
                        return ht[i][:], [("ht", i), ("htm", i)]
                    rd = [("h", b)] + ([("hm", 0)] if b == 0 else [])
                    S.dma("sp", ht[i][:], self.h_d[b * 128:(b + 1) * 128, :], reads=rd, writes=[("ht", i), ("htm", i)])
                    return ht[i][:], [("ht", i), ("htm", i)]

                self.norm_transpose(sa, "norm_mix", l * D, hnT, src_tiles)
                wss = self.sbuf(sa, "wss", [128, 8, 512], BF16)
                utm = [self.sbuf(sa, "utm%d" % i, [128, 512], BF16) for i in range(2)]
                ucm = [self.sbuf(sa, "ucm%d" % i, [128, 4096], BF16) for i in range(3)]
                self.load_w(wss[:], "w_in", l * D * DIN, 0, 8, DIN, OFF_SSM, 512, ["wss"])
                for b in range(NT):
                    i = b % 2
                    bank, bk = self.next_ps()
                    for k in range(8):
                        self.mm(bank[:, :], hnT[:, k, b * 128:(b + 1) * 128], wss[:, k, :], k == 0, k == 7, ["wss", ("hnT", b)], [bk])
                    self.evac(utm[i][:], bank[:, :], [bk], [("utm", i)])
                    S.dma("sp", self.u_d[b * 128:(b + 1) * 128, :], utm[i][:], reads=[("utm", i)], writes=[("ud", b)])
                for ct in range(3):
                    n = 128 if ct < 2 else 16
                    S.dma("sp", ucm[ct][0:n, :], self.u_d.rearrange("(c s) x -> c (s x)", s=8)[ct * 128:ct * 128 + n, :],
                          reads=[("ud", b) for b in range(NT)], writes=[("ucm", ct)])
                ucg = [self.sbuf(sa, "ucg%d" % i, [128, 4096], BF16) for i in range(3)]
                for ct in range(3):
                    n = 128 if ct < 2 else 16
                    engs = ("dve", "pool", "act")
                    src = ucm[ct][0:n, :].rearrange("c (s g q) -> c g s q", s=8, g=32, q=16)
                    dstv = ucg[ct][0:n, :].rearrange("c (g s q) -> c g s q", s=8, g=32, q=16)
                    for g4 in range(4):
                        self.evac(dstv[:, 8 * g4:8 * g4 + 8], src[:, 8 * g4:8 * g4 + 8], [("ucm", ct)], [("ucg", ct)], eng=engs[(ct + g4) % 3])
                for ct in range(2):
                    uv = ucg[ct][:, :].rearrange("c (g x) -> c g x", g=32)
                    for gg in range(4):
                        bank, bk = self.next_pt()
                        for gi in range(8):
                            self.tr(bank[:, gi * 128:(gi + 1) * 128], uv[:, 8 * gg + gi], self.ident[:, :], [("ucg", ct)], [bk])
                        self.evac(UT[:, 8 * gg:8 * gg + 8, ct * 128:(ct + 1) * 128], bank[:, :].rearrange("p (g c) -> p g c", g=8), [bk], ["UT"])
                uv = ucg[2][0:16, :].rearrange("c (g x) -> c g x", g=32)
                bank, bk = self.next_pt()
                for g in range(32):
                    self.tr(bank[:, g * 16:(g + 1) * 16], uv[:, g], self.ident[0:16, 0:16], [("ucg", 2)], [bk])
                self.evac(UT[:, :, 256:272], bank[:, 0:512].rearrange("p (g c) -> p g c", g=32), [bk], ["UT"])
                S.barrier()
            if self.stop_after == "ssmA":
                return
            with ExitStack() as sb_:
                Bp = self.sbuf(sb_, "Bp", [128, 2, 16, 2, 128], BF16)
                dcol = self.sbuf(sb_, "dcol", [128, 32], F32)
                mfb = self.sbuf(sb_, "mfb", [128, 256], F32)
                tau = self.sbuf(sb_, "tau", [128, 2, 16, 9], F32)
                S.dma("sp", mfb[:], self.A["c_maskfb"], writes=["mfb"])
                S.dma("sp", tau[:].rearrange("p d m t -> p (d m t)"), self.A["c_tau"], writes=["tau"])
                for s in range(8):
                    S.dma("sp", dcol[16 * s:16 * s + 16, :], AP(T["ssm_d"], l * 512, [[1, 16], [16, 32]]), writes=["dcol"], allow_slow_non_contiguous=True)
                prm = self.sbuf(sb_, "prm", [128, 12, 2, 16], F32)
                LR, LI, DT, X, TH, DEN, NRE, FR, FI, ABR, ABI, TMP = range(12)
                for idx_, nm_ in ((LR, "ssm_lam_re"), (LI, "ssm_lam_im")):
                    for d in range(2):
                        for m4 in range(4):
                            S.dma("sp", prm[:, idx_, d, 4 * m4:4 * m4 + 4], AP(T[nm_], l * 4096 + d * 2048 + m4 * 512, [[1, 128], [128, 4]]),
                                  writes=["prm"], allow_slow_non_contiguous=True)
                ldt = self.sbuf(sb_, "ldt", [128, 2, 16, 2], F32)
                S.dma("sp", ldt[:].rearrange("p d m j -> p (d m j)"), AP(T["ssm_log_dt"], l * 64, [[0, 128], [1, 64]]), writes=["ldt"])
                for j in range(2):
                    S.op("dve", lambda h, j=j: h.tensor_copy(out=prm[64 * j:64 * j + 64, DT], in_=ldt[64 * j:64 * j + 64, :, :, j]), reads=["ldt"], writes=["prm"])
                braw = self.sbuf(sb_, "braw", [128, 2, 2, 16, 16], F32)
                craw = self.sbuf(sb_, "craw", [128, 2, 2, 16, 16], F32)
                for ri, nm in enumerate(("ssm_b_re", "ssm_b_im")):
                    for d in range(2):
                        for m4 in range(4):
                            S.dma("sp", braw[:, ri, d, 4 * m4:4 * m4 + 4, :], AP(T[nm], l * 65536 + d * 32768 + m4 * 8192, [[16, 128], [2048, 4], [1, 16]]), writes=["braw"])
                ctmp = [self.sbuf(sb_, "ctmp%d" % i, [128, 2, 64], F32) for i in range(2)]
                cc = 0
                for ri, nm in enumerate(("ssm_c_re", "ssm_c_im")):
                    for d in range(2):
                        for i4 in range(4):
                            ci = cc % 2
                            cc += 1
                            S.dma("sp", ctmp[ci][:], AP(T[nm], l * 65536 + d * 32768 + i4 * 8192, [[64, 128], [0, 2], [1, 64]]), writes=[("ctmp", ci)])
                            bank, bk = self.next_ps()
                            S.op("pe", lambda h, bank=bank, ci=ci: h.transpose(out=bank[:, 0:128], in_=ctmp[ci][:].rearrange("p a n -> p (a n)"), identity=self.identf[:]),
                                 reads=[("ctmp", ci), "identf"], writes=[bk])
                            for j in range(2):
                                src = bank[64 * j:64 * j + 64, 0:128].rearrange("p (ml jj q) -> p ml jj q", ml=4, jj=2, q=16)[:, :, j, :]
                                self.evac(craw[64 * j:64 * j + 64, ri, d, 4 * i4:4 * i4 + 4, :], src, [bk], ["craw"], eng="dve")
                P = prm
                v = lambda idx: P[:, idx].rearrange("p d m -> p (d m)")
                S.op("act", lambda h: h.activation(out=v(DT), in_=v(DT), func=AF.Exp), reads=["prm"], writes=["prm"])
                S.op("dve", lambda h: h.tensor_tensor(out=v(X), in0=v(LR), in1=v(DT), op=ALU.mult), reads=["prm"], writes=["prm"])
                S.op("dve", lambda h: h.tensor_tensor(out=v(TH), in0=v(LI), in1=v(DT), op=ALU.mult), reads=["prm"], writes=["prm"])
                pw = self.sbuf(sb_, "pw", [128, 10, 288], F32)
                XE, THE, MC, MB, SN, CS, KI, PCR, PCI, PBR = range(10)
                pbi = self.sbuf(sb_, "pbi", [128, 288], F32)
                ki = self.sbuf(sb_, "ki", [128, 288], I32)
                w3 = lambda a: a.rearrange("p (dm t) -> p dm t", t=9)
                tau3 = tau[:].rearrange("p d m t -> p (d m) t")
                S.op("dve", lambda h: h.tensor_tensor(out=w3(pw[:, XE]), in0=bc(v(X), 2, 9), in1=tau3, op=ALU.mult), reads=["prm", "tau"], writes=["pw"])
                S.op("dve", lambda h: h.tensor_tensor(out=w3(pw[:, THE]), in0=bc(v(TH), 2, 9), in1=tau3, op=ALU.mult), reads=["prm", "tau"], writes=["pw"])
                S.op("act", lambda h: h.activation(out=pw[:, MC], in_=pw[:, XE], func=AF.Exp), reads=["pw"], writes=["pw"])
                S.op("act", lambda h: h.activation(out=pw[:, MB], in_=pw[:, XE], func=AF.Exp, scale=-1.0), reads=["pw"], writes=["pw"])

                def sin_of(dst, shift):
                    S.op("dve", lambda h: h.tensor_scalar(out=ki[:], in0=pw[:, THE], scalar1=shift, scalar2=1.0 / TWO_PI, op0=ALU.add, op1=ALU.mult), reads=["pw"], writes=["ki"])
                    S.op("dve", lambda h: h.tensor_copy(out=pw[:, KI], in_=ki[:]), reads=["ki"], writes=["pw"])
                    S.op("dve", lambda h: h.scalar_tensor_tensor(out=pw[:, KI], in0=pw[:, KI], scalar=-TWO_PI, in1=pw[:, THE], op0=ALU.mult, op1=ALU.add), reads=["pw"], writes=["pw"])
                    S.op("dve", lambda h: h.tensor_scalar(out=pw[:, KI], in0=pw[:, KI], scalar1=shift, scalar2=math.pi, op0=ALU.add, op1=ALU.min), reads=["pw"], writes=["pw"])
                    S.op("dve", lambda h: h.tensor_scalar(out=pw[:, KI], in0=pw[:, KI], scalar1=-math.pi, scalar2=None, op0=ALU.max), reads=["pw"], writes=["pw"])
                    S.op("act", lambda h: h.activation(out=dst, in_=pw[:, KI], func=AF.Sin), reads=["pw"], writes=["pw"])

                sin_of(pw[:, SN], 0.0)
                sin_of(pw[:, CS], math.pi / 2)
                S.op("dve", lambda h: h.tensor_tensor(out=pw[:, PCR], in0=pw[:, MC], in1=pw[:, CS], op=ALU.mult), reads=["pw"], writes=["pw"])
                S.op("dve", lambda h: h.tensor_tensor(out=pw[:, PCI], in0=pw[:, MC], in1=pw[:, SN], op=ALU.mult), reads=["pw"], writes=["pw"])
                S.op("dve", lambda h: h.tensor_tensor(out=pw[:, PBR], in0=pw[:, MB], in1=pw[:, CS], op=ALU.mult), reads=["pw"], writes=["pw"])
                S.op("dve", lambda h: h.scalar_tensor_tensor(out=pbi[:], in0=pw[:, MB], scalar=-1.0, in1=pw[:, SN], op0=ALU.mult, op1=ALU.mult), reads=["pw"], writes=["pbi"])
                pcr4 = pw[:, PCR].rearrange("p (d m t) -> p d m t", d=2, m=16)
                pci4 = pw[:, PCI].rearrange("p (d m t) -> p d m t", d=2, m=16)
                for d, ti in ((0, 1), (1, 6)):
                    S.op("dve", lambda h, d=d, ti=ti: h.tensor_copy(out=P[:, ABR, d, :], in_=pcr4[:, d, :, ti]), reads=["pw"], writes=["prm"])
                    S.op("dve", lambda h, d=d, ti=ti: h.tensor_copy(out=P[:, ABI, d, :], in_=pci4[:, d, :, ti]), reads=["pw"], writes=["prm"])
                for d in range(2):
                    S.op("dve", lambda h, d=d: h.tensor_copy(out=a8[:, d, :, 0, :], in_=bc(pcr4[:, d, :, 8], 2, 2)), reads=["pw"], writes=["a8"])
                    S.op("dve", lambda h, d=d: h.tensor_copy(out=a8[:, d, :, 1, 0], in_=pci4[:, d, :, 8]), reads=["pw"], writes=["a8"])
                    S.op("dve", lambda h, d=d: h.tensor_scalar(out=a8[:, d, :, 1, 1], in0=pci4[:, d, :, 8], scalar1=-1.0, scalar2=None, op0=ALU.mult), reads=["pw"], writes=["a8"])
                tt = lambda o, a, b, op: S.op("dve", lambda h: h.tensor_tensor(out=v(o), in0=v(a), in1=v(b), op=op), reads=["prm"], writes=["prm"])
                tt(DEN, LR, LR, ALU.mult)
                tt(TMP, LI, LI, ALU.mult)
                tt(DEN, DEN, TMP, ALU.add)
                S.op("dve", lambda h: h.reciprocal(out=v(DEN), in_=v(DEN)), reads=["prm"], writes=["prm"])
                S.op("dve", lambda h: h.tensor_scalar(out=v(NRE), in0=v(ABR), scalar1=-1.0, scalar2=None, op0=ALU.add), reads=["prm"], writes=["prm"])
                tt(FR, NRE, LR, ALU.mult)
                tt(TMP, ABI, LI, ALU.mult)
                tt(FR, FR, TMP, ALU.add)
                tt(FR, FR, DEN, ALU.mult)
                tt(FI, ABI, LR, ALU.mult)
                tt(TMP, NRE, LI, ALU.mult)
                tt(FI, FI, TMP, ALU.subtract)
                tt(FI, FI, DEN, ALU.mult)
                bb = self.sbuf(sb_, "bb", [128, 2, 32, 16], F32)
                m1 = self.sbuf(sb_, "m1", [128, 4096], F32)
                m2 = self.sbuf(sb_, "m2", [128, 4096], F32)
                b3 = lambda ri: braw[:, ri].rearrange("p d m q -> p (d m) q")
                c3 = lambda ri: craw[:, ri].rearrange("p d m q -> p (d m) q")
                m13 = m1[:, 0:512].rearrange("p (a q) -> p a q", q=16)
                m23 = m2[:, 0:512].rearrange("p (a q) -> p a q", q=16)
                fr3, fi3 = bc(v(FR), 2, 16), bc(v(FI), 2, 16)
                S.op("dve", lambda h: h.tensor_tensor(out=m13, in0=fr3, in1=b3(0), op=ALU.mult), reads=["prm", "braw"], writes=["m1"])
                S.op("pool", lambda h: h.tensor_tensor(out=m23, in0=fi3, in1=b3(1), op=ALU.mult), reads=["prm", "braw"], writes=["m2"])
                S.op("dve", lambda h: h.tensor_tensor(out=bb[:, 0], in0=m13, in1=m23, op=ALU.subtract), reads=["m1", "m2"], writes=["bb"])
                S.op("dve", lambda h: h.tensor_tensor(out=m13, in0=fr3, in1=b3(1), op=ALU.mult), reads=["prm", "braw", "bb"], writes=["m1"])
                S.op("pool", lambda h: h.tensor_tensor(out=m23, in0=fi3, in1=b3(0), op=ALU.mult), reads=["prm", "braw", "bb"], writes=["m2"])
                S.op("dve", lambda h: h.tensor_tensor(out=bb[:, 1], in0=m13, in1=m23, op=ALU.add), reads=["m1", "m2"], writes=["bb"])
                pw3 = lambda idx: w3(pw[:, idx])[:, :, 0:8]
                pbi3 = w3(pbi[:])[:, :, 0:8]
                m14 = m1[:].rearrange("p (a s q) -> p a s q", s=8, q=16)
                m24 = m2[:].rearrange("p (a s q) -> p a s q", s=8, q=16)

                def cplx(dst, outkey, ar, ai, br, bi, neg_im):
                    A_r, A_i = bc(ar, 3, 16), bc(ai, 3, 16)
                    B_r, B_i = bc(br, 2, 8), bc(bi, 2, 8)
                    o_re = dst[:, :, :, 0, :].rearrange("p d m (s q) -> p (d m) s q", s=8)
                    o_im = dst[:, :, :, 1, :].rearrange("p d m (s q) -> p (d m) s q", s=8)
                    S.op("dve", lambda h: h.tensor_tensor(out=m14, in0=A_r, in1=B_r, op=ALU.mult), reads=["pw", "pbi", "bb", "craw", outkey], writes=["m1"])
                    S.op("pool", lambda h: h.tensor_tensor(out=m24, in0=A_i, in1=B_i, op=ALU.mult), reads=["pw", "pbi", "bb", "craw", outkey], writes=["m2"])
                    S.op("dve", lambda h: h.tensor_tensor(out=o_re, in0=m14, in1=m24, op=ALU.subtract), reads=["m1", "m2"], writes=[outkey])
                    S.op("dve", lambda h: h.tensor_tensor(out=m14, in0=A_r, in1=B_i, op=ALU.mult), reads=["pw", "pbi", "bb", "craw", outkey], writes=["m1"])
                    S.op("pool", lambda h: h.tensor_tensor(out=m24, in0=A_i, in1=B_r, op=ALU.mult), reads=["pw", "pbi", "bb", "craw", outkey], writes=["m2"])
                    if neg_im:
                        S.op("dve", lambda h: h.scalar_tensor_tensor(out=o_im, in0=m14, scalar=-1.0, in1=m24, op0=ALU.mult, op1=ALU.subtract), reads=["m1", "m2"], writes=[outkey])
                    else:
                        S.op("dve", lambda h: h.tensor_tensor(out=o_im, in0=m14, in1=m24, op=ALU.add), reads=["m1", "m2"], writes=[outkey])

                cplx(Bp, "Bp", pw3(PBR), pbi3, bb[:, 0], bb[:, 1], False)
                cplx(Cp, "Cp", pw3(PCR), pw3(PCI), c3(0), c3(1), True)
                if l == 0 and "BpCp" in self.tap_out:
                    S.op("dve", lambda h: h.tensor_copy(out=m1[:], in_=Bp[:].rearrange("p d m c x -> p (d m c x)")[:, 0:4096]), reads=["Bp", "m1"], writes=["m1"])
                    S.op("dve", lambda h: h.tensor_copy(out=m2[:], in_=Cp[:].rearrange("p d m c x -> p (d m c x)")[:, 0:4096]), reads=["Cp", "m2"], writes=["m2"])
                    S.dma("sp", self.tap_out["BpCp"][:, 0:4096], m1[:], reads=["m1"])
                    S.dma("sp", self.tap_out["BpCp"][:, 4096:8192], m2[:], reads=["m2"])
                    S.barrier()
                tk = [self.sbuf(sb_, "tk%d" % i, [128, 256], F32) for i in range(2)]
                t2k = [self.sbuf(sb_, "t2k%d" % i, [128, 128], F32) for i in range(2)]
                for g in range(32):
                    m, j = g // 2, g % 2
                    r0 = 64 * j
                    i = g % 2
                    bank, bk = self.next_ps()
                    for d in range(2):
                        for c2 in range(2):
                            self.mm(bank[:, d * 128:(d + 1) * 128], Bp[r0:r0 + 64, d, m, c2, :], Cp[r0:r0 + 64, d, m, c2, :], c2 == 0, c2 == 1, ["Bp", "Cp"], [bk])
                    S.op("dve", lambda h, bank=bank, i=i: h.tensor_tensor(out=tk[i][:], in0=bank[:, 0:256], in1=mfb[:], op=ALU.mult), reads=[bk, "mfb"], writes=[("tk", i)])
                    S.op("pool", lambda h, i=i: h.tensor_tensor(out=t2k[i][:], in0=tk[i][:, 0:128], in1=tk[i][:, 128:256], op=ALU.add), reads=[("tk", i)], writes=[("t2k", i)])
                    S.op("dve", lambda h, i=i, g=g: h.scalar_tensor_tensor(out=Kb[:, g, :], in0=self.identf[:], scalar=dcol[:, g:g + 1], in1=t2k[i][:], op0=ALU.mult, op1=ALU.add),
                         reads=[("t2k", i), "dcol", "identf"], writes=["Kb"])
                for d in range(2):
                    for mm4 in range(4):
                        bank, bk = self.next_pt()
                        for mi in range(4):
                            m = 4 * mm4 + mi
                            for c2 in range(2):
                                o0 = (mi * 2 + c2) * 128
                                self.tr(bank[:, o0:o0 + 128], Bp[:, d, m, c2, :], self.ident[:, :], ["Bp"], [bk])
                        self.evac(BpT[:, d, 4 * mm4:4 * mm4 + 4].rearrange("p m c x -> p (m c x)"), bank[:, :], [bk], ["BpT"])
                S.barrier()
            if self.stop_after == "ssmB":
                return
            with ExitStack() as sc:
                W_ = NCH + 2
                Rbf = self.sbuf(sc, "Rbf", [128, 2, 16, 2, W_], BF16)
                with ExitStack() as sc2:
                    RZ = self.sbuf(sc2, "RZ", [128, 2, 16, 2, W_], F32)
                    Wst = self.sbuf(sc2, "Wst", [128, 32, 2], F32)
                    tT = self.sbuf(sc2, "tT", [128, 32, 2], F32)
                    pP = self.sbuf(sc2, "pP", [128, 32, 2, 2], F32)
                    plane = 16 * 2 * W_

                    def rz_cols(colf, colb):
                        a = RZ[:, :, :, :, colf]
                        return bass.AP(tensor=a.tensor, offset=a.offset, ap=[list(a.ap[0]), [plane + colb - colf, 2], list(a.ap[2]), list(a.ap[3])])

                    S.op("pool", lambda h: h.memset(RZ[:, 0, :, :, 0:1], 0.0), writes=["RZ"])
                    S.op("pool", lambda h: h.memset(RZ[:, 1, :, :, W_ - 1:W_], 0.0), writes=["RZ"])
                    S.op("pool", lambda h: h.memset(Wst[:], 0.0), writes=["W"])
                    for d in range(2):
                        for m in range(16):
                            bre, kre = self.next_ps()
                            bim, kim = self.next_ps()
                            for j in range(2):
                                g = 2 * m + j
                                self.mm(bre[64 * j:64 * j + 64, 0:NCH], BpT[:, d, m, 0, 64 * j:64 * j + 64], UT[:, g, :], True, True, ["BpT", "UT"], [kre])
                                self.mm(bim[64 * j:64 * j + 64, 0:NCH], BpT[:, d, m, 1, 64 * j:64 * j + 64], UT[:, g, :], True, True, ["BpT", "UT"], [kim])
                            self.evac(RZ[:, d, m, 0, 1:1 + NCH], bre[:, 0:NCH], [kre], ["RZ"])
                            self.evac(RZ[:, d, m, 1, 1:1 + NCH], bim[:, 0:NCH], [kim], ["RZ"])
                    coef = a8[:].rearrange("p d m u c -> p (d m) u c")
                    p0 = pP[:, :, 0, :]
                    p1 = pP[:, :, 1, :]
                    p1r = bass.AP(tensor=p1.tensor, offset=p1.offset + 1, ap=[list(p1.ap[0]), list(p1.ap[1]), [-1, 2]])
                    wst3 = Wst[:].rearrange("p (d m) c -> p d m c", d=2)
                    tt3 = tT[:].rearrange("p (d m) c -> p d m c", d=2)
                    prev_out = None
                    for k in range(NCH - 1):
                        zc = rz_cols(k + 1, NCH - k)
                        S.op("dve", lambda h, zc=zc: h.tensor_tensor(out=tt3, in0=wst3, in1=zc, op=ALU.add), reads=["W", "RZ"], writes=["T"])
                        if prev_out is not None:
                            S.op("dve", lambda h, po=prev_out: h.tensor_copy(out=po, in_=wst3), reads=["W"], writes=["RZ"])
                        S.op("dve", lambda h: h.tensor_tensor(out=pP[:], in0=bc(tT[:], 2, 2), in1=coef, op=ALU.mult), reads=["T", "a8"], writes=["P"])
                        S.op("dve", lambda h: h.tensor_tensor(out=Wst[:], in0=p0, in1=p1r, op=ALU.add), reads=["P"], writes=["W"])
                        prev_out = zc
                    S.op("dve", lambda h, po=prev_out: h.tensor_copy(out=po, in_=wst3), reads=["W"], writes=["RZ"])
                    cengs = ("act", "dve", "pool", "act")
                    for d in range(2):
                        for mh in range(2):
                            self.evac(Rbf[:, d, 8 * mh:8 * mh + 8].rearrange("p m c x -> p (m c x)"), RZ[:, d, 8 * mh:8 * mh + 8].rearrange("p m c x -> p (m c x)"),
                                      ["RZ"], ["Rbf"], eng=cengs[2 * d + mh])
                    S.barrier()
                if self.stop_after == "ssmC" and "Rbf" not in self.tap_out:
                    return
                if l == 0 and "Rbf" in self.tap_out:
                    with ExitStack() as sc3:
                        tmp = self.sbuf(sc3, "tmp", [128, 2 * 16 * 2 * (NCH + 2)], F32)
                        S.op("dve", lambda h: h.tensor_copy(out=tmp[:], in_=Rbf[:].rearrange("p d m c x -> p (d m c x)")), reads=["Rbf"], writes=["tmp"])
                        self.tap("Rbf", tmp[:], ["tmp"])
                        S.barrier()
                    if self.stop_after == "ssmC":
                        return
                with ExitStack() as sd:
                    zcm = [self.sbuf(sd, "zcm%d" % i, [128, 4096], BF16) for i in range(3)]
                    for ct in range(3):
                        n = 128 if ct < 2 else 16
                        c0 = ct * 128
                        for g4 in range(8):
                            bank, bk = self.next_ps()
                            for gi in range(4):
                                g = 4 * g4 + gi
                                m, j = g // 2, g % 2
                                r0 = 64 * j
                                o_ap = bank[0:n, gi * 128:(gi + 1) * 128]
                                self.mm(o_ap, UT[:, g, c0:c0 + n], Kb[:, g, :], True, False, ["Kb", "UT"], [bk])
                                for d in range(2):
                                    off = 0 if d == 0 else 2
                                    for c2 in range(2):
                                        self.mm(o_ap, Rbf[r0:r0 + 64, d, m, c2, off + c0:off + c0 + n], Cp[r0:r0 + 64, d, m, c2, :], False, (d == 1 and c2 == 1), ["Cp", "Rbf"], [bk])
                            dst = zcm[ct][0:n, :].rearrange("c (t g p) -> c g t p", t=8, g=32, p=16)[:, 4 * g4:4 * g4 + 4]
                            src = bank[0:n, :].rearrange("c (g t p) -> c g t p", g=4, t=8, p=16)
                            S.op("act", lambda h, dst=dst, src=src: h.activation(out=dst, in_=src, func=AF.Gelu), reads=[bk], writes=[("zcm", ct)])
                    for ct in range(3):
                        if self.stop_after in ("ssmD1", "ssmD2"):
                            continue
                        n = 128 if ct < 2 else 16
                        S.dma("sp", self.z_d.rearrange("(c s) x -> c (s x)", s=8)[ct * 128:ct * 128 + n, :], zcm[ct][0:n, :], reads=[("zcm", ct)], writes=["zd"])
                    S.barrier()

    def phase_ssm_glu(self, l, yTs):
        S = self.S
        with ExitStack() as st:
            zT = self.sbuf(st, "zT", [128, 4, LP], BF16)
            ztm = [self.sbuf(st, "ztm%d" % i, [128, 512], BF16) for i in range(2)]
            wgl = self.sbuf(st, "wgl", [128, 4, 512], BF16)
            glb = self.sbuf(st, "glb", [128, 4], F32)
            sgs = [self.sbuf(st, "sgs%d" % i, [128, 512], BF16) for i in range(2)]
            self.load_w(wgl[:], "ssm_glu_w", l * 512 * 512, 0, 4, 512, 0, 512, ["wgl"])
            S.dma("sp", glb[:], AP(self.T["ssm_glu_b"], l * 512, [[1, 128], [128, 4]]), writes=["glb"], allow_slow_non_contiguous=True)
            for b in range(NT):
                i = b % 2
                S.dma("sp", ztm[i][:], self.z_d[b * 128:(b + 1) * 128, :], reads=["zd"], writes=[("ztm", i)])
                bank, bk = self.next_pt()
                for j in range(4):
                    self.tr(bank[:, j * 128:(j + 1) * 128], ztm[i][:, j * 128:(j + 1) * 128], self.ident[:, :], [("ztm", i)], [bk])
                self.evac(zT[:, :, b * 128:(b + 1) * 128], bank[:, 0:512].rearrange("p (j t) -> p j t", j=4), [bk], ["zT"])
            cnt = 0
            for co in range(4):
                for (t0, n) in TCH:
                    bank, bk = self.next_ps()
                    for k in range(4):
                        self.mm(bank[:, 0:n], wgl[:, k, co * 128:(co + 1) * 128], zT[:, k, t0:t0 + n], k == 0, k == 3, ["wgl", "zT"], [bk])
                    i = cnt % 2
                    cnt += 1
                    S.op("act", lambda h, bank=bank, n=n, i=i, co=co: h.activation(out=sgs[i][:, 0:n], in_=bank[:, 0:n], func=AF.Sigmoid, bias=glb[:, co:co + 1]),
                         reads=[bk, "glb"], writes=[("sgs", i)])
                    S.op(self.ew(), lambda h, n=n, i=i, co=co, t0=t0: h.tensor_tensor(out=yTs[:, co, t0:t0 + n], in0=zT[:, co, t0:t0 + n], in1=sgs[i][:, 0:n], op=ALU.mult),
                         reads=[("sgs", i), "zT"], writes=["yT_all"])
            S.barrier()
        self.tap_yT("yTs", yTs, l)


def build_nc(nlayers=NL, taps=(), stop_after=None):
    return K(nlayers=nlayers, taps=taps, stop_after=stop_after).build()


def make_in_maps(inputs):
    consts = make_consts()
    shared = {k: np.ascontiguousarray(np.asarray(v), dtype=np.float32) for k, v in inputs.items() if k != "x"}
    x = np.asarray(inputs["x"], dtype=np.float32)
    maps = []
    for c in range(8):
        m = dict(shared)
        m["x"] = np.ascontiguousarray(x[c])
        m.update(consts)
        maps.append(m)
    return maps


def kernel(**inputs):
    nc = build_nc()
    res = run_bass_kernel_spmd(nc, make_in_maps(inputs), core_ids=list(range(8)))
    return np.stack([np.asarray(r["out"], dtype=np.float32) for r in res.results], axis=0)
```

```python
import math
from contextlib import ExitStack

import numpy as np
import concourse.bass as bass
import concourse.mybir as mybir
from concourse.bass_utils import run_bass_kernel_spmd
from concourse.alu_op_type import AluOpType as ALU

F32 = mybir.dt.float32
BF16 = mybir.dt.bfloat16
I32 = mybir.dt.int32
AF = mybir.ActivationFunctionType

D = 1024
SEQ = 2048
NMETA = 16
LP = 2176
NT = 17
FRONT = 112
DIN = 4864
DFF = 4096
NL = 2
EPS = 1e-6
OFF_K, OFF_V, OFF_SSM, OFF_POOL, OFF_GATE = 512, 640, 768, 1280, 1792
NCH = 272
TCH = [(0, 512), (512, 512), (1024, 512), (1536, 512), (2048, 128)]


class Sched:
    ENG = ("pe", "act", "dve", "pool", "sp")

    def __init__(self, nc, es, n_dma_sems=28):
        self.nc = nc
        self.lists = {e: [] for e in self.ENG}
        self.sem = {e: es.enter_context(nc.semaphore("s_" + e)) for e in self.ENG}
        self.count = {e: 0 for e in self.ENG}
        self.waited = {e: {} for e in self.ENG}
        self.dma_sems = [es.enter_context(nc.semaphore("s_dma%d" % i)) for i in range(n_dma_sems)]
        self.dma_tot = [0] * n_dma_sems
        self.dma_rr = 0
        self.dma_rr_sw = 0
        self.res = {}
        self.ninstr = 0

    def _semobj(self, k):
        return self.sem[k] if isinstance(k, str) else self.dma_sems[k[1]]

    def _wait(self, eng, tok, raw=False):
        k, v = tok
        if k == eng and (not raw or eng == "pe"):
            return
        cur = self.waited[eng].get(k, 0)
        if cur >= v:
            return
        self.waited[eng][k] = v
        so = self._semobj(k)
        self.lists[eng].append(lambda h, so=so, v=v: h.wait_ge(so, v))

    def _deps(self, eng, reads, writes):
        for r in reads:
            st = self.res.get(r)
            if st and st[0] is not None:
                self._wait(eng, st[0], raw=True)
        for w in writes:
            st = self.res.get(w)
            if st:
                if st[0] is not None:
                    self._wait(eng, st[0])
                for k, v in st[1].items():
                    self._wait(eng, (k, v))

    def _commit(self, tok, reads, writes):
        k, v = tok
        for r in reads:
            st = self.res.setdefault(r, [None, {}])
            if st[1].get(k, 0) < v:
                st[1][k] = v
        for w in writes:
            self.res[w] = [tok, {}]

    def op(self, eng, fn, reads=(), writes=()):
        self._deps(eng, reads, writes)
        self.count[eng] += 1
        v = self.count[eng]
        so = self.sem[eng]
        self.lists[eng].append(lambda h, fn=fn, so=so: fn(h).then_inc(so, 1))
        self._commit((eng, v), reads, writes)
        self.ninstr += 1

    def dma(self, eng, out, in_, reads=(), writes=(), **kw):
        self._deps(eng, reads, writes)
        if eng == "pool":
            i = 16 + self.dma_rr_sw
            self.dma_rr_sw = (self.dma_rr_sw + 1) % (len(self.dma_sems) - 16)
        else:
            i = self.dma_rr
            self.dma_rr = (self.dma_rr + 1) % 16
        if self.dma_tot[i] > 0:
            k = ("d", i)
            cur = self.waited[eng].get(k, 0)
            if cur < self.dma_tot[i]:
                self.waited[eng][k] = self.dma_tot[i]
                so0, v0 = self.dma_sems[i], self.dma_tot[i]
                self.lists[eng].append(lambda h, so0=so0, v0=v0: h.wait_ge(so0, v0))
        self.dma_tot[i] += 16
        v = self.dma_tot[i]
        so = self.dma_sems[i]
        self.lists[eng].append(
            lambda h, so=so, out=out, in_=in_, kw=kw: h.dma_start(out=out, in_=in_, **kw).then_inc(so, 16))
        self._commit((("d", i), v), reads, writes)
        self.ninstr += 1

    def barrier(self):
        for e in self.ENG:
            for f in self.ENG:
                if f != e and self.count[f] > 0:
                    self._wait(e, (f, self.count[f]))
            for i, t in enumerate(self.dma_tot):
                if t > 0:
                    self._wait(e, (("d", i), t))
        self.res = {}

    def final_wait(self, eng="sp"):
        for f in self.ENG:
            if f != eng and self.count[f] > 0:
                self._wait(eng, (f, self.count[f]))
        for i, t in enumerate(self.dma_tot):
            if t > 0:
                self._wait(eng, (("d", i), t))

    def emit(self, block):
        L = self.lists

        @block.sync
        def _(h):
            for f in L["sp"]:
                f(h)

        @block.scalar
        def _(h):
            for f in L["act"]:
                f(h)

        @block.vector
        def _(h):
            for f in L["dve"]:
                f(h)

        @block.gpsimd
        def _(h):
            for f in L["pool"]:
                f(h)

        @block.tensor
        def _(h):
            for f in L["pe"]:
                f(h)


def make_consts():
    c = {}
    c["c_ident"] = np.eye(128, dtype=np.float32)
    t = np.arange(LP, dtype=np.float32) - FRONT
    r = np.arange(128)
    j = r % 64
    invf = (10000.0 ** (-(np.arange(32, dtype=np.float32)) * 2.0 / 64)).astype(np.float32)
    ang = (t[None, :] * invf[j % 32][:, None]).astype(np.float32)
    c["c_cos"] = np.cos(ang).astype(np.float32)
    sgn = np.where(j < 32, -1.0, 1.0).astype(np.float32)
    c["c_sin"] = (np.sin(ang) * sgn[:, None]).astype(np.float32)
    kl = np.arange(128)[:, None]
    ql = np.arange(128)[None, :]
    mp = np.where(kl >= ql, 0.0, -30000.0).astype(np.float32)
    mn = np.where(kl <= ql, 0.0, -30000.0).astype(np.float32)
    c["c_maskp"] = np.tile(mp, (1, 4))
    c["c_maskn"] = np.tile(mn, (1, 4))
    s = np.arange(128)[:, None] // 16
    tt = np.arange(128)[None, :] // 16
    c["c_maskfb"] = np.concatenate([(tt >= s), (tt <= s)], axis=1).astype(np.float32)
    tau = np.zeros((128, 2, 16, 9), np.float32)
    tau[:, 0, :, 0:8] = np.arange(8, dtype=np.float32)[None, None, :]
    tau[:, 1, :, 0:8] = (7 - np.arange(8, dtype=np.float32))[None, None, :]
    tau[:, :, :, 8] = 8.0
    c["c_tau"] = tau.reshape(128, 288)
    rc = np.zeros((4, LP), np.float32)
    L = NMETA + SEQ
    idx = np.arange(L)
    for gi, w in enumerate((2, 4, 8, 16)):
        lo = np.clip(idx - w // 2, 0, L)
        hi = np.clip(idx + w // 2, 0, L)
        rc[gi, FRONT:] = 1.0 / (hi - lo).astype(np.float32)
    c["c_rcnt"] = rc
    return c


CONST_SHAPES = {"c_ident": [128, 128], "c_cos": [128, LP], "c_sin": [128, LP], "c_maskp": [128, 512],
                "c_maskn": [128, 512], "c_maskfb": [128, 256], "c_tau": [128, 288], "c_rcnt": [4, LP]}

IN_SHAPES = {
    "x": [SEQ, D], "meta_tokens": [NMETA, D], "norm_mix": [NL, D], "w_in": [NL, D, DIN], "attn_sink": [NL, 8],
    "ssm_lam_re": [NL, 2, 32, 64], "ssm_lam_im": [NL, 2, 32, 64], "ssm_log_dt": [NL, 2, 32],
    "ssm_b_re": [NL, 2, 32, 64, 16], "ssm_b_im": [NL, 2, 32, 64, 16], "ssm_c_re": [NL, 2, 32, 16, 64],
    "ssm_c_im": [NL, 2, 32, 16, 64], "ssm_d": [NL, 512], "ssm_glu_w": [NL, 512, 512], "ssm_glu_b": [NL, 512],
    "pool_w": [NL, 4, 128, 128], "pool_scale": [NL, 512], "w_branch": [NL, 3, 512, D], "w_out": [NL, D, D],
    "norm_mlp": [NL, D], "w_up": [NL, D, DFF], "w_down": [NL, DFF, D], "norm_final": [D],
}


def AP(t, offset, ap):
    return bass.AP(tensor=t, offset=offset, ap=[list(a) for a in ap])


def bc(ap, axis, n):
    a = ap.unsqueeze(axis)
    shp = list(a.shape)
    shp[axis] = n
    return a.to_broadcast(shp)


class K:
    def __init__(self, nlayers=NL, taps=(), stop_after=None):
        self.nlayers = nlayers
        self.stop_after = stop_after
        nc = self.nc = bass.Bass("TRN2", target_bir_lowering=False)
        self.T = {}
        for k, shp in IN_SHAPES.items():
            self.T[k] = nc.dram_tensor(k, shp, F32, kind="ExternalInput")
        for k, shp in CONST_SHAPES.items():
            self.T[k] = nc.dram_tensor(k, shp, F32, kind="ExternalInput")
        self.A = {k: v.ap() for k, v in self.T.items()}
        self.out = nc.dram_tensor("out", [SEQ, D], F32, kind="ExternalOutput").ap()
        self.h_d = nc.dram_tensor("h_scr", [LP, D], F32, kind="Internal").ap()
        self.u_t = nc.dram_tensor("u_scr", [LP, 512], BF16, kind="Internal")
        self.u_d = self.u_t.ap()
        self.z_t = nc.dram_tensor("z_scr", [LP, 512], BF16, kind="Internal")
        self.z_d = self.z_t.ap()
        self.tap_out = {}
        for name, shp in taps:
            self.tap_out[name] = nc.dram_tensor("tap_" + name, shp, F32, kind="ExternalOutput").ap()
        self.uid = 0
        self.rr = {"ps": 0, "pt": 0, "ev": 0, "ew": 0}

    def sbuf(self, st, name, shape, dt):
        self.uid += 1
        return st.enter_context(self.nc.sbuf_tensor("%s_%d" % (name, self.uid), shape, dt))

    def next_ps(self):
        i = self.rr["ps"]
        self.rr["ps"] = (i + 1) % len(self.ps)
        return self.ps[i], ("ps", i)

    def next_pt(self):
        i = self.rr["pt"]
        self.rr["pt"] = (i + 1) % len(self.pt)
        return self.pt[i], ("pt", i)

    def evac(self, out_ap, in_ap, reads, writes, eng=None):
        S = self.S
        if eng is None:
            self.rr["ev"] ^= 1
            eng = "act" if self.rr["ev"] else "dve"
        if eng == "act":
            S.op("act", lambda h: h.activation(out=out_ap, in_=in_ap, func=AF.Copy), reads=reads, writes=writes)
        else:
            S.op(eng, lambda h: h.tensor_copy(out=out_ap, in_=in_ap), reads=reads, writes=writes)

    def ew(self):
        self.rr["ew"] ^= 1
        return "dve" if self.rr["ew"] else "pool"

    def tap(self, name, src_ap, reads):
        if name in self.tap_out:
            self.S.dma("sp", self.tap_out[name], src_ap, reads=reads)

    def load_w(self, dst_ap, name, base, rows0, nk, ld, c0, ncols, writes, eng="pool"):
        src = AP(self.T[name], base + rows0 * ld + c0, [[ld, 128], [128 * ld, nk], [1, ncols]])
        self.S.dma(eng, dst_ap, src, writes=writes)

    def mm(self, out_ap, lhsT, rhs, start, stop, reads, writes):
        self.S.op("pe", lambda h: h.matmul(out_ap, lhsT=lhsT, rhs=rhs, start=start, stop=stop), reads=reads, writes=writes)

    def tr(self, out_ap, in_ap, idn, reads, writes):
        self.S.op("pe", lambda h: h.transpose(out=out_ap, in_=in_ap, identity=idn), reads=list(reads) + ["ident"], writes=writes)

    def build(self):
        nc = self.nc
        with ExitStack() as es:
            S = self.S = Sched(nc, es)
            self.ps = [es.enter_context(nc.psum_tensor("ps%d" % i, [128, 512], F32)) for i in range(6)]
            self.pt = [es.enter_context(nc.psum_tensor("pt%d" % i, [128, 1024], BF16)) for i in range(2)]
            self.identf = self.sbuf(es, "identf", [128, 128], F32)
            self.ident = self.sbuf(es, "ident", [128, 128], BF16)
            self.small = self.sbuf(es, "small", [128, 16], F32)
            self.zt = self.sbuf(es, "zt", [FRONT, 256], F32)
            S.dma("sp", self.identf[:], self.A["c_ident"], writes=["identf"])
            S.op("dve", lambda h: h.tensor_copy(out=self.ident[:], in_=self.identf[:]), reads=["identf"], writes=["ident"])
            S.op("pool", lambda h: h.memset(self.small[:, 0:1], EPS), writes=["eps"])
            self.init_h()
            for l in range(self.nlayers):
                self.layer(l)
            S.final_wait("sp")
            with nc.Block() as block:
                S.emit(block)
        return nc

    def init_h(self):
        S = self.S
        zt = self.zt
        S.op("pool", lambda h: h.memset(zt[:], 0.0), writes=["zt"])
        for c in range(4):
            S.dma("sp", self.h_d[0:FRONT, c * 256:(c + 1) * 256], zt[:], reads=["zt"], writes=[("h", 0)])
        S.dma("sp", self.h_d[FRONT:128, :], self.A["meta_tokens"], writes=[("hm", 0)])
        for b in range(1, NT):
            S.dma("sp", self.h_d[b * 128:(b + 1) * 128, :], self.A["x"][(b - 1) * 128:b * 128, :], writes=[("h", b)])

    def norm_transpose(self, st, gname, goff, hnT, src_tiles):
        S = self.S
        small = self.small
        gain = self.sbuf(st, "gain", [128, D], F32)
        S.dma("sp", gain[:], AP(self.T[gname], goff, [[0, 128], [1, D]]), writes=["gain"])
        junk = self.sbuf(st, "junk", [128, D], BF16)
        hnb = [self.sbuf(st, "hnb%d" % i, [128, D], BF16) for i in range(2)]
        ss = self.sbuf(st, "ss", [128, 3 * NT], F32)
        for b in range(NT):
            src, rd = src_tiles(b)
            S.op("act", lambda h, src=src, b=b: h.activation(out=junk[:], in_=src, func=AF.Square, accum_out=ss[:, b:b + 1]),
                 reads=rd, writes=["junk", ("ss", b)])
            S.op("act", lambda h, b=b: h.activation(out=ss[:, NT + b:NT + b + 1], in_=ss[:, b:b + 1], func=AF.Sqrt,
                                                    scale=1.0 / D, bias=small[:, 0:1]),
                 reads=[("ss", b), "eps"], writes=[("ss1", b)])
            S.op("dve", lambda h, b=b: h.reciprocal(out=ss[:, 2 * NT + b:2 * NT + b + 1], in_=ss[:, NT + b:NT + b + 1]),
                 reads=[("ss1", b)], writes=[("ss2", b)])
            i = b % 2
            S.op("dve", lambda h, src=src, b=b, i=i: h.scalar_tensor_tensor(
                out=hnb[i][:], in0=src, scalar=ss[:, 2 * NT + b:2 * NT + b + 1], in1=gain[:], op0=ALU.mult, op1=ALU.mult),
                reads=list(rd) + [("ss2", b), "gain"], writes=[("hnb", i)])
            bank, bk = self.next_pt()
            for k in range(8):
                self.tr(bank[:, k * 128:(k + 1) * 128], hnb[i][:, k * 128:(k + 1) * 128], self.ident[:, :], [("hnb", i)], [bk])
            self.evac(hnT[:, :, b * 128:(b + 1) * 128], bank[:, :].rearrange("p (k t) -> p k t", k=8), [bk], [("hnT", b)])

    def layer(self, l):
        S = self.S
        with ExitStack() as lst:
            hnT = self.sbuf(lst, "hnT", [128, 8, LP], BF16)
            if l == 0 and "hnT" in self.tap_out:
                with ExitStack() as st:
                    tmp = self.sbuf(st, "tmp", [128, 8 * LP], F32)
                    S.op("dve", lambda h: h.tensor_copy(out=tmp[:], in_=hnT[:].rearrange("p k t -> p (k t)")), reads=[("hnT", b) for b in range(NT)], writes=["tmp"])
                    self.tap("hnT", tmp[:], ["tmp"])
                    S.barrier()
            if self.stop_after == "n1":
                return
            self.phase_ssm_core(l, hnT)
            if self.stop_after in ("ssmA", "ssmB", "ssmC", "ssmD", "ssmD1", "ssmD2"):
                return
            yT = [self.sbuf(lst, "yT%d" % c, [128, 4, LP], BF16) for c in (1,)]
            self.phase_ssm_glu(l, yT[0])
            if self.stop_after == "ssm":
                return
            yTa = self.sbuf(lst, "yTa", [128, 4, LP], BF16)
            self.phase_attn(l, hnT, yTa)
            if self.stop_after == "attn":
                return
            yTp = self.sbuf(lst, "yTp", [128, 4, LP], BF16)
            self.phase_pool(l, hnT, yTp)
            if self.stop_after == "pool":
                return
            self.phase_merge(l, hnT, [yTa, yT[0], yTp])
        if self.stop_after == "merge":
            return
        self.phase_ffn(l)

    def tap_yT(self, name, yT, l):
        S = self.S
        if l == 0 and name in self.tap_out:
            with ExitStack() as st:
                tmp = self.sbuf(st, "tmp", [128, 4 * LP], F32)
                S.op("dve", lambda h: h.tensor_copy(out=tmp[:], in_=yT[:].rearrange("p k t -> p (k t)")), reads=["yT_all"], writes=["tmp"])
                self.tap(name, tmp[:], ["tmp"])
                S.barrier()

    def phase_pool(self, l, hnT, yTp):
        S = self.S
        WX, OFF = LP + 64, 32
        with ExitStack() as st:
            wp = [self.sbuf(st, "wp%d" % i, [128, 8, 128], BF16) for i in range(2)]
            pw = [self.sbuf(st, "pw%d" % i, [128, 128], BF16) for i in range(2)]
            ua = self.sbuf(st, "ua", [128, WX], F32)
            pb = [self.sbuf(st, "pb%d" % i, [128, WX], F32) for i in range(2)]
            rcb = self.sbuf(st, "rcb", [128, LP], F32)
            dd = self.sbuf(st, "dd", [128, LP], BF16)
            psc = self.sbuf(st, "psc", [128, 4], F32)
            S.dma("sp", psc[:], AP(self.T["pool_scale"], l * 512, [[1, 128], [128, 4]]), writes=["psc"], allow_slow_non_contiguous=True)
            S.op("pool", lambda h: h.memset(ua[:], 0.0), writes=["ua"])
            S.op("pool", lambda h: h.memset(pb[0][:], 0.0), writes=["pb0"])
            S.op("pool", lambda h: h.memset(pb[1][:], 0.0), writes=["pb1"])
            shifts = [(-1, 0), (-1, 1), (-2, 2), (-4, 4)]
            for g in range(4):
                i = g % 2
                self.load_w(wp[i][:], "w_in", l * D * DIN, 0, 8, DIN, OFF_POOL + 128 * g, 128, [("wp", i)])
                S.dma("pool", pw[i][:], AP(self.T["pool_w"], (l * 4 + g) * 16384, [[128, 128], [1, 128]]), writes=[("pw", i)])
                S.dma("sp", rcb[:], AP(self.T["c_rcnt"], g * LP, [[0, 128], [1, LP]]), writes=["rcb"])
                for (t0, n) in TCH:
                    bank, bk = self.next_ps()
                    for k in range(8):
                        self.mm(bank[:, 0:n], wp[i][:, k, :], hnT[:, k, t0:t0 + n], k == 0, k == 7, [("wp", i)], [bk])
                    self.evac(ua[:, OFF + t0:OFF + t0 + n], bank[:, 0:n], [bk], ["ua"])
                cur, curk = ua, "ua"
                lo, hi = OFF - 16, OFF + LP + 16
                for lev in range(g + 1):
                    s0, s1 = shifts[lev]
                    dst, dk = pb[lev % 2], "pb%d" % (lev % 2)
                    S.op(self.ew(), lambda h, dst=dst, cur=cur, s0=s0, s1=s1: h.tensor_tensor(
                        out=dst[:, lo:hi], in0=cur[:, lo + s0:hi + s0], in1=cur[:, lo + s1:hi + s1], op=ALU.add),
                        reads=[curk], writes=[dk])
                    cur, curk = dst, dk
                oth, ok = pb[(g + 1) % 2], "pb%d" % ((g + 1) % 2)
                S.op("dve", lambda h, cur=cur, oth=oth: h.tensor_tensor(out=oth[:, OFF:OFF + LP], in0=cur[:, OFF:OFF + LP], in1=rcb[:], op=ALU.mult),
                     reads=[curk, "rcb"], writes=[ok])
                S.op("pool", lambda h, oth=oth: h.tensor_tensor(out=dd[:], in0=oth[:, OFF:OFF + LP], in1=ua[:, OFF:OFF + LP], op=ALU.subtract),
                     reads=[ok, "ua"], writes=["dd"])
                for (t0, n) in TCH:
                    bank, bk = self.next_ps()
                    self.mm(bank[:, 0:n], pw[i][:], dd[:, t0:t0 + n], True, True, [("pw", i), "dd"], [bk])
                    S.op("act", lambda h, bank=bank, n=n, t0=t0, g=g: h.activation(out=yTp[:, g, t0:t0 + n], in_=bank[:, 0:n], func=AF.Copy,
                                                                                    scale=psc[:, g:g + 1]),
                         reads=[bk, "psc"], writes=["yT_all"])
            S.barrier()
        self.tap_yT("yTp", yTp, l)

    def phase_attn(self, l, hnT, yTa):
        S = self.S
        A = self.A
        with ExitStack() as st:
            qT = self.sbuf(st, "qT", [128, 4, LP], BF16)
            kT = self.sbuf(st, "kT", [128, 2, LP], BF16)
            Va = self.sbuf(st, "Va", [128, NT, 2, 65], BF16)
            Vm = self.sbuf(st, "Vm", [16, 2, 65], BF16)
            cosT = self.sbuf(st, "cosT", [128, LP], F32)
            sinT = self.sbuf(st, "sinT", [128, LP], F32)
            mkf = self.sbuf(st, "mkf", [128, 1024], F32)
            mk = self.sbuf(st, "mk", [128, 1024], BF16)
            ww = [self.sbuf(st, "ww%d" % i, [128, 8, 128], BF16) for i in range(2)]
            wr = [self.sbuf(st, "wr%d" % i, [128, 8, 128], BF16) for i in range(2)]
            t1 = [self.sbuf(st, "t1%d" % i, [128, 512], F32) for i in range(2)]
            t2 = [self.sbuf(st, "t2%d" % i, [128, 512], F32) for i in range(2)]
            PT = [self.sbuf(st, "PT%d" % i, [128, 512], BF16) for i in range(8)]
            den = self.sbuf(st, "den", [128, 16], F32)
            snk = self.sbuf(st, "snk", [128, 8], F32)
            S.dma("sp", cosT[:], A["c_cos"], writes=["cosT"])
            S.dma("sp", sinT[:], A["c_sin"], writes=["sinT"])
            S.dma("sp", mkf[:, 0:512], A["c_maskp"], writes=["mkf"])
            S.dma("sp", mkf[:, 512:1024], A["c_maskn"], writes=["mkf"])
            S.op("dve", lambda h: h.tensor_copy(out=mk[:], in_=mkf[:]), reads=["mkf"], writes=["mk"])
            S.dma("sp", snk[:], AP(self.T["attn_sink"], l * 8, [[0, 128], [1, 8]]), writes=["snk"])
            S.op("act", lambda h: h.activation(out=snk[:], in_=snk[:], func=AF.Exp), reads=["snk"], writes=["snk"])
            S.op("pool", lambda h: h.memset(Va[:, :, :, 64:65], 1.0), writes=["Va1"])
            S.op("pool", lambda h: h.memset(Vm[:, :, 64:65], 1.0), writes=["Vm1"])
            for ct in range(6):
                i = ct % 2
                if ct < 4:
                    self.load_w(ww[i][:], "w_in", l * D * DIN, 0, 8, DIN, 128 * ct, 128, [("ww", i)])
                    dst = lambda t0, n, ct=ct: qT[:, ct, t0:t0 + n]
                else:
                    kh = ct - 4
                    self.load_w(ww[i][:, :, 0:64], "w_in", l * D * DIN, 0, 8, DIN, OFF_K + 64 * kh, 64, [("ww", i)])
                    self.load_w(ww[i][:, :, 64:128], "w_in", l * D * DIN, 0, 8, DIN, OFF_K + 64 * kh, 64, [("ww", i)])
                    dst = lambda t0, n, kh=kh: kT[:, kh, t0:t0 + n]
                wv = ww[i][:].rearrange("p k (a two j) -> p k a two j", a=2, two=2, j=32)
                rv = wr[i][:].rearrange("p k (a two j) -> p k a two j", a=2, two=2, j=32)
                S.op("dve", lambda h, wv=wv, rv=rv: h.tensor_copy(out=rv[:, :, :, 0, :], in_=wv[:, :, :, 1, :]), reads=[("ww", i)], writes=[("wr", i)])
                S.op("act", lambda h, wv=wv, rv=rv: h.activation(out=rv[:, :, :, 1, :], in_=wv[:, :, :, 0, :], func=AF.Copy), reads=[("ww", i)], writes=[("wr", i)])
                for ti, (t0, n) in enumerate(TCH):
                    j = ti % 2
                    ba, bka = self.next_ps()
                    bb_, bkb = self.next_ps()
                    for k in range(8):
                        self.mm(ba[:, 0:n], ww[i][:, k, :], hnT[:, k, t0:t0 + n], k == 0, k == 7, [("ww", i)], [bka])
                    for k in range(8):
                        self.mm(bb_[:, 0:n], wr[i][:, k, :], hnT[:, k, t0:t0 + n], k == 0, k == 7, [("wr", i)], [bkb])
                    S.op("dve", lambda h, ba=ba, n=n, t0=t0, j=j: h.tensor_tensor(out=t1[j][:, 0:n], in0=ba[:, 0:n], in1=cosT[:, t0:t0 + n], op=ALU.mult),
                         reads=[bka, "cosT"], writes=[("t1", j)])
                    S.op("dve", lambda h, bb_=bb_, n=n, t0=t0, j=j: h.tensor_tensor(out=t2[j][:, 0:n], in0=bb_[:, 0:n], in1=sinT[:, t0:t0 + n], op=ALU.mult),
                         reads=[bkb, "sinT"], writes=[("t2", j)])
                    d_ap = dst(t0, n)
                    S.op("dve", lambda h, d_ap=d_ap, n=n, j=j: h.tensor_tensor(out=d_ap, in0=t1[j][:, 0:n], in1=t2[j][:, 0:n], op=ALU.add),
                         reads=[("t1", j), ("t2", j)], writes=["qk"])
            wvv = ww[0]
            self.load_w(wvv[:], "w_in", l * D * DIN, 0, 8, DIN, OFF_V, 128, [("ww", 0)])
            for b in range(NT):
                bank, bk = self.next_ps()
                for k in range(8):
                    self.mm(bank[:, 0:128], hnT[:, k, b * 128:(b + 1) * 128], wvv[:, k, :], k == 0, k == 7, [("ww", 0)], [bk])
                self.evac(Va[:, b, :, 0:64], bank[:, 0:128].rearrange("p (a d) -> p a d", a=2), [bk], [("Va", b)])
            bank, bk = self.next_ps()
            for k in range(8):
                self.mm(bank[0:16, 0:128], hnT[:, k, FRONT:128], wvv[:, k, :], k == 0, k == 7, [("ww", 0)], [bk])
            self.evac(Vm[:, :, 0:64], bank[0:16, 0:128].rearrange("p (a d) -> p a d", a=2), [bk], ["Vm"])
            if l == 0 and "qT" in self.tap_out:
                with ExitStack() as st2:
                    tmp = self.sbuf(st2, "tmp", [128, 6 * LP], F32)
                    S.op("dve", lambda h: h.tensor_copy(out=tmp[:, 0:4 * LP], in_=qT[:].rearrange("p k t -> p (k t)")), reads=["qk"], writes=["tmp"])
                    S.op("dve", lambda h: h.tensor_copy(out=tmp[:, 4 * LP:6 * LP], in_=kT[:].rearrange("p k t -> p (k t)")), reads=["qk"], writes=["tmp"])
                    self.tap("qT", tmp[:], ["tmp"])
                    S.barrier()
            ytm = self.sbuf(st, "ytmall", [128, NT, 512], BF16)
            for n_ in range(NT):
                for kh in range(2):
                    kbs = []
                    if n_ - 1 >= 1:
                        kbs.append((n_ - 1, 0))
                    if n_ >= 1:
                        kbs.append((n_, None))
                    if n_ + 1 <= NT - 1:
                        kbs.append((n_ + 1, 1))
                    kbs.append((-1, None))
                    slots = []
                    for idx, (kb, mi) in enumerate(kbs):
                        nk = 128 if kb >= 0 else 16
                        kc0 = kb * 128 if kb >= 0 else FRONT
                        sl = (kh * 4 + idx)
                        for half in range(2):
                            bank, bk = self.next_ps()
                            if mi is not None:
                                self.mm(bank[:, 0:256], self.ident[:], mk[:, mi * 512:mi * 512 + 256], True, False, ["ident", "mk"], [bk])
                            for ii in range(2):
                                i = 2 * ii + half
                                hq = 4 * kh + i
                                tile_ = hq // 2
                                r0 = 64 * half
                                self.mm(bank[0:nk, ii * 128:(ii + 1) * 128], kT[r0:r0 + 64, kh, kc0:kc0 + nk],
                                        qT[r0:r0 + 64, tile_, n_ * 128:(n_ + 1) * 128], mi is None, True, ["qk"], [bk])
                            S.op("act", lambda h, bank=bank, nk=nk, sl=sl, half=half: h.activation(out=PT[sl][0:nk, half * 256:(half + 1) * 256], in_=bank[0:nk, 0:256], func=AF.Exp, scale=0.125),
                                 reads=[bk], writes=[("PT", sl)])
                        slots.append((sl, kb, nk))
                    ob, obk = self.next_ps()
                    for i in range(4):
                        pc0 = (i % 2) * 256 + (i // 2) * 128
                        for idx, (sl, kb, nk) in enumerate(slots):
                            vsrc = Va[:, kb, kh, :] if kb >= 0 else Vm[:, kh, :]
                            rds = [("PT", sl)] + ([("Va", kb), "Va1"] if kb >= 0 else ["Vm", "Vm1"])
                            self.mm(ob[:, i * 65:(i + 1) * 65], PT[sl][0:nk, pc0:pc0 + 128], vsrc, idx == 0, idx == len(slots) - 1, rds, [obk])
                    ov = ob[:, 0:260].rearrange("p (i c) -> p i c", i=4)
                    dsl = den[:, kh * 8:kh * 8 + 4]
                    rsl = den[:, kh * 8 + 4:kh * 8 + 8]
                    S.op("dve", lambda h, ov=ov, dsl=dsl, kh=kh: h.tensor_tensor(out=dsl.unsqueeze(2), in0=ov[:, :, 64:65], in1=snk[:, 4 * kh:4 * kh + 4].unsqueeze(2), op=ALU.add),
                         reads=[obk, "snk"], writes=[("den", kh)])
                    S.op("dve", lambda h, dsl=dsl, rsl=rsl: h.reciprocal(out=rsl, in_=dsl), reads=[("den", kh)], writes=[("rden", kh)])
                    for i in range(4):
                        hq = 4 * kh + i
                        S.op("act", lambda h, ob=ob, i=i, hq=hq, n_=n_, rsl=rsl: h.activation(out=ytm[:, n_, hq * 64:(hq + 1) * 64], in_=ob[:, i * 65:i * 65 + 64],
                                                                                           func=AF.Copy, scale=rsl[:, i:i + 1]),
                             reads=[obk, ("rden", kh)], writes=["ytm"])
            S.barrier()
            for n_ in range(NT):
                bank, bk = self.next_pt()
                for j in range(4):
                    self.tr(bank[:, j * 128:(j + 1) * 128], ytm[:, n_, j * 128:(j + 1) * 128], self.ident[:, :], ["ytm"], [bk])
                self.evac(yTa[:, :, n_ * 128:(n_ + 1) * 128], bank[:, 0:512].rearrange("p (j t) -> p j t", j=4), [bk], ["yT_all"])
            S.barrier()
        self.tap_yT("yTa", yTa, l)

    def phase_merge(self, l, hnT, yTs):
        S = self.S
        with ExitStack() as st:
            wg = [self.sbuf(st, "wg%d" % i, [128, 8, 3, 128], BF16) for i in range(2)]
            wb = [self.sbuf(st, "wb%d" % i, [128, 12, 128], BF16) for i in range(2)]
            wo = self.sbuf(st, "wo", [128, 4, D], BF16)
            mT = self.sbuf(st, "mT", [128, 4, LP], BF16)
            sg = [self.sbuf(st, "sg%d" % i, [128, 512], F32) for i in range(2)]
            pa = [self.sbuf(st, "pa%d" % i, [128, 512], F32) for i in range(3)]
            ht = [self.sbuf(st, "ht%d" % i, [128, D], F32) for i in range(4)]
            cnt = 0
            for half in range(2):
                for dt in range(4):
                    dti = 4 * half + dt
                    i = dti % 2
                    for c in range(3):
                        self.load_w(wg[i][:, :, c, :], "w_in", l * D * DIN, 0, 8, DIN, OFF_GATE + c * D + dti * 128, 128, [("wg", i)])
                    self.load_w(wb[i][:], "w_branch", l * 3 * 512 * D, 0, 12, D, dti * 128, 128, [("wb", i)])
                    for (t0, n) in TCH:
                        for c in range(3):
                            bB, kB = self.next_ps()
                            bG, kG = self.next_ps()
                            for k in range(4):
                                self.mm(bB[:, 0:n], wb[i][:, 4 * c + k, :], yTs[c][:, k, t0:t0 + n], k == 0, k == 3, [("wb", i), "yT_all"], [kB])
                            for k in range(8):
                                self.mm(bG[:, 0:n], wg[i][:, k, c, :], hnT[:, k, t0:t0 + n], k == 0, k == 7, [("wg", i)], [kG])
                            j = cnt % 2
                            cnt += 1
                            S.op("act", lambda h, bG=bG, n=n, j=j: h.activation(out=sg[j][:, 0:n], in_=bG[:, 0:n], func=AF.Sigmoid), reads=[kG], writes=[("sg", j)])
                            S.op("dve", lambda h, bB=bB, n=n, j=j, c=c: h.tensor_tensor(out=pa[c][:, 0:n], in0=bB[:, 0:n], in1=sg[j][:, 0:n], op=ALU.mult),
                                 reads=[kB, ("sg", j)], writes=[("pa", c)])
                        S.op("dve", lambda h, n=n: h.tensor_tensor(out=pa[0][:, 0:n], in0=pa[0][:, 0:n], in1=pa[1][:, 0:n], op=ALU.add),
                             reads=[("pa", 0), ("pa", 1)], writes=[("pa", 0)])
                        S.op("dve", lambda h, n=n, t0=t0, dt=dt: h.tensor_tensor(out=mT[:, dt, t0:t0 + n], in0=pa[0][:, 0:n], in1=pa[2][:, 0:n], op=ALU.add),
                             reads=[("pa", 0), ("pa", 2)], writes=["mT"])
                self.load_w(wo[:], "w_out", l * D * D, half * 512, 4, D, 0, D, ["wo"])
                NB = 4

                def load_h(b):
                    rd = [("h", b)] + ([("hm", 0)] if b == 0 else [])
                    S.dma("sp", ht[b % NB][:], self.h_d[b * 128:(b + 1) * 128, :], reads=rd, writes=[("ht", b % NB)])

                for b in range(NB - 1):
                    load_h(b)
                for b in range(NT):
                    i = b % NB
                    if b + NB - 1 < NT:
                        load_h(b + NB - 1)
                    for ch in range(2):
                        bank, bk = self.next_ps()
                        for dt in range(4):
                            self.mm(bank[:, :], mT[:, dt, b * 128:(b + 1) * 128], wo[:, dt, ch * 512:(ch + 1) * 512], dt == 0, dt == 3, ["mT", "wo"], [bk])
                        S.op("dve", lambda h, bank=bank, i=i, ch=ch: h.tensor_tensor(out=ht[i][:, ch * 512:(ch + 1) * 512], in0=bank[:, :], in1=ht[i][:, ch * 512:(ch + 1) * 512], op=ALU.add),
                             reads=[bk, ("ht", i)], writes=[("ht", i)])
                    S.dma("sp", self.h_d[b * 128:(b + 1) * 128, :], ht[i][:], reads=[("ht", i)], writes=[("h", b), ("hm", 0)] if b == 0 else [("h", b)])
            S.barrier()
        if l == 0 and "h_mix" in self.tap_out:
            S.dma("sp", self.tap_out["h_mix"], self.h_d, reads=[("h", b) for b in range(NT)])
            S.barrier()

    def phase_ffn(self, l):
        S = self.S
        last = (l == NL - 1)
        with ExitStack() as st:
            hs = self.sbuf(st, "hs", [128, NT, D], F32)
            hn2T = self.sbuf(st, "hn2T", [128, 8, LP], BF16)
            for b in range(NT):
                rd = [("h", b)] + ([("hm", 0)] if b == 0 else [])
                S.dma("sp", hs[:, b, :], self.h_d[b * 128:(b + 1) * 128, :], reads=rd, writes=[("hs", b, 0), ("hs", b, 1)])
            with ExitStack() as st2:
                self.norm_transpose(st2, "norm_mlp", l * D, hn2T, lambda b: (hs[:, b, :], [("hs", b, 0), ("hs", b, 1)]))
                wu = [self.sbuf(st2, "wu%d" % i, [128, 8, 512], BF16) for i in range(2)]
                wd = [self.sbuf(st2, "wd%d" % i, [128, 4, D], BF16) for i in range(2)]
                aT = self.sbuf(st2, "aT", [128, 4, LP], BF16)
                rl = [self.sbuf(st2, "rl%d" % i, [128, 512], F32) for i in range(2)]
                cnt = 0
                for fc in range(8):
                    i = fc % 2
                    self.load_w(wu[i][:], "w_up", l * D * DFF, 0, 8, DFF, fc * 512, 512, [("wu", i)])
                    self.load_w(wd[i][:], "w_down", l * DFF * D, fc * 512, 4, D, 0, D, [("wd", i)])
                    for ft in range(4):
                        for (t0, n) in TCH:
                            bank, bk = self.next_ps()
                            for k in range(8):
                                self.mm(bank[:, 0:n], wu[i][:, k, ft * 128:(ft + 1) * 128], hn2T[:, k, t0:t0 + n], k == 0, k == 7,
                                        [("wu", i)] + [("hnT", bb2) for bb2 in range(t0 // 128, (t0 + n) // 128)], [bk])
                            j = cnt % 2
                            cnt += 1
                            S.op("act", lambda h, bank=bank, n=n, j=j: h.activation(out=rl[j][:, 0:n], in_=bank[:, 0:n], func=AF.Relu), reads=[bk], writes=[("rl", j)])
                            S.op("dve", lambda h, n=n, j=j, ft=ft, t0=t0: h.tensor_tensor(out=aT[:, ft, t0:t0 + n], in0=rl[j][:, 0:n], in1=rl[j][:, 0:n], op=ALU.mult),
                                 reads=[("rl", j)], writes=["aT"])
                    for b in range(NT):
                        for ch in range(2):
                            bank, bk = self.next_ps()
                            for ft in range(4):
                                self.mm(bank[:, :], aT[:, ft, b * 128:(b + 1) * 128], wd[i][:, ft, ch * 512:(ch + 1) * 512], ft == 0, ft == 3, ["aT", ("wd", i)], [bk])
                            S.op("dve", lambda h, bank=bank, b=b, ch=ch: h.tensor_tensor(out=hs[:, b, ch * 512:(ch + 1) * 512], in0=bank[:, :], in1=hs[:, b, ch * 512:(ch + 1) * 512], op=ALU.add),
                                 reads=[bk, ("hs", b, ch)], writes=[("hs", b, ch)])
                S.barrier()
            if l == 0 and "h_new" in self.tap_out:
                for b in range(NT):
                    S.dma("sp", self.tap_out["h_new"][b * 128:(b + 1) * 128, :], hs[:, b, :], reads=[("hs", b, 0), ("hs", b, 1)])
            if not last or self.nlayers < NL:
                for b in range(NT):
                    S.dma("sp", self.h_d[b * 128:(b + 1) * 128, :], hs[:, b, :], reads=[("hs", b, 0), ("hs", b, 1)], writes=[("h", b), ("hm", 0)] if b == 0 else [("h", b)])
            if l == self.nlayers - 1:
                with ExitStack() as st2:
                    small = self.small
                    gain = self.sbuf(st2, "gainf", [128, D], F32)
                    S.dma("sp", gain[:], AP(self.T["norm_final"], 0, [[0, 128], [1, D]]), writes=["gainf"])
                    junk = self.sbuf(st2, "junkf", [128, D], BF16)
                    ss = self.sbuf(st2, "ssf", [128, 3 * NT], F32)
                    ob = [self.sbuf(st2, "ob%d" % i, [128, D], F32) for i in range(2)]
                    for b in range(1, NT):
                        i = b % 2
                        S.op("act", lambda h, b=b: h.activation(out=junk[:], in_=hs[:, b, :], func=AF.Square, accum_out=ss[:, b:b + 1]),
                             reads=[("hs", b, 0), ("hs", b, 1)], writes=["junkf", ("ssf", b)])
                        S.op("act", lambda h, b=b: h.activation(out=ss[:, NT + b:NT + b + 1], in_=ss[:, b:b + 1], func=AF.Sqrt, scale=1.0 / D, bias=small[:, 0:1]),
                             reads=[("ssf", b), "eps"], writes=[("ssf1", b)])
                        S.op("dve", lambda h, b=b: h.reciprocal(out=ss[:, 2 * NT + b:2 * NT + b + 1], in_=ss[:, NT + b:NT + b + 1]), reads=[("ssf1", b)], writes=[("ssf2", b)])
                        S.op("dve", lambda h, b=b, i=i: h.scalar_tensor_tensor(out=ob[i][:], in0=hs[:, b, :], scalar=ss[:, 2 * NT + b:2 * NT + b + 1], in1=gain[:], op0=ALU.mult, op1=ALU.mult),
                             reads=[("hs", b, 0), ("hs", b, 1), ("ssf2", b), "gainf"], writes=[("ob", i)])
                        S.dma("sp", self.out[(b - 1) * 128:b * 128, :], ob[i][:], reads=[("ob", i)])
            S.barrier()

    def phase_ssm_core(self, l, hnT):
        S = self.S
        T = self.T
        TWO_PI = 2.0 * math.pi
        with ExitStack() as st:
            UT = self.sbuf(st, "UT", [128, 32, NCH], BF16)
            Kb = self.sbuf(st, "Kb", [128, 32, 128], BF16)
            Cp = self.sbuf(st, "Cp", [128, 2, 16, 2, 128], BF16)
            BpT = self.sbuf(st, "BpT", [128, 2, 16, 2, 128], BF16)
            a8 = self.sbuf(st, "a8", [128, 2, 16, 2, 2], F32)
            with ExitStack() as sa:
                ht = [self.sbuf(sa, "ht%d" % i, [128, D], F32) for i in range(3)]

                def src_tiles(b, l=l):
                    i = b % 3
                    if l == 0:
                        if b == 0:
                            S.op("pool", lambda h, i=i: h.memset(ht[i][0:FRONT, :], 0.0), writes=[("ht", i)])
                            S.dma("sp", ht[i][FRONT:128, :], self.A["meta_tokens"], writes=[("htm", i)])
                            return ht[i][:], [("ht", i), ("htm", i)]
                        S.dma("sp", ht[i][:], self.A["x"][(b - 1) * 128:b * 128, :], writes=[("ht", i), ("htm", i)])
                        return ht[i][:], [("ht", i), ("htm", i)]
                    rd = [("h", b)] + ([("hm", 0)] if b == 0 else [])
                    S.dma("sp", ht[i][:], self.h_d[b * 128:(b + 1) * 128, :], reads=rd, writes=[("ht", i), ("htm", i)])
                    return ht[i][:], [("ht", i), ("htm", i)]

                self.norm_transpose(sa, "norm_mix", l * D, hnT, src_tiles)
                wss = self.sbuf(sa, "wss", [128, 8, 512], BF16)
                utm = [self.sbuf(sa, "utm%d" % i, [128, 512], BF16) for i in range(2)]
                ucm = [self.sbuf(sa, "ucm%d" % i, [128, 4096], BF16) for i in range(3)]
                self.load_w(wss[:], "w_in", l * D * DIN, 0, 8, DIN, OFF_SSM, 512, ["wss"])
                for b in range(NT):
                    i = b % 2
                    bank, bk = self.next_ps()
                    for k in range(8):
                        self.mm(bank[:, :], hnT[:, k, b * 128:(b + 1) * 128], wss[:, k, :], k == 0, k == 7, ["wss", ("hnT", b)], [bk])
                    self.evac(utm[i][:], bank[:, :], [bk], [("utm", i)])
                    S.dma("sp", self.u_d[b * 128:(b + 1) * 128, :], utm[i][:], reads=[("utm", i)], writes=[("ud", b)])
                for ct in range(3):
                    n = 128 if ct < 2 else 16
                    S.dma("sp", ucm[ct][0:n, :], self.u_d.rearrange("(c s) x -> c (s x)", s=8)[ct * 128:ct * 128 + n, :],
                          reads=[("ud", b) for b in range(NT)], writes=[("ucm", ct)])
                ucg = [self.sbuf(sa, "ucg%d" % i, [128, 4096], BF16) for i in range(3)]
                for ct in range(3):
                    n = 128 if ct < 2 else 16
                    engs = ("dve", "pool", "act")
                    src = ucm[ct][0:n, :].rearrange("c (s g q) -> c g s q", s=8, g=32, q=16)
                    dstv = ucg[ct][0:n, :].rearrange("c (g s q) -> c g s q", s=8, g=32, q=16)
                    for g4 in range(4):
                        self.evac(dstv[:, 8 * g4:8 * g4 + 8], src[:, 8 * g4:8 * g4 + 8], [("ucm", ct)], [("ucg", ct)], eng=engs[(ct + g4) % 3])
                for ct in range(2):
                    uv = ucg[ct][:, :].rearrange("c (g x) -> c g x", g=32)
                    for gg in range(4):
                        bank, bk = self.next_pt()
                        for gi in range(8):
                            self.tr(bank[:, gi * 128:(gi + 1) * 128], uv[:, 8 * gg + gi], self.ident[:, :], [("ucg", ct)], [bk])
                        self.evac(UT[:, 8 * gg:8 * gg + 8, ct * 128:(ct + 1) * 128], bank[:, :].rearrange("p (g c) -> p g c", g=8), [bk], ["UT"])
                uv = ucg[2][0:16, :].rearrange("c (g x) -> c g x", g=32)
                bank, bk = self.next_pt()
                for g in range(32):
                    self.tr(bank[:, g * 16:(g + 1) * 16], uv[:, g], self.ident[0:16, 0:16], [("ucg", 2)], [bk])
                self.evac(UT[:, :, 256:272], bank[:, 0:512].rearrange("p (g c) -> p g c", g=32), [bk], ["UT"])
                S.barrier()
            if self.stop_after == "ssmA":
                return
            with ExitStack() as sb_:
                Bp = self.sbuf(sb_, "Bp", [128, 2, 16, 2, 128], BF16)
                dcol = self.sbuf(sb_, "dcol", [128, 32], F32)
                mfb = self.sbuf(sb_, "mfb", [128, 256], F32)
                tau = self.sbuf(sb_, "tau", [128, 2, 16, 9], F32)
                S.dma("sp", mfb[:], self.A["c_maskfb"], writes=["mfb"])
                S.dma("sp", tau[:].rearrange("p d m t -> p (d m t)"), self.A["c_tau"], writes=["tau"])
                for s in range(8):
                    S.dma("sp", dcol[16 * s:16 * s + 16, :], AP(T["ssm_d"], l * 512, [[1, 16], [16, 32]]), writes=["dcol"], allow_slow_non_contiguous=True)
                prm = self.sbuf(sb_, "prm", [128, 12, 2, 16], F32)
                LR, LI, DT, X, TH, DEN, NRE, FR, FI, ABR, ABI, TMP = range(12)
                for idx_, nm_ in ((LR, "ssm_lam_re"), (LI, "ssm_lam_im")):
                    for d in range(2):
                        for m4 in range(4):
                            S.dma("sp", prm[:, idx_, d, 4 * m4:4 * m4 + 4], AP(T[nm_], l * 4096 + d * 2048 + m4 * 512, [[1, 128], [128, 4]]),
                                  writes=["prm"], allow_slow_non_contiguous=True)
                ldt = self.sbuf(sb_, "ldt", [128, 2, 16, 2], F32)
                S.dma("sp", ldt[:].rearrange("p d m j -> p (d m j)"), AP(T["ssm_log_dt"], l * 64, [[0, 128], [1, 64]]), writes=["ldt"])
                for j in range(2):
                    S.op("dve", lambda h, j=j: h.tensor_copy(out=prm[64 * j:64 * j + 64, DT], in_=ldt[64 * j:64 * j + 64, :, :, j]), reads=["ldt"], writes=["prm"])
                braw = self.sbuf(sb_, "braw", [128, 2, 2, 16, 16], F32)
                craw = self.sbuf(sb_, "craw", [128, 2, 2, 16, 16], F32)
                for ri, nm in enumerate(("ssm_b_re", "ssm_b_im")):
                    for d in range(2):
                        for m4 in range(4):
                            S.dma("sp", braw[:, ri, d, 4 * m4:4 * m4 + 4, :], AP(T[nm], l * 65536 + d * 32768 + m4 * 8192, [[16, 128], [2048, 4], [1, 16]]), writes=["braw"])
                ctmp = [self.sbuf(sb_, "ctmp%d" % i, [128, 2, 64], F32) for i in range(2)]
                cc = 0
                for ri, nm in enumerate(("ssm_c_re", "ssm_c_im")):
                    for d in range(2):
                        for i4 in range(4):
                            ci = cc % 2
                            cc += 1
                            S.dma("sp", ctmp[ci][:], AP(T[nm], l * 65536 + d * 32768 + i4 * 8192, [[64, 128], [0, 2], [1, 64]]), writes=[("ctmp", ci)])
                            bank, bk = self.next_ps()
                            S.op("pe", lambda h, bank=bank, ci=ci: h.transpose(out=bank[:, 0:128], in_=ctmp[ci][:].rearrange("p a n -> p (a n)"), identity=self.identf[:]),
                                 reads=[("ctmp", ci), "identf"], writes=[bk])
                            for j in range(2):
                                src = bank[64 * j:64 * j + 64, 0:128].rearrange("p (ml jj q) -> p ml jj q", ml=4, jj=2, q=16)[:, :, j, :]
                                self.evac(craw[64 * j:64 * j + 64, ri, d, 4 * i4:4 * i4 + 4, :], src, [bk], ["craw"], eng="dve")
                P = prm
                v = lambda idx: P[:, idx].rearrange("p d m -> p (d m)")
                S.op("act", lambda h: h.activation(out=v(DT), in_=v(DT), func=AF.Exp), reads=["prm"], writes=["prm"])
                S.op("dve", lambda h: h.tensor_tensor(out=v(X), in0=v(LR), in1=v(DT), op=ALU.mult), reads=["prm"], writes=["prm"])
                S.op("dve", lambda h: h.tensor_tensor(out=v(TH), in0=v(LI), in1=v(DT), op=ALU.mult), reads=["prm"], writes=["prm"])
                pw = self.sbuf(sb_, "pw", [128, 10, 288], F32)
                XE, THE, MC, MB, SN, CS, KI, PCR, PCI, PBR = range(10)
                pbi = self.sbuf(sb_, "pbi", [128, 288], F32)
                ki = self.sbuf(sb_, "ki", [128, 288], I32)
                w3 = lambda a: a.rearrange("p (dm t) -> p dm t", t=9)
                tau3 = tau[:].rearrange("p d m t -> p (d m) t")
                S.op("dve", lambda h: h.tensor_tensor(out=w3(pw[:, XE]), in0=bc(v(X), 2, 9), in1=tau3, op=ALU.mult), reads=["prm", "tau"], writes=["pw"])
                S.op("dve", lambda h: h.tensor_tensor(out=w3(pw[:, THE]), in0=bc(v(TH), 2, 9), in1=tau3, op=ALU.mult), reads=["prm", "tau"], writes=["pw"])
                S.op("act", lambda h: h.activation(out=pw[:, MC], in_=pw[:, XE], func=AF.Exp), reads=["pw"], writes=["pw"])
                S.op("act", lambda h: h.activation(out=pw[:, MB], in_=pw[:, XE], func=AF.Exp, scale=-1.0), reads=["pw"], writes=["pw"])

                def sin_of(dst, shift):
                    S.op("dve", lambda h: h.tensor_scalar(out=ki[:], in0=pw[:, THE], scalar1=shift, scalar2=1.0 / TWO_PI, op0=ALU.add, op1=ALU.mult), reads=["pw"], writes=["ki"])
                    S.op("dve", lambda h: h.tensor_copy(out=pw[:, KI], in_=ki[:]), reads=["ki"], writes=["pw"])
                    S.op("dve", lambda h: h.scalar_tensor_tensor(out=pw[:, KI], in0=pw[:, KI], scalar=-TWO_PI, in1=pw[:, THE], op0=ALU.mult, op1=ALU.add), reads=["pw"], writes=["pw"])
                    S.op("dve", lambda h: h.tensor_scalar(out=pw[:, KI], in0=pw[:, KI], scalar1=shift, scalar2=math.pi, op0=ALU.add, op1=ALU.min), reads=["pw"], writes=["pw"])
                    S.op("dve", lambda h: h.tensor_scalar(out=pw[:, KI], in0=pw[:, KI], scalar1=-math.pi, scalar2=None, op0=ALU.max), reads=["pw"], writes=["pw"])
                    S.op("act", lambda h: h.activation(out=dst, in_=pw[:, KI], func=AF.Sin), reads=["pw"], writes=["pw"])

                sin_of(pw[:, SN], 0.0)
                sin_of(pw[:, CS], math.pi / 2)
                S.op("dve", lambda h: h.tensor_tensor(out=pw[:, PCR], in0=pw[:, MC], in1=pw[:, CS], op=ALU.mult), reads=["pw"], writes=["pw"])
                S.op("dve", lambda h: h.tensor_tensor(out=pw[:, PCI], in0=pw[:, MC], in1=pw[:, SN], op=ALU.mult), reads=["pw"], writes=["pw"])
                S.op("dve", lambda h: h.tensor_tensor(out=pw[:, PBR], in0=pw[:, MB], in1=pw[:, CS], op=ALU.mult), reads=["pw"], writes=["pw"])
                S.op("dve", lambda h: h.scalar_tensor_tensor(out=pbi[:], in0=pw[:, MB], scalar=-1.0, in1=pw[:, SN], op0=ALU.mult, op1=ALU.mult), reads=["pw"], writes=["pbi"])
                pcr4 = pw[:, PCR].rearrange("p (d m t) -> p d m t", d=2, m=16)
                pci4 = pw[:, PCI].rearrange("p (d m t) -> p d m t", d=2, m=16)
                for d, ti in ((0, 1), (1, 6)):
                    S.op("dve", lambda h, d=d, ti=ti: h.tensor_copy(out=P[:, ABR, d, :], in_=pcr4[:, d, :, ti]), reads=["pw"], writes=["prm"])
                    S.op("dve", lambda h, d=d, ti=ti: h.tensor_copy(out=P[:, ABI, d, :], in_=pci4[:, d, :, ti]), reads=["pw"], writes=["prm"])
                for d in range(2):
                    S.op("dve", lambda h, d=d: h.tensor_copy(out=a8[:, d, :, 0, :], in_=bc(pcr4[:, d, :, 8], 2, 2)), reads=["pw"], writes=["a8"])
                    S.op("dve", lambda h, d=d: h.tensor_copy(out=a8[:, d, :, 1, 0], in_=pci4[:, d, :, 8]), reads=["pw"], writes=["a8"])
                    S.op("dve", lambda h, d=d: h.tensor_scalar(out=a8[:, d, :, 1, 1], in0=pci4[:, d, :, 8], scalar1=-1.0, scalar2=None, op0=ALU.mult), reads=["pw"], writes=["a8"])
                tt = lambda o, a, b, op: S.op("dve", lambda h: h.tensor_tensor(out=v(o), in0=v(a), in1=v(b), op=op), reads=["prm"], writes=["prm"])
                tt(DEN, LR, LR, ALU.mult)
                tt(TMP, LI, LI, ALU.mult)
                tt(DEN, DEN, TMP, ALU.add)
                S.op("dve", lambda h: h.reciprocal(out=v(DEN), in_=v(DEN)), reads=["prm"], writes=["prm"])
                S.op("dve", lambda h: h.tensor_scalar(out=v(NRE), in0=v(ABR), scalar1=-1.0, scalar2=None, op0=ALU.add), reads=["prm"], writes=["prm"])
                tt(FR, NRE, LR, ALU.mult)
                tt(TMP, ABI, LI, ALU.mult)
                tt(FR, FR, TMP, ALU.add)
                tt(FR, FR, DEN, ALU.mult)
                tt(FI, ABI, LR, ALU.mult)
                tt(TMP, NRE, LI, ALU.mult)
                tt(FI, FI, TMP, ALU.subtract)
                tt(FI, FI, DEN, ALU.mult)
                bb = self.sbuf(sb_, "bb", [128, 2, 32, 16], F32)
                m1 = self.sbuf(sb_, "m1", [128, 4096], F32)
                m2 = self.sbuf(sb_, "m2", [128, 4096], F32)
                b3 = lambda ri: braw[:, ri].rearrange("p d m q -> p (d m) q")
                c3 = lambda ri: craw[:, ri].rearrange("p d m q -> p (d m) q")
                m13 = m1[:, 0:512].rearrange("p (a q) -> p a q", q=16)
                m23 = m2[:, 0:512].rearrange("p (a q) -> p a q", q=16)
                fr3, fi3 = bc(v(FR), 2, 16), bc(v(FI), 2, 16)
                S.op("dve", lambda h: h.tensor_tensor(out=m13, in0=fr3, in1=b3(0), op=ALU.mult), reads=["prm", "braw"], writes=["m1"])
                S.op("pool", lambda h: h.tensor_tensor(out=m23, in0=fi3, in1=b3(1), op=ALU.mult), reads=["prm", "braw"], writes=["m2"])
                S.op("dve", lambda h: h.tensor_tensor(out=bb[:, 0], in0=m13, in1=m23, op=ALU.subtract), reads=["m1", "m2"], writes=["bb"])
                S.op("dve", lambda h: h.tensor_tensor(out=m13, in0=fr3, in1=b3(1), op=ALU.mult), reads=["prm", "braw", "bb"], writes=["m1"])
                S.op("pool", lambda h: h.tensor_tensor(out=m23, in0=fi3, in1=b3(0), op=ALU.mult), reads=["prm", "braw", "bb"], writes=["m2"])
                S.op("dve", lambda h: h.tensor_tensor(out=bb[:, 1], in0=m13, in1=m23, op=ALU.add), reads=["m1", "m2"], writes=["bb"])
                pw3 = lambda idx: w3(pw[:, idx])[:, :, 0:8]
                pbi3 = w3(pbi[:])[:, :, 0:8]
                m14 = m1[:].rearrange("p (a s q) -> p a s q", s=8, q=16)
                m24 = m2[:].rearrange("p (a s q) -> p a s q", s=8, q=16)

                def cplx(dst, outkey, ar, ai, br, bi, neg_im):
                    A_r, A_i = bc(ar, 3, 16), bc(ai, 3, 16)
                    B_r, B_i = bc(br, 2, 8), bc(bi, 2, 8)
                    o_re = dst[:, :, :, 0, :].rearrange("p d m (s q) -> p (d m) s q", s=8)
                    o_im = dst[:, :, :, 1, :].rearrange("p d m (s q) -> p (d m) s q", s=8)
                    S.op("dve", lambda h: h.tensor_tensor(out=m14, in0=A_r, in1=B_r, op=ALU.mult), reads=["pw", "pbi", "bb", "craw", outkey], writes=["m1"])
                    S.op("pool", lambda h: h.tensor_tensor(out=m24, in0=A_i, in1=B_i, op=ALU.mult), reads=["pw", "pbi", "bb", "craw", outkey], writes=["m2"])
                    S.op("dve", lambda h: h.tensor_tensor(out=o_re, in0=m14, in1=m24, op=ALU.subtract), reads=["m1", "m2"], writes=[outkey])
                    S.op("dve", lambda h: h.tensor_tensor(out=m14, in0=A_r, in1=B_i, op=ALU.mult), reads=["pw", "pbi", "bb", "craw", outkey], writes=["m1"])
                    S.op("pool", lambda h: h.tensor_tensor(out=m24, in0=A_i, in1=B_r, op=ALU.mult), reads=["pw", "pbi", "bb", "craw", outkey], writes=["m2"])
                    if neg_im:
                        S.op("dve", lambda h: h.scalar_tensor_tensor(out=o_im, in0=m14, scalar=-1.0, in1=m24, op0=ALU.mult, op1=ALU.subtract), reads=["m1", "m2"], writes=[outkey])
                    else:
                        S.op("dve", lambda h: h.tensor_tensor(out=o_im, in0=m14, in1=m24, op=ALU.add), reads=["m1", "m2"], writes=[outkey])

                cplx(Bp, "Bp", pw3(PBR), pbi3, bb[:, 0], bb[:, 1], False)
                cplx(Cp, "Cp", pw3(PCR), pw3(PCI), c3(0), c3(1), True)
                if l == 0 and "BpCp" in self.tap_out:
                    S.op("dve", lambda h: h.tensor_copy(out=m1[:], in_=Bp[:].rearrange("p d m c x -> p (d m c x)")[:, 0:4096]), reads=["Bp", "m1"], writes=["m1"])
                    S.op("dve", lambda h: h.tensor_copy(out=m2[:], in_=Cp[:].rearrange("p d m c x -> p (d m c x)")[:, 0:4096]), reads=["Cp", "m2"], writes=["m2"])
                    S.dma("sp", self.tap_out["BpCp"][:, 0:4096], m1[:], reads=["m1"])
                    S.dma("sp", self.tap_out["BpCp"][:, 4096:8192], m2[:], reads=["m2"])
                    S.barrier()
                tk = [self.sbuf(sb_, "tk%d" % i, [128, 256], F32) for i in range(2)]
                t2k = [self.sbuf(sb_, "t2k%d" % i, [128, 128], F32) for i in range(2)]
                for g in range(32):
                    m, j = g // 2, g % 2
                    r0 = 64 * j
                    i = g % 2
                    bank, bk = self.next_ps()
                    for d in range(2):
                        for c2 in range(2):
                            self.mm(bank[:, d * 128:(d + 1) * 128], Bp[r0:r0 + 64, d, m, c2, :], Cp[r0:r0 + 64, d, m, c2, :], c2 == 0, c2 == 1, ["Bp", "Cp"], [bk])
                    S.op("dve", lambda h, bank=bank, i=i: h.tensor_tensor(out=tk[i][:], in0=bank[:, 0:256], in1=mfb[:], op=ALU.mult), reads=[bk, "mfb"], writes=[("tk", i)])
                    S.op("pool", lambda h, i=i: h.tensor_tensor(out=t2k[i][:], in0=tk[i][:, 0:128], in1=tk[i][:, 128:256], op=ALU.add), reads=[("tk", i)], writes=[("t2k", i)])
                    S.op("dve", lambda h, i=i, g=g: h.scalar_tensor_tensor(out=Kb[:, g, :], in0=self.identf[:], scalar=dcol[:, g:g + 1], in1=t2k[i][:], op0=ALU.mult, op1=ALU.add),
                         reads=[("t2k", i), "dcol", "identf"], writes=["Kb"])
                for d in range(2):
                    for mm4 in range(4):
                        bank, bk = self.next_pt()
                        for mi in range(4):
                            m = 4 * mm4 + mi
                            for c2 in range(2):
                                o0 = (mi * 2 + c2) * 128
                                self.tr(bank[:, o0:o0 + 128], Bp[:, d, m, c2, :], self.ident[:, :], ["Bp"], [bk])
                        self.evac(BpT[:, d, 4 * mm4:4 * mm4 + 4].rearrange("p m c x -> p (m c x)"), bank[:, :], [bk], ["BpT"])
                S.barrier()
            if self.stop_after == "ssmB":
                return
            with ExitStack() as sc:
                W_ = NCH + 2
                Rbf = self.sbuf(sc, "Rbf", [128, 2, 16, 2, W_], BF16)
                with ExitStack() as sc2:
                    RZ = self.sbuf(sc2, "RZ", [128, 2, 16, 2, W_], F32)
                    Wst = self.sbuf(sc2, "Wst", [128, 32, 2], F32)
                    tT = self.sbuf(sc2, "tT", [128, 32, 2], F32)
                    pP = self.sbuf(sc2, "pP", [128, 32, 2, 2], F32)
                    plane = 16 * 2 * W_

                    def rz_cols(colf, colb):
                        a = RZ[:, :, :, :, colf]
                        return bass.AP(tensor=a.tensor, offset=a.offset, ap=[list(a.ap[0]), [plane + colb - colf, 2], list(a.ap[2]), list(a.ap[3])])

                    S.op("pool", lambda h: h.memset(RZ[:, 0, :, :, 0:1], 0.0), writes=["RZ"])
                    S.op("pool", lambda h: h.memset(RZ[:, 1, :, :, W_ - 1:W_], 0.0), writes=["RZ"])
                    S.op("pool", lambda h: h.memset(Wst[:], 0.0), writes=["W"])
                    for d in range(2):
                        for m in range(16):
                            bre, kre = self.next_ps()
                            bim, kim = self.next_ps()
                            for j in range(2):
                                g = 2 * m + j
                                self.mm(bre[64 * j:64 * j + 64, 0:NCH], BpT[:, d, m, 0, 64 * j:64 * j + 64], UT[:, g, :], True, True, ["BpT", "UT"], [kre])
                                self.mm(bim[64 * j:64 * j + 64, 0:NCH], BpT[:, d, m, 1, 64 * j:64 * j + 64], UT[:, g, :], True, True, ["BpT", "UT"], [kim])
                            self.evac(RZ[:, d, m, 0, 1:1 + NCH], bre[:, 0:NCH], [kre], ["RZ"])
                            self.evac(RZ[:, d, m, 1, 1:1 + NCH], bim[:, 0:NCH], [kim], ["RZ"])
                    coef = a8[:].rearrange("p d m u c -> p (d m) u c")
                    p0 = pP[:, :, 0, :]
                    p1 = pP[:, :, 1, :]
                    p1r = bass.AP(tensor=p1.tensor, offset=p1.offset + 1, ap=[list(p1.ap[0]), list(p1.ap[1]), [-1, 2]])
                    wst3 = Wst[:].rearrange("p (d m) c -> p d m c", d=2)
                    tt3 = tT[:].rearrange("p (d m) c -> p d m c", d=2)
                    prev_out = None
                    for k in range(NCH - 1):
                        zc = rz_cols(k + 1, NCH - k)
                        S.op("dve", lambda h, zc=zc: h.tensor_tensor(out=tt3, in0=wst3, in1=zc, op=ALU.add), reads=["W", "RZ"], writes=["T"])
                        if prev_out is not None:
                            S.op("dve", lambda h, po=prev_out: h.tensor_copy(out=po, in_=wst3), reads=["W"], writes=["RZ"])
                        S.op("dve", lambda h: h.tensor_tensor(out=pP[:], in0=bc(tT[:], 2, 2), in1=coef, op=ALU.mult), reads=["T", "a8"], writes=["P"])
                        S.op("dve", lambda h: h.tensor_tensor(out=Wst[:], in0=p0, in1=p1r, op=ALU.add), reads=["P"], writes=["W"])
                        prev_out = zc
                    S.op("dve", lambda h, po=prev_out: h.tensor_copy(out=po, in_=wst3), reads=["W"], writes=["RZ"])
                    cengs = ("act", "dve", "pool", "act")
                    for d in range(2):
                        for mh in range(2):
                            self.evac(Rbf[:, d, 8 * mh:8 * mh + 8].rearrange("p m c x -> p (m c x)"), RZ[:, d, 8 * mh:8 * mh + 8].rearrange("p m c x -> p (m c x)"),
                                      ["RZ"], ["Rbf"], eng=cengs[2 * d + mh])
                    S.barrier()
                if self.stop_after == "ssmC" and "Rbf" not in self.tap_out:
                    return
                if l == 0 and "Rbf" in self.tap_out:
                    with ExitStack() as sc3:
                        tmp = self.sbuf(sc3, "tmp", [128, 2 * 16 * 2 * (NCH + 2)], F32)
                        S.op("dve", lambda h: h.tensor_copy(out=tmp[:], in_=Rbf[:].rearrange("p d m c x -> p (d m c x)")), reads=["Rbf"], writes=["tmp"])
                        self.tap("Rbf", tmp[:], ["tmp"])
                        S.barrier()
                    if self.stop_after == "ssmC":
                        return
                with ExitStack() as sd:
                    zcm = [self.sbuf(sd, "zcm%d" % i, [128, 4096], BF16) for i in range(3)]
                    for ct in range(3):
                        n = 128 if ct < 2 else 16
                        c0 = ct * 128
                        for g4 in range(8):
                            bank, bk = self.next_ps()
                            for gi in range(4):
                                g = 4 * g4 + gi
                                m, j = g // 2, g % 2
                                r0 = 64 * j
                                o_ap = bank[0:n, gi * 128:(gi + 1) * 128]
                                self.mm(o_ap, UT[:, g, c0:c0 + n], Kb[:, g, :], True, False, ["Kb", "UT"], [bk])
                                for d in range(2):
                                    off = 0 if d == 0 else 2
                                    for c2 in range(2):
                                        self.mm(o_ap, Rbf[r0:r0 + 64, d, m, c2, off + c0:off + c0 + n], Cp[r0:r0 + 64, d, m, c2, :], False, (d == 1 and c2 == 1), ["Cp", "Rbf"], [bk])
                            dst = zcm[ct][0:n, :].rearrange("c (t g p) -> c g t p", t=8, g=32, p=16)[:, 4 * g4:4 * g4 + 4]
                            src = bank[0:n, :].rearrange("c (g t p) -> c g t p", g=4, t=8, p=16)
                            S.op("act", lambda h, dst=dst, src=src: h.activation(out=dst, in_=src, func=AF.Gelu), reads=[bk], writes=[("zcm", ct)])
                    for ct in range(3):
                        if self.stop_after in ("ssmD1", "ssmD2"):
                            continue
                        n = 128 if ct < 2 else 16
                        S.dma("sp", self.z_d.rearrange("(c s) x -> c (s x)", s=8)[ct * 128:ct * 128 + n, :], zcm[ct][0:n, :], reads=[("zcm", ct)], writes=["zd"])
                    S.barrier()

    def phase_ssm_glu(self, l, yTs):
        S = self.S
        with ExitStack() as st:
            zT = self.sbuf(st, "zT", [128, 4, LP], BF16)
            ztm = [self.sbuf(st, "ztm%d" % i, [128, 512], BF16) for i in range(2)]
            wgl = self.sbuf(st, "wgl", [128, 4, 512], BF16)
            glb = self.sbuf(st, "glb", [128, 4], F32)
            sgs = [self.sbuf(st, "sgs%d" % i, [128, 512], BF16) for i in range(2)]
            self.load_w(wgl[:], "ssm_glu_w", l * 512 * 512, 0, 4, 512, 0, 512, ["wgl"])
            S.dma("sp", glb[:], AP(self.T["ssm_glu_b"], l * 512, [[1, 128], [128, 4]]), writes=["glb"], allow_slow_non_contiguous=True)
            for b in range(NT):
                i = b % 2
                S.dma("sp", ztm[i][:], self.z_d[b * 128:(b + 1) * 128, :], reads=["zd"], writes=[("ztm", i)])
                bank, bk = self.next_pt()
                for j in range(4):
                    self.tr(bank[:, j * 128:(j + 1) * 128], ztm[i][:, j * 128:(j + 1) * 128], self.ident[:, :], [("ztm", i)], [bk])
                self.evac(zT[:, :, b * 128:(b + 1) * 128], bank[:, 0:512].rearrange("p (j t) -> p j t", j=4), [bk], ["zT"])
            cnt = 0
            for co in range(4):
                for (t0, n) in TCH:
                    bank, bk = self.next_ps()
                    for k in range(4):
                        self.mm(bank[:, 0:n], wgl[:, k, co * 128:(co + 1) * 128], zT[:, k, t0:t0 + n], k == 0, k == 3, ["wgl", "zT"], [bk])
                    i = cnt % 2
                    cnt += 1
                    S.op("act", lambda h, bank=bank, n=n, i=i, co=co: h.activation(out=sgs[i][:, 0:n], in_=bank[:, 0:n], func=AF.Sigmoid, bias=glb[:, co:co + 1]),
                         reads=[bk, "glb"], writes=[("sgs", i)])
                    S.op(self.ew(), lambda h, n=n, i=i, co=co, t0=t0: h.tensor_tensor(out=yTs[:, co, t0:t0 + n], in0=zT[:, co, t0:t0 + n], in1=sgs[i][:, 0:n], op=ALU.mult),
                         reads=[("sgs", i), "zT"], writes=["yT_all"])
            S.barrier()
        self.tap_yT("yTs", yTs, l)


def build_nc(nlayers=NL, taps=(), stop_after=None):
    return K(nlayers=nlayers, taps=taps, stop_after=stop_after).build()


def make_in_maps(inputs):
    consts = make_consts()
    shared = {k: np.ascontiguousarray(np.asarray(v), dtype=np.float32) for k, v in inputs.items() if k != "x"}
    x = np.asarray(inputs["x"], dtype=np.float32)
    maps = []
    for c in range(8):
        m = dict(shared)
        m["x"] = np.ascontiguousarray(x[c])
        m.update(consts)
        maps.append(m)
    return maps


def kernel(**inputs):
    nc = build_nc()
    res = run_bass_kernel_spmd(nc, make_in_maps(inputs), core_ids=list(range(8)))
    return np.stack([np.asarray(r["out"], dtype=np.float32) for r in res.results], axis=0)
```

```python
import math
from contextlib import ExitStack

import numpy as np
import concourse.bass as bass
import concourse.mybir as mybir
from concourse.bass_utils import run_bass_kernel_spmd
from concourse.alu_op_type import AluOpType as ALU

F32 = mybir.dt.float32
BF16 = mybir.dt.bfloat16
I32 = mybir.dt.int32
AF = mybir.ActivationFunctionType

D = 1024
SEQ = 2048
NMETA = 16
LP = 2176
NT = 17
FRONT = 112
DIN = 4864
DFF = 4096
NL = 2
EPS = 1e-6
OFF_K, OFF_V, OFF_SSM, OFF_POOL, OFF_GATE = 512, 640, 768, 1280, 1792
NCH = 272
TCH = [(0, 512), (512, 512), (1024, 512), (1536, 512), (2048, 128)]


class Sched:
    ENG = ("pe", "act", "dve", "pool", "sp")

    def __init__(self, nc, es, n_dma_sems=28):
        self.nc = nc
        self.lists = {e: [] for e in self.ENG}
        self.sem = {e: es.enter_context(nc.semaphore("s_" + e)) for e in self.ENG}
        self.count = {e: 0 for e in self.ENG}
        self.waited = {e: {} for e in self.ENG}
        self.dma_sems = [es.enter_context(nc.semaphore("s_dma%d" % i)) for i in range(n_dma_sems)]
        self.dma_tot = [0] * n_dma_sems
        self.dma_rr = 0
        self.dma_rr_sw = 0
        self.res = {}
        self.ninstr = 0

    def _semobj(self, k):
        return self.sem[k] if isinstance(k, str) else self.dma_sems[k[1]]

    def _wait(self, eng, tok, raw=False):
        k, v = tok
        if k == eng and (not raw or eng == "pe"):
            return
        cur = self.waited[eng].get(k, 0)
        if cur >= v:
            return
        self.waited[eng][k] = v
        so = self._semobj(k)
        self.lists[eng].append(lambda h, so=so, v=v: h.wait_ge(so, v))

    def _deps(self, eng, reads, writes):
        for r in reads:
            st = self.res.get(r)
            if st and st[0] is not None:
                self._wait(eng, st[0], raw=True)
        for w in writes:
            st = self.res.get(w)
            if st:
                if st[0] is not None:
                    self._wait(eng, st[0])
                for k, v in st[1].items():
                    self._wait(eng, (k, v))

    def _commit(self, tok, reads, writes):
        k, v = tok
        for r in reads:
            st = self.res.setdefault(r, [None, {}])
            if st[1].get(k, 0) < v:
                st[1][k] = v
        for w in writes:
            self.res[w] = [tok, {}]

    def op(self, eng, fn, reads=(), writes=()):
        self._deps(eng, reads, writes)
        self.count[eng] += 1
        v = self.count[eng]
        so = self.sem[eng]
        self.lists[eng].append(lambda h, fn=fn, so=so: fn(h).then_inc(so, 1))
        self._commit((eng, v), reads, writes)
        self.ninstr += 1

    def dma(self, eng, out, in_, reads=(), writes=(), **kw):
        self._deps(eng, reads, writes)
        if eng == "pool":
            i = 16 + self.dma_rr_sw
            self.dma_rr_sw = (self.dma_rr_sw + 1) % (len(self.dma_sems) - 16)
        else:
            i = self.dma_rr
            self.dma_rr = (self.dma_rr + 1) % 16
        if self.dma_tot[i] > 0:
            k = ("d", i)
            cur = self.waited[eng].get(k, 0)
            if cur < self.dma_tot[i]:
                self.waited[eng][k] = self.dma_tot[i]
                so0, v0 = self.dma_sems[i], self.dma_tot[i]
                self.lists[eng].append(lambda h, so0=so0, v0=v0: h.wait_ge(so0, v0))
        self.dma_tot[i] += 16
        v = self.dma_tot[i]
        so = self.dma_sems[i]
        self.lists[eng].append(
            lambda h, so=so, out=out, in_=in_, kw=kw: h.dma_start(out=out, in_=in_, **kw).then_inc(so, 16))
        self._commit((("d", i), v), reads, writes)
        self.ninstr += 1

    def barrier(self):
        for e in self.ENG:
            for f in self.ENG:
                if f != e and self.count[f] > 0:
                    self._wait(e, (f, self.count[f]))
            for i, t in enumerate(self.dma_tot):
                if t > 0:
                    self._wait(e, (("d", i), t))
        self.res = {}

    def final_wait(self, eng="sp"):
        for f in self.ENG:
            if f != eng and self.count[f] > 0:
                self._wait(eng, (f, self.count[f]))
        for i, t in enumerate(self.dma_tot):
            if t > 0:
                self._wait(eng, (("d", i), t))

    def emit(self, block):
        L = self.lists

        @block.sync
        def _(h):
            for f in L["sp"]:
                f(h)

        @block.scalar
        def _(h):
            for f in L["act"]:
                f(h)

        @block.vector
        def _(h):
            for f in L["dve"]:
                f(h)

        @block.gpsimd
        def _(h):
            for f in L["pool"]:
                f(h)

        @block.tensor
        def _(h):
            for f in L["pe"]:
                f(h)


def make_consts():
    c = {}
    c["c_ident"] = np.eye(128, dtype=np.float32)
    t = np.arange(LP, dtype=np.float32) - FRONT
    r = np.arange(128)
    j = r % 64
    invf = (10000.0 ** (-(np.arange(32, dtype=np.float32)) * 2.0 / 64)).astype(np.float32)
    ang = (t[None, :] * invf[j % 32][:, None]).astype(np.float32)
    c["c_cos"] = np.cos(ang).astype(np.float32)
    sgn = np.where(j < 32, -1.0, 1.0).astype(np.float32)
    c["c_sin"] = (np.sin(ang) * sgn[:, None]).astype(np.float32)
    kl = np.arange(128)[:, None]
    ql = np.arange(128)[None, :]
    mp = np.where(kl >= ql, 0.0, -30000.0).astype(np.float32)
    mn = np.where(kl <= ql, 0.0, -30000.0).astype(np.float32)
    c["c_maskp"] = np.tile(mp, (1, 4))
    c["c_maskn"] = np.tile(mn, (1, 4))
    s = np.arange(128)[:, None] // 16
    tt = np.arange(128)[None, :] // 16
    c["c_maskfb"] = np.concatenate([(tt >= s), (tt <= s)], axis=1).astype(np.float32)
    tau = np.zeros((128, 2, 16, 9), np.float32)
    tau[:, 0, :, 0:8] = np.arange(8, dtype=np.float32)[None, None, :]
    tau[:, 1, :, 0:8] = (7 - np.arange(8, dtype=np.float32))[None, None, :]
    tau[:, :, :, 8] = 8.0
    c["c_tau"] = tau.reshape(128, 288)
    rc = np.zeros((4, LP), np.float32)
    L = NMETA + SEQ
    idx = np.arange(L)
    for gi, w in enumerate((2, 4, 8, 16)):
        lo = np.clip(idx - w // 2, 0, L)
        hi = np.clip(idx + w // 2, 0, L)
        rc[gi, FRONT:] = 1.0 / (hi - lo).astype(np.float32)
    c["c_rcnt"] = rc
    return c


CONST_SHAPES = {"c_ident": [128, 128], "c_cos": [128, LP], "c_sin": [128, LP], "c_maskp": [128, 512],
                "c_maskn": [128, 512], "c_maskfb": [128, 256], "c_tau": [128, 288], "c_rcnt": [4, LP]}

IN_SHAPES = {
    "x": [SEQ, D], "meta_tokens": [NMETA, D], "norm_mix": [NL, D], "w_in": [NL, D, DIN], "attn_sink": [NL, 8],
    "ssm_lam_re": [NL, 2, 32, 64], "ssm_lam_im": [NL, 2, 32, 64], "ssm_log_dt": [NL, 2, 32],
    "ssm_b_re": [NL, 2, 32, 64, 16], "ssm_b_im": [NL, 2, 32, 64, 16], "ssm_c_re": [NL, 2, 32, 16, 64],
    "ssm_c_im": [NL, 2, 32, 16, 64], "ssm_d": [NL, 512], "ssm_glu_w": [NL, 512, 512], "ssm_glu_b": [NL, 512],
    "pool_w": [NL, 4, 128, 128], "pool_scale": [NL, 512], "w_branch": [NL, 3, 512, D], "w_out": [NL, D, D],
    "norm_mlp": [NL, D], "w_up": [NL, D, DFF], "w_down": [NL, DFF, D], "norm_final": [D],
}


def AP(t, offset, ap):
    return bass.AP(tensor=t, offset=offset, ap=[list(a) for a in ap])


def bc(ap, axis, n):
    a = ap.unsqueeze(axis)
    shp = list(a.shape)
    shp[axis] = n
    return a.to_broadcast(shp)


class K:
    def __init__(self, nlayers=NL, taps=(), stop_after=None):
        self.nlayers = nlayers
        self.stop_after = stop_after
        nc = self.nc = bass.Bass("TRN2", target_bir_lowering=False)
        self.T = {}
        for k, shp in IN_SHAPES.items():
            self.T[k] = nc.dram_tensor(k, shp, F32, kind="ExternalInput")
        for k, shp in CONST_SHAPES.items():
            self.T[k] = nc.dram_tensor(k, shp, F32, kind="ExternalInput")
        self.A = {k: v.ap() for k, v in self.T.items()}
        self.out = nc.dram_tensor("out", [SEQ, D], F32, kind="ExternalOutput").ap()
        self.h_d = nc.dram_tensor("h_scr", [LP, D], F32, kind="Internal").ap()
        self.u_t = nc.dram_tensor("u_scr", [LP, 512], BF16, kind="Internal")
        self.u_d = self.u_t.ap()
        self.z_t = nc.dram_tensor("z_scr", [LP, 512], BF16, kind="Internal")
        self.z_d = self.z_t.ap()
        self.tap_out = {}
        for name, shp in taps:
            self.tap_out[name] = nc.dram_tensor("tap_" + name, shp, F32, kind="ExternalOutput").ap()
        self.uid = 0
        self.rr = {"ps": 0, "pt": 0, "ev": 0, "ew": 0}

    def sbuf(self, st, name, shape, dt):
        self.uid += 1
        return st.enter_context(self.nc.sbuf_tensor("%s_%d" % (name, self.uid), shape, dt))

    def next_ps(self):
        i = self.rr["ps"]
        self.rr["ps"] = (i + 1) % len(self.ps)
        return self.ps[i], ("ps", i)

    def next_pt(self):
        i = self.rr["pt"]
        self.rr["pt"] = (i + 1) % len(self.pt)
        return self.pt[i], ("pt", i)

    def evac(self, out_ap, in_ap, reads, writes, eng=None):
        S = self.S
        if eng is None:
            self.rr["ev"] ^= 1
            eng = "act" if self.rr["ev"] else "dve"
        if eng == "act":
            S.op("act", lambda h: h.activation(out=out_ap, in_=in_ap, func=AF.Copy), reads=reads, writes=writes)
        else:
            S.op(eng, lambda h: h.tensor_copy(out=out_ap, in_=in_ap), reads=reads, writes=writes)

    def ew(self):
        self.rr["ew"] ^= 1
        return "dve" if self.rr["ew"] else "pool"

    def tap(self, name, src_ap, reads):
        if name in self.tap_out:
            self.S.dma("sp", self.tap_out[name], src_ap, reads=reads)

    def load_w(self, dst_ap, name, base, rows0, nk, ld, c0, ncols, writes, eng="pool"):
        src = AP(self.T[name], base + rows0 * ld + c0, [[ld, 128], [128 * ld, nk], [1, ncols]])
        self.S.dma(eng, dst_ap, src, writes=writes)

    def mm(self, out_ap, lhsT, rhs, start, stop, reads, writes):
        self.S.op("pe", lambda h: h.matmul(out_ap, lhsT=lhsT, rhs=rhs, start=start, stop=stop), reads=reads, writes=writes)

    def tr(self, out_ap, in_ap, idn, reads, writes):
        self.S.op("pe", lambda h: h.transpose(out=out_ap, in_=in_ap, identity=idn), reads=list(reads) + ["ident"], writes=writes)

    def build(self):
        nc = self.nc
        with ExitStack() as es:
            S = self.S = Sched(nc, es)
            self.ps = [es.enter_context(nc.psum_tensor("ps%d" % i, [128, 512], F32)) for i in range(6)]
            self.pt = [es.enter_context(nc.psum_tensor("pt%d" % i, [128, 1024], BF16)) for i in range(2)]
            self.identf = self.sbuf(es, "identf", [128, 128], F32)
            self.ident = self.sbuf(es, "ident", [128, 128], BF16)
            self.small = self.sbuf(es, "small", [128, 16], F32)
            self.zt = self.sbuf(es, "zt", [FRONT, 256], F32)
            S.dma("sp", self.identf[:], self.A["c_ident"], writes=["identf"])
            S.op("dve", lambda h: h.tensor_copy(out=self.ident[:], in_=self.identf[:]), reads=["identf"], writes=["ident"])
            S.op("pool", lambda h: h.memset(self.small[:, 0:1], EPS), writes=["eps"])
            self.init_h()
            for l in range(self.nlayers):
                self.layer(l)
            S.final_wait("sp")
            with nc.Block() as block:
                S.emit(block)
        return nc

    def init_h(self):
        S = self.S
        zt = self.zt
        S.op("pool", lambda h: h.memset(zt[:], 0.0), writes=["zt"])
        for c in range(4):
            S.dma("sp", self.h_d[0:FRONT, c * 256:(c + 1) * 256], zt[:], reads=["zt"], writes=[("h", 0)])
        S.dma("sp", self.h_d[FRONT:128, :], self.A["meta_tokens"], writes=[("hm", 0)])
        for b in range(1, NT):
            S.dma("sp", self.h_d[b * 128:(b + 1) * 128, :], self.A["x"][(b - 1) * 128:b * 128, :], writes=[("h", b)])

    def norm_transpose(self, st, gname, goff, hnT, src_tiles):
        S = self.S
        small = self.small
        gain = self.sbuf(st, "gain", [128, D], F32)
        S.dma("sp", gain[:], AP(self.T[gname], goff, [[0, 128], [1, D]]), writes=["gain"])
        junk = self.sbuf(st, "junk", [128, D], BF16)
        hnb = [self.sbuf(st, "hnb%d" % i, [128, D], BF16) for i in range(2)]
        ss = self.sbuf(st, "ss", [128, 3 * NT], F32)
        for b in range(NT):
            src, rd = src_tiles(b)
            S.op("act", lambda h, src=src, b=b: h.activation(out=junk[:], in_=src, func=AF.Square, accum_out=ss[:, b:b + 1]),
                 reads=rd, writes=["junk", ("ss", b)])
            S.op("act", lambda h, b=b: h.activation(out=ss[:, NT + b:NT + b + 1], in_=ss[:, b:b + 1], func=AF.Sqrt,
                                                    scale=1.0 / D, bias=small[:, 0:1]),
                 reads=[("ss", b), "eps"], writes=[("ss1", b)])
            S.op("dve", lambda h, b=b: h.reciprocal(out=ss[:, 2 * NT + b:2 * NT + b + 1], in_=ss[:, NT + b:NT + b + 1]),
                 reads=[("ss1", b)], writes=[("ss2", b)])
            i = b % 2
            S.op("dve", lambda h, src=src, b=b, i=i: h.scalar_tensor_tensor(
                out=hnb[i][:], in0=src, scalar=ss[:, 2 * NT + b:2 * NT + b + 1], in1=gain[:], op0=ALU.mult, op1=ALU.mult),
                reads=list(rd) + [("ss2", b), "gain"], writes=[("hnb", i)])
            bank, bk = self.next_pt()
            for k in range(8):
                self.tr(bank[:, k * 128:(k + 1) * 128], hnb[i][:, k * 128:(k + 1) * 128], self.ident[:, :], [("hnb", i)], [bk])
            self.evac(hnT[:, :, b * 128:(b + 1) * 128], bank[:, :].rearrange("p (k t) -> p k t", k=8), [bk], [("hnT", b)])

    def layer(self, l):
        S = self.S
        with ExitStack() as lst:
            hnT = self.sbuf(lst, "hnT", [128, 8, LP], BF16)
            if l == 0 and "hnT" in self.tap_out:
                with ExitStack() as st:
                    tmp = self.sbuf(st, "tmp", [128, 8 * LP], F32)
                    S.op("dve", lambda h: h.tensor_copy(out=tmp[:], in_=hnT[:].rearrange("p k t -> p (k t)")), reads=[("hnT", b) for b in range(NT)], writes=["tmp"])
                    self.tap("hnT", tmp[:], ["tmp"])
                    S.barrier()
            if self.stop_after == "n1":
                return
            self.phase_ssm_core(l, hnT)
            if self.stop_after in ("ssmA", "ssmB", "ssmC", "ssmD", "ssmD1", "ssmD2"):
                return
            yT = [self.sbuf(lst, "yT%d" % c, [128, 4, LP], BF16) for c in (1,)]
            self.phase_ssm_glu(l, yT[0])
            if self.stop_after == "ssm":
                return
            yTa = self.sbuf(lst, "yTa", [128, 4, LP], BF16)
            self.phase_attn(l, hnT, yTa)
            if self.stop_after == "attn":
                return
            yTp = self.sbuf(lst, "yTp", [128, 4, LP], BF16)
            self.phase_pool(l, hnT, yTp)
            if self.stop_after == "pool":
                return
            self.phase_merge(l, hnT, [yTa, yT[0], yTp])
        if self.stop_after == "merge":
            return
        self.phase_ffn(l)

    def tap_yT(self, name, yT, l):
        S = self.S
        if l == 0 and name in self.tap_out:
            with ExitStack() as st:
                tmp = self.sbuf(st, "tmp", [128, 4 * LP], F32)
                S.op("dve", lambda h: h.tensor_copy(out=tmp[:], in_=yT[:].rearrange("p k t -> p (k t)")), reads=["yT_all"], writes=["tmp"])
                self.tap(name, tmp[:], ["tmp"])
                S.barrier()

    def phase_pool(self, l, hnT, yTp):
        S = self.S
        WX, OFF = LP + 64, 32
        with ExitStack() as st:
            wp = [self.sbuf(st, "wp%d" % i, [128, 8, 128], BF16) for i in range(2)]
            pw = [self.sbuf(st, "pw%d" % i, [128, 128], BF16) for i in range(2)]
            ua = self.sbuf(st, "ua", [128, WX], F32)
            pb = [self.sbuf(st, "pb%d" % i, [128, WX], F32) for i in range(2)]
            rcb = self.sbuf(st, "rcb", [128, LP], F32)
            dd = self.sbuf(st, "dd", [128, LP], BF16)
            psc = self.sbuf(st, "psc", [128, 4], F32)
            S.dma("sp", psc[:], AP(self.T["pool_scale"], l * 512, [[1, 128], [128, 4]]), writes=["psc"], allow_slow_non_contiguous=True)
            S.op("pool", lambda h: h.memset(ua[:], 0.0), writes=["ua"])
            S.op("pool", lambda h: h.memset(pb[0][:], 0.0), writes=["pb0"])
            S.op("pool", lambda h: h.memset(pb[1][:], 0.0), writes=["pb1"])
            shifts = [(-1, 0), (-1, 1), (-2, 2), (-4, 4)]
            for g in range(4):
                i = g % 2
                self.load_w(wp[i][:], "w_in", l * D * DIN, 0, 8, DIN, OFF_POOL + 128 * g, 128, [("wp", i)])
                S.dma("pool", pw[i][:], AP(self.T["pool_w"], (l * 4 + g) * 16384, [[128, 128], [1, 128]]), writes=[("pw", i)])
                S.dma("sp", rcb[:], AP(self.T["c_rcnt"], g * LP, [[0, 128], [1, LP]]), writes=["rcb"])
                for (t0, n) in TCH:
                    bank, bk = self.next_ps()
                    for k in range(8):
                        self.mm(bank[:, 0:n], wp[i][:, k, :], hnT[:, k, t0:t0 + n], k == 0, k == 7, [("wp", i)], [bk])
                    self.evac(ua[:, OFF + t0:OFF + t0 + n], bank[:, 0:n], [bk], ["ua"])
                cur, curk = ua, "ua"
                lo, hi = OFF - 16, OFF + LP + 16
                for lev in range(g + 1):
                    s0, s1 = shifts[lev]
                    dst, dk = pb[lev % 2], "pb%d" % (lev % 2)
                    S.op(self.ew(), lambda h, dst=dst, cur=cur, s0=s0, s1=s1: h.tensor_tensor(
                        out=dst[:, lo:hi], in0=cur[:, lo + s0:hi + s0], in1=cur[:, lo + s1:hi + s1], op=ALU.add),
                        reads=[curk], writes=[dk])
                    cur, curk = dst, dk
                oth, ok = pb[(g + 1) % 2], "pb%d" % ((g + 1) % 2)
                S.op("dve", lambda h, cur=cur, oth=oth: h.tensor_tensor(out=oth[:, OFF:OFF + LP], in0=cur[:, OFF:OFF + LP], in1=rcb[:], op=ALU.mult),
                     reads=[curk, "rcb"], writes=[ok])
                S.op("pool", lambda h, oth=oth: h.tensor_tensor(out=dd[:], in0=oth[:, OFF:OFF + LP], in1=ua[:, OFF:OFF + LP], op=ALU.subtract),
                     reads=[ok, "ua"], writes=["dd"])
                for (t0, n) in TCH:
                    bank, bk = self.next_ps()
                    self.mm(bank[:, 0:n], pw[i][:], dd[:, t0:t0 + n], True, True, [("pw", i), "dd"], [bk])
                    S.op("act", lambda h, bank=bank, n=n, t0=t0, g=g: h.activation(out=yTp[:, g, t0:t0 + n], in_=bank[:, 0:n], func=AF.Copy,
                                                                                    scale=psc[:, g:g + 1]),
                         reads=[bk, "psc"], writes=["yT_all"])
            S.barrier()
        self.tap_yT("yTp", yTp, l)

    def phase_attn(self, l, hnT, yTa):
        S = self.S
        A = self.A
        with ExitStack() as st:
            qT = self.sbuf(st, "qT", [128, 4, LP], BF16)
            kT = self.sbuf(st, "kT", [128, 2, LP], BF16)
            Va = self.sbuf(st, "Va", [128, NT, 2, 65], BF16)
            Vm = self.sbuf(st, "Vm", [16, 2, 65], BF16)
            cosT = self.sbuf(st, "cosT", [128, LP], F32)
            sinT = self.sbuf(st, "sinT", [128, LP], F32)
            mkf = self.sbuf(st, "mkf", [128, 1024], F32)
            mk = self.sbuf(st, "mk", [128, 1024], BF16)
            ww = [self.sbuf(st, "ww%d" % i, [128, 8, 128], BF16) for i in range(2)]
            wr = [self.sbuf(st, "wr%d" % i, [128, 8, 128], BF16) for i in range(2)]
            t1 = [self.sbuf(st, "t1%d" % i, [128, 512], F32) for i in range(2)]
            t2 = [self.sbuf(st, "t2%d" % i, [128, 512], F32) for i in range(2)]
            PT = [self.sbuf(st, "PT%d" % i, [128, 512], BF16) for i in range(8)]
            den = self.sbuf(st, "den", [128, 16], F32)
            snk = self.sbuf(st, "snk", [128, 8], F32)
            S.dma("sp", cosT[:], A["c_cos"], writes=["cosT"])
            S.dma("sp", sinT[:], A["c_sin"], writes=["sinT"])
            S.dma("sp", mkf[:, 0:512], A["c_maskp"], writes=["mkf"])
            S.dma("sp", mkf[:, 512:1024], A["c_maskn"], writes=["mkf"])
            S.op("dve", lambda h: h.tensor_copy(out=mk[:], in_=mkf[:]), reads=["mkf"], writes=["mk"])
            S.dma("sp", snk[:], AP(self.T["attn_sink"], l * 8, [[0, 128], [1, 8]]), writes=["snk"])
            S.op("act", lambda h: h.activation(out=snk[:], in_=snk[:], func=AF.Exp), reads=["snk"], writes=["snk"])
            S.op("pool", lambda h: h.memset(Va[:, :, :, 64:65], 1.0), writes=["Va1"])
            S.op("pool", lambda h: h.memset(Vm[:, :, 64:65], 1.0), writes=["Vm1"])
            for ct in range(6):
                i = ct % 2
                if ct < 4:
                    self.load_w(ww[i][:], "w_in", l * D * DIN, 0, 8, DIN, 128 * ct, 128, [("ww", i)])
                    dst = lambda t0, n, ct=ct: qT[:, ct, t0:t0 + n]
                else:
                    kh = ct - 4
                    self.load_w(ww[i][:, :, 0:64], "w_in", l * D * DIN, 0, 8, DIN, OFF_K + 64 * kh, 64, [("ww", i)])
                    self.load_w(ww[i][:, :, 64:128], "w_in", l * D * DIN, 0, 8, DIN, OFF_K + 64 * kh, 64, [("ww", i)])
                    dst = lambda t0, n, kh=kh: kT[:, kh, t0:t0 + n]
                wv = ww[i][:].rearrange("p k (a two j) -> p k a two j", a=2, two=2, j=32)
                rv = wr[i][:].rearrange("p k (a two j) -> p k a two j", a=2, two=2, j=32)
                S.op("dve", lambda h, wv=wv, rv=rv: h.tensor_copy(out=rv[:, :, :, 0, :], in_=wv[:, :, :, 1, :]), reads=[("ww", i)], writes=[("wr", i)])
                S.op("act", lambda h, wv=wv, rv=rv: h.activation(out=rv[:, :, :, 1, :], in_=wv[:, :, :, 0, :], func=AF.Copy), reads=[("ww", i)], writes=[("wr", i)])
                for ti, (t0, n) in enumerate(TCH):
                    j = ti % 2
                    ba, bka = self.next_ps()
                    bb_, bkb = self.next_ps()
                    for k in range(8):
                        self.mm(ba[:, 0:n], ww[i][:, k, :], hnT[:, k, t0:t0 + n], k == 0, k == 7, [("ww", i)], [bka])
                    for k in range(8):
                        self.mm(bb_[:, 0:n], wr[i][:, k, :], hnT[:, k, t0:t0 + n], k == 0, k == 7, [("wr", i)], [bkb])
                    S.op("dve", lambda h, ba=ba, n=n, t0=t0, j=j: h.tensor_tensor(out=t1[j][:, 0:n], in0=ba[:, 0:n], in1=cosT[:, t0:t0 + n], op=ALU.mult),
                         reads=[bka, "cosT"], writes=[("t1", j)])
                    S.op("dve", lambda h, bb_=bb_, n=n, t0=t0, j=j: h.tensor_tensor(out=t2[j][:, 0:n], in0=bb_[:, 0:n], in1=sinT[:, t0:t0 + n], op=ALU.mult),
                         reads=[bkb, "sinT"], writes=[("t2", j)])
                    d_ap = dst(t0, n)
                    S.op("dve", lambda h, d_ap=d_ap, n=n, j=j: h.tensor_tensor(out=d_ap, in0=t1[j][:, 0:n], in1=t2[j][:, 0:n], op=ALU.add),
                         reads=[("t1", j), ("t2", j)], writes=["qk"])
            wvv = ww[0]
            self.load_w(wvv[:], "w_in", l * D * DIN, 0, 8, DIN, OFF_V, 128, [("ww", 0)])
            for b in range(NT):
                bank, bk = self.next_ps()
                for k in range(8):
                    self.mm(bank[:, 0:128], hnT[:, k, b * 128:(b + 1) * 128], wvv[:, k, :], k == 0, k == 7, [("ww", 0)], [bk])
                self.evac(Va[:, b, :, 0:64], bank[:, 0:128].rearrange("p (a d) -> p a d", a=2), [bk], [("Va", b)])
            bank, bk = self.next_ps()
            for k in range(8):
                self.mm(bank[0:16, 0:128], hnT[:, k, FRONT:128], wvv[:, k, :], k == 0, k == 7, [("ww", 0)], [bk])
            self.evac(Vm[:, :, 0:64], bank[0:16, 0:128].rearrange("p (a d) -> p a d", a=2), [bk], ["Vm"])
            if l == 0 and "qT" in self.tap_out:
                with ExitStack() as st2:
                    tmp = self.sbuf(st2, "tmp", [128, 6 * LP], F32)
                    S.op("dve", lambda h: h.tensor_copy(out=tmp[:, 0:4 * LP], in_=qT[:].rearrange("p k t -> p (k t)")), reads=["qk"], writes=["tmp"])
                    S.op("dve", lambda h: h.tensor_copy(out=tmp[:, 4 * LP:6 * LP], in_=kT[:].rearrange("p k t -> p (k t)")), reads=["qk"], writes=["tmp"])
                    self.tap("qT", tmp[:], ["tmp"])
                    S.barrier()
            ytm = self.sbuf(st, "ytmall", [128, NT, 512], BF16)
            for n_ in range(NT):
                for kh in range(2):
                    kbs = []
                    if n_ - 1 >= 1:
                        kbs.append((n_ - 1, 0))
                    if n_ >= 1:
                        kbs.append((n_, None))
                    if n_ + 1 <= NT - 1:
                        kbs.append((n_ + 1, 1))
                    kbs.append((-1, None))
                    slots = []
                    for idx, (kb, mi) in enumerate(kbs):
                        nk = 128 if kb >= 0 else 16
                        kc0 = kb * 128 if kb >= 0 else FRONT
                        sl = (kh * 4 + idx)
                        for half in range(2):
                            bank, bk = self.next_ps()
                            if mi is not None:
                                self.mm(bank[:, 0:256], self.ident[:], mk[:, mi * 512:mi * 512 + 256], True, False, ["ident", "mk"], [bk])
                            for ii in range(2):
                                i = 2 * ii + half
                                hq = 4 * kh + i
                                tile_ = hq // 2
                                r0 = 64 * half
                                self.mm(bank[0:nk, ii * 128:(ii + 1) * 128], kT[r0:r0 + 64, kh, kc0:kc0 + nk],
                                        qT[r0:r0 + 64, tile_, n_ * 128:(n_ + 1) * 128], mi is None, True, ["qk"], [bk])
                            S.op("act", lambda h, bank=bank, nk=nk, sl=sl, half=half: h.activation(out=PT[sl][0:nk, half * 256:(half + 1) * 256], in_=bank[0:nk, 0:256], func=AF.Exp, scale=0.125),
                                 reads=[bk], writes=[("PT", sl)])
                        slots.append((sl, kb, nk))
                    ob, obk = self.next_ps()
                    for i in range(4):
                        pc0 = (i % 2) * 256 + (i // 2) * 128
                        for idx, (sl, kb, nk) in enumerate(slots):
                            vsrc = Va[:, kb, kh, :] if kb >= 0 else Vm[:, kh, :]
                            rds = [("PT", sl)] + ([("Va", kb), "Va1"] if kb >= 0 else ["Vm", "Vm1"])
                            self.mm(ob[:, i * 65:(i + 1) * 65], PT[sl][0:nk, pc0:pc0 + 128], vsrc, idx == 0, idx == len(slots) - 1, rds, [obk])
                    ov = ob[:, 0:260].rearrange("p (i c) -> p i c", i=4)
                    dsl = den[:, kh * 8:kh * 8 + 4]
                    rsl = den[:, kh * 8 + 4:kh * 8 + 8]
                    S.op("dve", lambda h, ov=ov, dsl=dsl, kh=kh: h.tensor_tensor(out=dsl.unsqueeze(2), in0=ov[:, :, 64:65], in1=snk[:, 4 * kh:4 * kh + 4].unsqueeze(2), op=ALU.add),
                         reads=[obk, "snk"], writes=[("den", kh)])
                    S.op("dve", lambda h, dsl=dsl, rsl=rsl: h.reciprocal(out=rsl, in_=dsl), reads=[("den", kh)], writes=[("rden", kh)])
                    for i in range(4):
                        hq = 4 * kh + i
                        S.op("act", lambda h, ob=ob, i=i, hq=hq, n_=n_, rsl=rsl: h.activation(out=ytm[:, n_, hq * 64:(hq + 1) * 64], in_=ob[:, i * 65:i * 65 + 64],
                                                                                           func=AF.Copy, scale=rsl[:, i:i + 1]),
                             reads=[obk, ("rden", kh)], writes=["ytm"])
            S.barrier()
            for n_ in range(NT):
                bank, bk = self.next_pt()
                for j in range(4):
                    self.tr(bank[:, j * 128:(j + 1) * 128], ytm[:, n_, j * 128:(j + 1) * 128], self.ident[:, :], ["ytm"], [bk])
                self.evac(yTa[:, :, n_ * 128:(n_ + 1) * 128], bank[:, 0:512].rearrange("p (j t) -> p j t", j=4), [bk], ["yT_all"])
            S.barrier()
        self.tap_yT("yTa", yTa, l)

    def phase_merge(self, l, hnT, yTs):
        S = self.S
        with ExitStack() as st:
            wg = [self.sbuf(st, "wg%d" % i, [128, 8, 3, 128], BF16) for i in range(2)]
            wb = [self.sbuf(st, "wb%d" % i, [128, 12, 128], BF16) for i in range(2)]
            wo = self.sbuf(st, "wo", [128, 4, D], BF16)
            mT = self.sbuf(st, "mT", [128, 4, LP], BF16)
            sg = [self.sbuf(st, "sg%d" % i, [128, 512], F32) for i in range(2)]
            pa = [self.sbuf(st, "pa%d" % i, [128, 512], F32) for i in range(3)]
            ht = [self.sbuf(st, "ht%d" % i, [128, D], F32) for i in range(4)]
            cnt = 0
            for half in range(2):
                for dt in range(4):
                    dti = 4 * half + dt
                    i = dti % 2
                    for c in range(3):
                        self.load_w(wg[i][:, :, c, :], "w_in", l * D * DIN, 0, 8, DIN, OFF_GATE + c * D + dti * 128, 128, [("wg", i)])
                    self.load_w(wb[i][:], "w_branch", l * 3 * 512 * D, 0, 12, D, dti * 128, 128, [("wb", i)])
                    for (t0, n) in TCH:
                        for c in range(3):
                            bB, kB = self.next_ps()
                            bG, kG = self.next_ps()
                            for k in range(4):
                                self.mm(bB[:, 0:n], wb[i][:, 4 * c + k, :], yTs[c][:, k, t0:t0 + n], k == 0, k == 3, [("wb", i), "yT_all"], [kB])
                            for k in range(8):
                                self.mm(bG[:, 0:n], wg[i][:, k, c, :], hnT[:, k, t0:t0 + n], k == 0, k == 7, [("wg", i)], [kG])
                            j = cnt % 2
                            cnt += 1
                            S.op("act", lambda h, bG=bG, n=n, j=j: h.activation(out=sg[j][:, 0:n], in_=bG[:, 0:n], func=AF.Sigmoid), reads=[kG], writes=[("sg", j)])
                            S.op("dve", lambda h, bB=bB, n=n, j=j, c=c: h.tensor_tensor(out=pa[c][:, 0:n], in0=bB[:, 0:n], in1=sg[j][:, 0:n], op=ALU.mult),
                                 reads=[kB, ("sg", j)], writes=[("pa", c)])
                        S.op("dve", lambda h, n=n: h.tensor_tensor(out=pa[0][:, 0:n], in0=pa[0][:, 0:n], in1=pa[1][:, 0:n], op=ALU.add),
                             reads=[("pa", 0), ("pa", 1)], writes=[("pa", 0)])
                        S.op("dve", lambda h, n=n, t0=t0, dt=dt: h.tensor_tensor(out=mT[:, dt, t0:t0 + n], in0=pa[0][:, 0:n], in1=pa[2][:, 0:n], op=ALU.add),
                             reads=[("pa", 0), ("pa", 2)], writes=["mT"])
                self.load_w(wo[:], "w_out", l * D * D, half * 512, 4, D, 0, D, ["wo"])
                NB = 4

                def load_h(b):
                    rd = [("h", b)] + ([("hm", 0)] if b == 0 else [])
                    S.dma("sp", ht[b % NB][:], self.h_d[b * 128:(b + 1) * 128, :], reads=rd, writes=[("ht", b % NB)])

                for b in range(NB - 1):
                    load_h(b)
                for b in range(NT):
                    i = b % NB
                    if b + NB - 1 < NT:
                        load_h(b + NB - 1)
                    for ch in range(2):
                        bank, bk = self.next_ps()
                        for dt in range(4):
                            self.mm(bank[:, :], mT[:, dt, b * 128:(b + 1) * 128], wo[:, dt, ch * 512:(ch + 1) * 512], dt == 0, dt == 3, ["mT", "wo"], [bk])
                        S.op("dve", lambda h, bank=bank, i=i, ch=ch: h.tensor_tensor(out=ht[i][:, ch * 512:(ch + 1) * 512], in0=bank[:, :], in1=ht[i][:, ch * 512:(ch + 1) * 512], op=ALU.add),
                             reads=[bk, ("ht", i)], writes=[("ht", i)])
                    S.dma("sp", self.h_d[b * 128:(b + 1) * 128, :], ht[i][:], reads=[("ht", i)], writes=[("h", b), ("hm", 0)] if b == 0 else [("h", b)])
            S.barrier()
        if l == 0 and "h_mix" in self.tap_out:
            S.dma("sp", self.tap_out["h_mix"], self.h_d, reads=[("h", b) for b in range(NT)])
            S.barrier()

    def phase_ffn(self, l):
        S = self.S
        last = (l == NL - 1)
        with ExitStack() as st:
            hs = self.sbuf(st, "hs", [128, NT, D], F32)
            hn2T = self.sbuf(st, "hn2T", [128, 8, LP], BF16)
            for b in range(NT):
                rd = [("h", b)] + ([("hm", 0)] if b == 0 else [])
                S.dma("sp", hs[:, b, :], self.h_d[b * 128:(b + 1) * 128, :], reads=rd, writes=[("hs", b, 0), ("hs", b, 1)])
            with ExitStack() as st2:
                self.norm_transpose(st2, "norm_mlp", l * D, hn2T, lambda b: (hs[:, b, :], [("hs", b, 0), ("hs", b, 1)]))
                wu = [self.sbuf(st2, "wu%d" % i, [128, 8, 512], BF16) for i in range(2)]
                wd = [self.sbuf(st2, "wd%d" % i, [128, 4, D], BF16) for i in range(2)]
                aT = self.sbuf(st2, "aT", [128, 4, LP], BF16)
                rl = [self.sbuf(st2, "rl%d" % i, [128, 512], F32) for i in range(2)]
                cnt = 0
                for fc in range(8):
                    i = fc % 2
                    self.load_w(wu[i][:], "w_up", l * D * DFF, 0, 8, DFF, fc * 512, 512, [("wu", i)])
                    self.load_w(wd[i][:], "w_down", l * DFF * D, fc * 512, 4, D, 0, D, [("wd", i)])
                    for ft in range(4):
                        for (t0, n) in TCH:
                            bank, bk = self.next_ps()
                            for k in range(8):
                                self.mm(bank[:, 0:n], wu[i][:, k, ft * 128:(ft + 1) * 128], hn2T[:, k, t0:t0 + n], k == 0, k == 7,
                                        [("wu", i)] + [("hnT", bb2) for bb2 in range(t0 // 128, (t0 + n) // 128)], [bk])
                            j = cnt % 2
                            cnt += 1
                            S.op("act", lambda h, bank=bank, n=n, j=j: h.activation(out=rl[j][:, 0:n], in_=bank[:, 0:n], func=AF.Relu), reads=[bk], writes=[("rl", j)])
                            S.op("dve", lambda h, n=n, j=j, ft=ft, t0=t0: h.tensor_tensor(out=aT[:, ft, t0:t0 + n], in0=rl[j][:, 0:n], in1=rl[j][:, 0:n], op=ALU.mult),
                                 reads=[("rl", j)], writes=["aT"])
                    for b in range(NT):
                        for ch in range(2):
                            bank, bk = self.next_ps()
                            for ft in range(4):
                                self.mm(bank[:, :], aT[:, ft, b * 128:(b + 1) * 128], wd[i][:, ft, ch * 512:(ch + 1) * 512], ft == 0, ft == 3, ["aT", ("wd", i)], [bk])
                            S.op("dve", lambda h, bank=bank, b=b, ch=ch: h.tensor_tensor(out=hs[:, b, ch * 512:(ch + 1) * 512], in0=bank[:, :], in1=hs[:, b, ch * 512:(ch + 1) * 512], op=ALU.add),
                                 reads=[bk, ("hs", b, ch)], writes=[("hs", b, ch)])
                S.barrier()
            if l == 0 and "h_new" in self.tap_out:
                for b in range(NT):
                    S.dma("sp", self.tap_out["h_new"][b * 128:(b + 1) * 128, :], hs[:, b, :], reads=[("hs", b, 0), ("hs", b, 1)])
            if not last or self.nlayers < NL:
                for b in range(NT):
                    S.dma("sp", self.h_d[b * 128:(b + 1) * 128, :], hs[:, b, :], reads=[("hs", b, 0), ("hs", b, 1)], writes=[("h", b), ("hm", 0)] if b == 0 else [("h", b)])
            if l == self.nlayers - 1:
                with ExitStack() as st2:
                    small = self.small
                    gain = self.sbuf(st2, "gainf", [128, D], F32)
                    S.dma("sp", gain[:], AP(self.T["norm_final"], 0, [[0, 128], [1, D]]), writes=["gainf"])
                    junk = self.sbuf(st2, "junkf", [128, D], BF16)
                    ss = self.sbuf(st2, "ssf", [128, 3 * NT], F32)
                    ob = [self.sbuf(st2, "ob%d" % i, [128, D], F32) for i in range(2)]
                    for b in range(1, NT):
                        i = b % 2
                        S.op("act", lambda h, b=b: h.activation(out=junk[:], in_=hs[:, b, :], func=AF.Square, accum_out=ss[:, b:b + 1]),
                             reads=[("hs", b, 0), ("hs", b, 1)], writes=["junkf", ("ssf", b)])
                        S.op("act", lambda h, b=b: h.activation(out=ss[:, NT + b:NT + b + 1], in_=ss[:, b:b + 1], func=AF.Sqrt, scale=1.0 / D, bias=small[:, 0:1]),
                             reads=[("ssf", b), "eps"], writes=[("ssf1", b)])
                        S.op("dve", lambda h, b=b: h.reciprocal(out=ss[:, 2 * NT + b:2 * NT + b + 1], in_=ss[:, NT + b:NT + b + 1]), reads=[("ssf1", b)], writes=[("ssf2", b)])
                        S.op("dve", lambda h, b=b, i=i: h.scalar_tensor_tensor(out=ob[i][:], in0=hs[:, b, :], scalar=ss[:, 2 * NT + b:2 * NT + b + 1], in1=gain[:], op0=ALU.mult, op1=ALU.mult),
                             reads=[("hs", b, 0), ("hs", b, 1), ("ssf2", b), "gainf"], writes=[("ob", i)])
                        S.dma("sp", self.out[(b - 1) * 128:b * 128, :], ob[i][:], reads=[("ob", i)])
            S.barrier()

    def phase_ssm_core(self, l, hnT):
        S = self.S
        T = self.T
        TWO_PI = 2.0 * math.pi
        with ExitStack() as st:
            UT = self.sbuf(st, "UT", [128, 32, NCH], BF16)
            Kb = self.sbuf(st, "Kb", [128, 32, 128], BF16)
            Cp = self.sbuf(st, "Cp", [128, 2, 16, 2, 128], BF16)
            BpT = self.sbuf(st, "BpT", [128, 2, 16, 2, 128], BF16)
            a8 = self.sbuf(st, "a8", [128, 2, 16, 2, 2], F32)
            with ExitStack() as sa:
                ht = [self.sbuf(sa, "ht%d" % i, [128, D], F32) for i in range(3)]

                def src_tiles(b, l=l):
                    i = b % 3
                    if l == 0:
                        if b == 0:
                            S.op("pool", lambda h, i=i: h.memset(ht[i][0:FRONT, :], 0.0), writes=[("ht", i)])
                            S.dma("sp", ht[i][FRONT:128, :], self.A["meta_tokens"], writes=[("htm", i)])
                            return ht[i][:], [("ht", i), ("htm", i)]
                        S.dma("sp", ht[i][:], self.A["x"][(b - 1) * 128:b * 128, :], writes=[("ht", i), ("htm", i)])
                        return ht[i][:], [("ht", i), ("htm", i)]
                    rd = [("h", b)] + ([("hm", 0)] if b == 0 else [])
                    S.dma("sp", ht[i][:], self.h_d[b * 128:(b + 1) * 128, :], reads=rd, writes=[("ht", i), ("htm", i)])
                    return ht[i][:], [("ht", i), ("htm", i)]

                self.norm_transpose(sa, "norm_mix", l * D, hnT, src_tiles)
                wss = self.sbuf(sa, "wss", [128, 8, 512], BF16)
                utm = [self.sbuf(sa, "utm%d" % i, [128, 512], BF16) for i in range(2)]
                self.load_w(wss[:], "w_in", l * D * DIN, 0, 8, DIN, OFF_SSM, 512, ["wss"])
                for b in range(NT):
                    i = b % 2
                    bank, bk = self.next_ps()
                    for k in range(8):
                        self.mm(bank[:, :], hnT[:, k, b * 128:(b + 1) * 128], wss[:, k, :], k == 0, k == 7, ["wss", ("hnT", b)], [bk])
                    self.evac(utm[i][:], bank[:, :], [bk], [("utm", i)])
                    S.dma("sp", self.u_d[b * 128:(b + 1) * 128, :], utm[i][:], reads=[("utm", i)], writes=[("ud", b)])
                S.barrier()
            su = ExitStack()
            st.enter_context(su)
            ucm = [self.sbuf(su, "ucm%d" % i, [128, 4096], BF16) for i in range(3)]
            if True:
                for ct in range(3):
                    n = 128 if ct < 2 else 16
                    S.dma("sp", ucm[ct][0:n, :], self.u_d.rearrange("(c s) x -> c (s x)", s=8)[ct * 128:ct * 128 + n, :],
                          reads=[("ud", b) for b in range(NT)], writes=[("ucm", ct)])
            with ExitStack() as sb_:
                Bp = self.sbuf(sb_, "Bp", [128, 2, 16, 2, 128], BF16)
                dcol = self.sbuf(sb_, "dcol", [128, 32], F32)
                mfb = self.sbuf(sb_, "mfb", [128, 256], F32)
                tau = self.sbuf(sb_, "tau", [128, 2, 16, 9], F32)
                S.dma("sp", mfb[:], self.A["c_maskfb"], writes=["mfb"])
                S.dma("sp", tau[:].rearrange("p d m t -> p (d m t)"), self.A["c_tau"], writes=["tau"])
                for s in range(8):
                    S.dma("sp", dcol[16 * s:16 * s + 16, :], AP(T["ssm_d"], l * 512, [[1, 16], [16, 32]]), writes=["dcol"], allow_slow_non_contiguous=True)
                prm = self.sbuf(sb_, "prm", [128, 12, 2, 16], F32)
                LR, LI, DT, X, TH, DEN, NRE, FR, FI, ABR, ABI, TMP = range(12)
                for idx_, nm_ in ((LR, "ssm_lam_re"), (LI, "ssm_lam_im")):
                    for d in range(2):
                        for m4 in range(4):
                            S.dma("sp", prm[:, idx_, d, 4 * m4:4 * m4 + 4], AP(T[nm_], l * 4096 + d * 2048 + m4 * 512, [[1, 128], [128, 4]]),
                                  writes=["prm"], allow_slow_non_contiguous=True)
                ldt = self.sbuf(sb_, "ldt", [128, 2, 16, 2], F32)
                S.dma("sp", ldt[:].rearrange("p d m j -> p (d m j)"), AP(T["ssm_log_dt"], l * 64, [[0, 128], [1, 64]]), writes=["ldt"])
                for j in range(2):
                    S.op("dve", lambda h, j=j: h.tensor_copy(out=prm[64 * j:64 * j + 64, DT], in_=ldt[64 * j:64 * j + 64, :, :, j]), reads=["ldt"], writes=["prm"])
                braw = self.sbuf(sb_, "braw", [128, 2, 2, 16, 16], F32)
                craw = self.sbuf(sb_, "craw", [128, 2, 2, 16, 16], F32)
                for ri, nm in enumerate(("ssm_b_re", "ssm_b_im")):
                    for d in range(2):
                        for m4 in range(4):
                            S.dma("sp", braw[:, ri, d, 4 * m4:4 * m4 + 4, :], AP(T[nm], l * 65536 + d * 32768 + m4 * 8192, [[16, 128], [2048, 4], [1, 16]]), writes=["braw"])
                ctmp = [self.sbuf(sb_, "ctmp%d" % i, [128, 2, 64], F32) for i in range(2)]
                cc = 0
                for ri, nm in enumerate(("ssm_c_re", "ssm_c_im")):
                    for d in range(2):
                        for i4 in range(4):
                            ci = cc % 2
                            cc += 1
                            S.dma("sp", ctmp[ci][:], AP(T[nm], l * 65536 + d * 32768 + i4 * 8192, [[64, 128], [0, 2], [1, 64]]), writes=[("ctmp", ci)])
                            bank, bk = self.next_ps()
                            S.op("pe", lambda h, bank=bank, ci=ci: h.transpose(out=bank[:, 0:128], in_=ctmp[ci][:].rearrange("p a n -> p (a n)"), identity=self.identf[:]),
                                 reads=[("ctmp", ci), "identf"], writes=[bk])
                            for j in range(2):
                                src = bank[64 * j:64 * j + 64, 0:128].rearrange("p (ml jj q) -> p ml jj q", ml=4, jj=2, q=16)[:, :, j, :]
                                self.evac(craw[64 * j:64 * j + 64, ri, d, 4 * i4:4 * i4 + 4, :], src, [bk], ["craw"], eng="dve")
                P = prm
                v = lambda idx: P[:, idx].rearrange("p d m -> p (d m)")
                S.op("act", lambda h: h.activation(out=v(DT), in_=v(DT), func=AF.Exp), reads=["prm"], writes=["prm"])
                S.op("dve", lambda h: h.tensor_tensor(out=v(X), in0=v(LR), in1=v(DT), op=ALU.mult), reads=["prm"], writes=["prm"])
                S.op("dve", lambda h: h.tensor_tensor(out=v(TH), in0=v(LI), in1=v(DT), op=ALU.mult), reads=["prm"], writes=["prm"])
                pw = self.sbuf(sb_, "pw", [128, 10, 288], F32)
                XE, THE, MC, MB, SN, CS, KI, PCR, PCI, PBR = range(10)
                pbi = self.sbuf(sb_, "pbi", [128, 288], F32)
                ki = self.sbuf(sb_, "ki", [128, 288], I32)
                w3 = lambda a: a.rearrange("p (dm t) -> p dm t", t=9)
                tau3 = tau[:].rearrange("p d m t -> p (d m) t")
                S.op("dve", lambda h: h.tensor_tensor(out=w3(pw[:, XE]), in0=bc(v(X), 2, 9), in1=tau3, op=ALU.mult), reads=["prm", "tau"], writes=["pw"])
                S.op("dve", lambda h: h.tensor_tensor(out=w3(pw[:, THE]), in0=bc(v(TH), 2, 9), in1=tau3, op=ALU.mult), reads=["prm", "tau"], writes=["pw"])
                S.op("act", lambda h: h.activation(out=pw[:, MC], in_=pw[:, XE], func=AF.Exp), reads=["pw"], writes=["pw"])
                S.op("act", lambda h: h.activation(out=pw[:, MB], in_=pw[:, XE], func=AF.Exp, scale=-1.0), reads=["pw"], writes=["pw"])

                def sin_of(dst, shift):
                    S.op("dve", lambda h: h.tensor_scalar(out=ki[:], in0=pw[:, THE], scalar1=shift, scalar2=1.0 / TWO_PI, op0=ALU.add, op1=ALU.mult), reads=["pw"], writes=["ki"])
                    S.op("dve", lambda h: h.tensor_copy(out=pw[:, KI], in_=ki[:]), reads=["ki"], writes=["pw"])
                    S.op("dve", lambda h: h.scalar_tensor_tensor(out=pw[:, KI], in0=pw[:, KI], scalar=-TWO_PI, in1=pw[:, THE], op0=ALU.mult, op1=ALU.add), reads=["pw"], writes=["pw"])
                    S.op("dve", lambda h: h.tensor_scalar(out=pw[:, KI], in0=pw[:, KI], scalar1=shift, scalar2=math.pi, op0=ALU.add, op1=ALU.min), reads=["pw"], writes=["pw"])
                    S.op("dve", lambda h: h.tensor_scalar(out=pw[:, KI], in0=pw[:, KI], scalar1=-math.pi, scalar2=None, op0=ALU.max), reads=["pw"], writes=["pw"])
                    S.op("act", lambda h: h.activation(out=dst, in_=pw[:, KI], func=AF.Sin), reads=["pw"], writes=["pw"])

                sin_of(pw[:, SN], 0.0)
                sin_of(pw[:, CS], math.pi / 2)
                S.op("dve", lambda h: h.tensor_tensor(out=pw[:, PCR], in0=pw[:, MC], in1=pw[:, CS], op=ALU.mult), reads=["pw"], writes=["pw"])
                S.op("dve", lambda h: h.tensor_tensor(out=pw[:, PCI], in0=pw[:, MC], in1=pw[:, SN], op=ALU.mult), reads=["pw"], writes=["pw"])
                S.op("dve", lambda h: h.tensor_tensor(out=pw[:, PBR], in0=pw[:, MB], in1=pw[:, CS], op=ALU.mult), reads=["pw"], writes=["pw"])
                S.op("dve", lambda h: h.scalar_tensor_tensor(out=pbi[:], in0=pw[:, MB], scalar=-1.0, in1=pw[:, SN], op0=ALU.mult, op1=ALU.mult), reads=["pw"], writes=["pbi"])
                pcr4 = pw[:, PCR].rearrange("p (d m t) -> p d m t", d=2, m=16)
                pci4 = pw[:, PCI].rearrange("p (d m t) -> p d m t", d=2, m=16)
                for d, ti in ((0, 1), (1, 6)):
                    S.op("dve", lambda h, d=d, ti=ti: h.tensor_copy(out=P[:, ABR, d, :], in_=pcr4[:, d, :, ti]), reads=["pw"], writes=["prm"])
                    S.op("dve", lambda h, d=d, ti=ti: h.tensor_copy(out=P[:, ABI, d, :], in_=pci4[:, d, :, ti]), reads=["pw"], writes=["prm"])
                for d in range(2):
                    S.op("dve", lambda h, d=d: h.tensor_copy(out=a8[:, d, :, 0, :], in_=bc(pcr4[:, d, :, 8], 2, 2)), reads=["pw"], writes=["a8"])
                    S.op("dve", lambda h, d=d: h.tensor_copy(out=a8[:, d, :, 1, 0], in_=pci4[:, d, :, 8]), reads=["pw"], writes=["a8"])
                    S.op("dve", lambda h, d=d: h.tensor_scalar(out=a8[:, d, :, 1, 1], in0=pci4[:, d, :, 8], scalar1=-1.0, scalar2=None, op0=ALU.mult), reads=["pw"], writes=["a8"])
                tt = lambda o, a, b, op: S.op("dve", lambda h: h.tensor_tensor(out=v(o), in0=v(a), in1=v(b), op=op), reads=["prm"], writes=["prm"])
                tt(DEN, LR, LR, ALU.mult)
                tt(TMP, LI, LI, ALU.mult)
                tt(DEN, DEN, TMP, ALU.add)
                S.op("dve", lambda h: h.reciprocal(out=v(DEN), in_=v(DEN)), reads=["prm"], writes=["prm"])
                S.op("dve", lambda h: h.tensor_scalar(out=v(NRE), in0=v(ABR), scalar1=-1.0, scalar2=None, op0=ALU.add), reads=["prm"], writes=["prm"])
                tt(FR, NRE, LR, ALU.mult)
                tt(TMP, ABI, LI, ALU.mult)
                tt(FR, FR, TMP, ALU.add)
                tt(FR, FR, DEN, ALU.mult)
                tt(FI, ABI, LR, ALU.mult)
                tt(TMP, NRE, LI, ALU.mult)
                tt(FI, FI, TMP, ALU.subtract)
                tt(FI, FI, DEN, ALU.mult)
                bb = self.sbuf(sb_, "bb", [128, 2, 32, 16], F32)
                m1 = self.sbuf(sb_, "m1", [128, 4096], F32)
                m2 = self.sbuf(sb_, "m2", [128, 4096], F32)
                b3 = lambda ri: braw[:, ri].rearrange("p d m q -> p (d m) q")
                c3 = lambda ri: craw[:, ri].rearrange("p d m q -> p (d m) q")
                m13 = m1[:, 0:512].rearrange("p (a q) -> p a q", q=16)
                m23 = m2[:, 0:512].rearrange("p (a q) -> p a q", q=16)
                fr3, fi3 = bc(v(FR), 2, 16), bc(v(FI), 2, 16)
                S.op("dve", lambda h: h.tensor_tensor(out=m13, in0=fr3, in1=b3(0), op=ALU.mult), reads=["prm", "braw"], writes=["m1"])
                S.op("pool", lambda h: h.tensor_tensor(out=m23, in0=fi3, in1=b3(1), op=ALU.mult), reads=["prm", "braw"], writes=["m2"])
                S.op("dve", lambda h: h.tensor_tensor(out=bb[:, 0], in0=m13, in1=m23, op=ALU.subtract), reads=["m1", "m2"], writes=["bb"])
                S.op("dve", lambda h: h.tensor_tensor(out=m13, in0=fr3, in1=b3(1), op=ALU.mult), reads=["prm", "braw", "bb"], writes=["m1"])
                S.op("pool", lambda h: h.tensor_tensor(out=m23, in0=fi3, in1=b3(0), op=ALU.mult), reads=["prm", "braw", "bb"], writes=["m2"])
                S.op("dve", lambda h: h.tensor_tensor(out=bb[:, 1], in0=m13, in1=m23, op=ALU.add), reads=["m1", "m2"], writes=["bb"])
                pw3 = lambda idx: w3(pw[:, idx])[:, :, 0:8]
                pbi3 = w3(pbi[:])[:, :, 0:8]
                m14 = m1[:].rearrange("p (a s q) -> p a s q", s=8, q=16)
                m24 = m2[:].rearrange("p (a s q) -> p a s q", s=8, q=16)

                def cplx(dst, outkey, ar, ai, br, bi, neg_im):
                    A_r, A_i = bc(ar, 3, 16), bc(ai, 3, 16)
                    B_r, B_i = bc(br, 2, 8), bc(bi, 2, 8)
                    o_re = dst[:, :, :, 0, :].rearrange("p d m (s q) -> p (d m) s q", s=8)
                    o_im = dst[:, :, :, 1, :].rearrange("p d m (s q) -> p (d m) s q", s=8)
                    S.op("dve", lambda h: h.tensor_tensor(out=m14, in0=A_r, in1=B_r, op=ALU.mult), reads=["pw", "pbi", "bb", "craw", outkey], writes=["m1"])
                    S.op("pool", lambda h: h.tensor_tensor(out=m24, in0=A_i, in1=B_i, op=ALU.mult), reads=["pw", "pbi", "bb", "craw", outkey], writes=["m2"])
                    S.op("dve", lambda h: h.tensor_tensor(out=o_re, in0=m14, in1=m24, op=ALU.subtract), reads=["m1", "m2"], writes=[outkey])
                    S.op("dve", lambda h: h.tensor_tensor(out=m14, in0=A_r, in1=B_i, op=ALU.mult), reads=["pw", "pbi", "bb", "craw", outkey], writes=["m1"])
                    S.op("pool", lambda h: h.tensor_tensor(out=m24, in0=A_i, in1=B_r, op=ALU.mult), reads=["pw", "pbi", "bb", "craw", outkey], writes=["m2"])
                    if neg_im:
                        S.op("dve", lambda h: h.scalar_tensor_tensor(out=o_im, in0=m14, scalar=-1.0, in1=m24, op0=ALU.mult, op1=ALU.subtract), reads=["m1", "m2"], writes=[outkey])
                    else:
                        S.op("dve", lambda h: h.tensor_tensor(out=o_im, in0=m14, in1=m24, op=ALU.add), reads=["m1", "m2"], writes=[outkey])

                cplx(Bp, "Bp", pw3(PBR), pbi3, bb[:, 0], bb[:, 1], False)
                cplx(Cp, "Cp", pw3(PCR), pw3(PCI), c3(0), c3(1), True)
                if l == 0 and "BpCp" in self.tap_out:
                    S.op("dve", lambda h: h.tensor_copy(out=m1[:], in_=Bp[:].rearrange("p d m c x -> p (d m c x)")[:, 0:4096]), reads=["Bp", "m1"], writes=["m1"])
                    S.op("dve", lambda h: h.tensor_copy(out=m2[:], in_=Cp[:].rearrange("p d m c x -> p (d m c x)")[:, 0:4096]), reads=["Cp", "m2"], writes=["m2"])
                    S.dma("sp", self.tap_out["BpCp"][:, 0:4096], m1[:], reads=["m1"])
                    S.dma("sp", self.tap_out["BpCp"][:, 4096:8192], m2[:], reads=["m2"])
                    S.barrier()
                tk = [self.sbuf(sb_, "tk%d" % i, [128, 256], F32) for i in range(2)]
                t2k = [self.sbuf(sb_, "t2k%d" % i, [128, 128], F32) for i in range(2)]
                for g in range(32):
                    m, j = g // 2, g % 2
                    r0 = 64 * j
                    i = g % 2
                    bank, bk = self.next_ps()
                    for d in range(2):
                        for c2 in range(2):
                            self.mm(bank[:, d * 128:(d + 1) * 128], Bp[r0:r0 + 64, d, m, c2, :], Cp[r0:r0 + 64, d, m, c2, :], c2 == 0, c2 == 1, ["Bp", "Cp"], [bk])
                    S.op("dve", lambda h, bank=bank, i=i: h.tensor_tensor(out=tk[i][:], in0=bank[:, 0:256], in1=mfb[:], op=ALU.mult), reads=[bk, "mfb"], writes=[("tk", i)])
                    S.op("pool", lambda h, i=i: h.tensor_tensor(out=t2k[i][:], in0=tk[i][:, 0:128], in1=tk[i][:, 128:256], op=ALU.add), reads=[("tk", i)], writes=[("t2k", i)])
                    S.op("dve", lambda h, i=i, g=g: h.scalar_tensor_tensor(out=Kb[:, g, :], in0=self.identf[:], scalar=dcol[:, g:g + 1], in1=t2k[i][:], op0=ALU.mult, op1=ALU.add),
                         reads=[("t2k", i), "dcol", "identf"], writes=["Kb"])
                for d in range(2):
                    for mm4 in range(4):
                        bank, bk = self.next_pt()
                        for mi in range(4):
                            m = 4 * mm4 + mi
                            for c2 in range(2):
                                o0 = (mi * 2 + c2) * 128
                                self.tr(bank[:, o0:o0 + 128], Bp[:, d, m, c2, :], self.ident[:, :], ["Bp"], [bk])
                        self.evac(BpT[:, d, 4 * mm4:4 * mm4 + 4].rearrange("p m c x -> p (m c x)"), bank[:, :], [bk], ["BpT"])
                S.barrier()
            if self.stop_after == "ssmB":
                return
            with ExitStack() as sa:
                ucg = [self.sbuf(sa, "ucg%d" % i, [128, 4096], BF16) for i in range(3)]
                for ct in range(3):
                    n = 128 if ct < 2 else 16
                    engs = ("dve", "pool", "act")
                    src = ucm[ct][0:n, :].rearrange("c (s g q) -> c g s q", s=8, g=32, q=16)
                    dstv = ucg[ct][0:n, :].rearrange("c (g s q) -> c g s q", s=8, g=32, q=16)
                    for g4 in range(4):
                        self.evac(dstv[:, 8 * g4:8 * g4 + 8], src[:, 8 * g4:8 * g4 + 8], [("ucm", ct)], [("ucg", ct)], eng=engs[(ct + g4) % 3])
                for ct in range(2):
                    uv = ucg[ct][:, :].rearrange("c (g x) -> c g x", g=32)
                    for gg in range(4):
                        bank, bk = self.next_pt()
                        for gi in range(8):
                            self.tr(bank[:, gi * 128:(gi + 1) * 128], uv[:, 8 * gg + gi], self.ident[:, :], [("ucg", ct)], [bk])
                        self.evac(UT[:, 8 * gg:8 * gg + 8, ct * 128:(ct + 1) * 128], bank[:, :].rearrange("p (g c) -> p g c", g=8), [bk], ["UT"])
                uv = ucg[2][0:16, :].rearrange("c (g x) -> c g x", g=32)
                bank, bk = self.next_pt()
                for g in range(32):
                    self.tr(bank[:, g * 16:(g + 1) * 16], uv[:, g], self.ident[0:16, 0:16], [("ucg", 2)], [bk])
                self.evac(UT[:, :, 256:272], bank[:, 0:512].rearrange("p (g c) -> p g c", g=32), [bk], ["UT"])
                S.barrier()
            if self.stop_after == "ssmA":
                return
            su.close()
            with ExitStack() as sc:
                W_ = NCH + 2
                Rbf = self.sbuf(sc, "Rbf", [128, 2, 16, 2, W_], BF16)
                with ExitStack() as sc2:
                    RZ = self.sbuf(sc2, "RZ", [128, 2, 16, 2, W_], F32)
                    Wst = self.sbuf(sc2, "Wst", [128, 32, 2], F32)
                    tT = self.sbuf(sc2, "tT", [128, 32, 2], F32)
                    pP = self.sbuf(sc2, "pP", [128, 32, 2, 2], F32)
                    plane = 16 * 2 * W_

                    def rz_cols(colf, colb):
                        a = RZ[:, :, :, :, colf]
                        return bass.AP(tensor=a.tensor, offset=a.offset, ap=[list(a.ap[0]), [plane + colb - colf, 2], list(a.ap[2]), list(a.ap[3])])

                    S.op("pool", lambda h: h.memset(RZ[:, 0, :, :, 0:1], 0.0), writes=["RZ"])
                    S.op("pool", lambda h: h.memset(RZ[:, 1, :, :, W_ - 1:W_], 0.0), writes=["RZ"])
                    S.op("pool", lambda h: h.memset(Wst[:], 0.0), writes=["W"])
                    for d in range(2):
                        for m in range(16):
                            bre, kre = self.next_ps()
                            bim, kim = self.next_ps()
                            for j in range(2):
                                g = 2 * m + j
                                self.mm(bre[64 * j:64 * j + 64, 0:NCH], BpT[:, d, m, 0, 64 * j:64 * j + 64], UT[:, g, :], True, True, ["BpT", "UT"], [kre])
                                self.mm(bim[64 * j:64 * j + 64, 0:NCH], BpT[:, d, m, 1, 64 * j:64 * j + 64], UT[:, g, :], True, True, ["BpT", "UT"], [kim])
                            self.evac(RZ[:, d, m, 0, 1:1 + NCH], bre[:, 0:NCH], [kre], ["RZ"])
                            self.evac(RZ[:, d, m, 1, 1:1 + NCH], bim[:, 0:NCH], [kim], ["RZ"])
                    coef = a8[:].rearrange("p d m u c -> p (d m) u c")
                    p0 = pP[:, :, 0, :]
                    p1 = pP[:, :, 1, :]
                    p1r = bass.AP(tensor=p1.tensor, offset=p1.offset + 1, ap=[list(p1.ap[0]), list(p1.ap[1]), [-1, 2]])
                    wst3 = Wst[:].rearrange("p (d m) c -> p d m c", d=2)
                    tt3 = tT[:].rearrange("p (d m) c -> p d m c", d=2)
                    prev_out = None
                    for k in range(NCH - 1):
                        zc = rz_cols(k + 1, NCH - k)
                        S.op("dve", lambda h, zc=zc: h.tensor_tensor(out=tt3, in0=wst3, in1=zc, op=ALU.add), reads=["W", "RZ"], writes=["T"])
                        if prev_out is not None:
                            S.op("dve", lambda h, po=prev_out: h.tensor_copy(out=po, in_=wst3), reads=["W"], writes=["RZ"])
                        S.op("dve", lambda h: h.tensor_tensor(out=pP[:], in0=bc(tT[:], 2, 2), in1=coef, op=ALU.mult), reads=["T", "a8"], writes=["P"])
                        S.op("dve", lambda h: h.tensor_tensor(out=Wst[:], in0=p0, in1=p1r, op=ALU.add), reads=["P"], writes=["W"])
                        prev_out = zc
                    S.op("dve", lambda h, po=prev_out: h.tensor_copy(out=po, in_=wst3), reads=["W"], writes=["RZ"])
                    cengs = ("act", "dve", "pool", "act")
                    for d in range(2):
                        for mh in range(2):
                            self.evac(Rbf[:, d, 8 * mh:8 * mh + 8].rearrange("p m c x -> p (m c x)"), RZ[:, d, 8 * mh:8 * mh + 8].rearrange("p m c x -> p (m c x)"),
                                      ["RZ"], ["Rbf"], eng=cengs[2 * d + mh])
                    S.barrier()
                if self.stop_after == "ssmC" and "Rbf" not in self.tap_out:
                    return
                if l == 0 and "Rbf" in self.tap_out:
                    with ExitStack() as sc3:
                        tmp = self.sbuf(sc3, "tmp", [128, 2 * 16 * 2 * (NCH + 2)], F32)
                        S.op("dve", lambda h: h.tensor_copy(out=tmp[:], in_=Rbf[:].rearrange("p d m c x -> p (d m c x)")), reads=["Rbf"], writes=["tmp"])
                        self.tap("Rbf", tmp[:], ["tmp"])
                        S.barrier()
                    if self.stop_after == "ssmC":
                        return
                with ExitStack() as sd:
                    zcm = [self.sbuf(sd, "zcm%d" % i, [128, 4096], BF16) for i in range(3)]
                    for ct in range(3):
                        n = 128 if ct < 2 else 16
                        c0 = ct * 128
                        for g4 in range(8):
                            bank, bk = self.next_ps()
                            for gi in range(4):
                                g = 4 * g4 + gi
                                m, j = g // 2, g % 2
                                r0 = 64 * j
                                o_ap = bank[0:n, gi * 128:(gi + 1) * 128]
                                self.mm(o_ap, UT[:, g, c0:c0 + n], Kb[:, g, :], True, False, ["Kb", "UT"], [bk])
                                for d in range(2):
                                    off = 0 if d == 0 else 2
                                    for c2 in range(2):
                                        self.mm(o_ap, Rbf[r0:r0 + 64, d, m, c2, off + c0:off + c0 + n], Cp[r0:r0 + 64, d, m, c2, :], False, (d == 1 and c2 == 1), ["Cp", "Rbf"], [bk])
                            dst = zcm[ct][0:n, :].rearrange("c (t g p) -> c g t p", t=8, g=32, p=16)[:, 4 * g4:4 * g4 + 4]
                            src = bank[0:n, :].rearrange("c (g t p) -> c g t p", g=4, t=8, p=16)
                            S.op("act", lambda h, dst=dst, src=src: h.activation(out=dst, in_=src, func=AF.Gelu), reads=[bk], writes=[("zcm", ct)])
                    for ct in range(3):
                        if self.stop_after in ("ssmD1", "ssmD2"):
                            continue
                        n = 128 if ct < 2 else 16
                        S.dma("sp", self.z_d.rearrange("(c s) x -> c (s x)", s=8)[ct * 128:ct * 128 + n, :], zcm[ct][0:n, :], reads=[("zcm", ct)], writes=["zd"])
                    S.barrier()

    def phase_ssm_glu(self, l, yTs):
        S = self.S
        with ExitStack() as st:
            zT = self.sbuf(st, "zT", [128, 4, LP], BF16)
            ztm = [self.sbuf(st, "ztm%d" % i, [128, 512], BF16) for i in range(2)]
            wgl = self.sbuf(st, "wgl", [128, 4, 512], BF16)
            glb = self.sbuf(st, "glb", [128, 4], F32)
            sgs = [self.sbuf(st, "sgs%d" % i, [128, 512], BF16) for i in range(2)]
            self.load_w(wgl[:], "ssm_glu_w", l * 512 * 512, 0, 4, 512, 0, 512, ["wgl"])
            S.dma("sp", glb[:], AP(self.T["ssm_glu_b"], l * 512, [[1, 128], [128, 4]]), writes=["glb"], allow_slow_non_contiguous=True)
            for b in range(NT):
                i = b % 2
                S.dma("sp", ztm[i][:], self.z_d[b * 128:(b + 1) * 128, :], reads=["zd"], writes=[("ztm", i)])
                bank, bk = self.next_pt()
                for j in range(4):
                    self.tr(bank[:, j * 128:(j + 1) * 128], ztm[i][:, j * 128:(j + 1) * 128], self.ident[:, :], [("ztm", i)], [bk])
                self.evac(zT[:, :, b * 128:(b + 1) * 128], bank[:, 0:512].rearrange("p (j t) -> p j t", j=4), [bk], ["zT"])
            cnt = 0
            for co in range(4):
                for (t0, n) in TCH:
                    bank, bk = self.next_ps()
                    for k in range(4):
                        self.mm(bank[:, 0:n], wgl[:, k, co * 128:(co + 1) * 128], zT[:, k, t0:t0 + n], k == 0, k == 3, ["wgl", "zT"], [bk])
                    i = cnt % 2
                    cnt += 1
                    S.op("act", lambda h, bank=bank, n=n, i=i, co=co: h.activation(out=sgs[i][:, 0:n], in_=bank[:, 0:n], func=AF.Sigmoid, bias=glb[:, co:co + 1]),
                         reads=[bk, "glb"], writes=[("sgs", i)])
                    S.op(self.ew(), lambda h, n=n, i=i, co=co, t0=t0: h.tensor_tensor(out=yTs[:, co, t0:t0 + n], in0=zT[:, co, t0:t0 + n], in1=sgs[i][:, 0:n], op=ALU.mult),
                         reads=[("sgs", i), "zT"], writes=["yT_all"])
            S.barrier()
        self.tap_yT("yTs", yTs, l)


def build_nc(nlayers=NL, taps=(), stop_after=None):
    return K(nlayers=nlayers, taps=taps, stop_after=stop_after).build()


def make_in_maps(inputs):
    consts = make_consts()
    shared = {k: np.ascontiguousarray(np.asarray(v), dtype=np.float32) for k, v in inputs.items() if k != "x"}
    x = np.asarray(inputs["x"], dtype=np.float32)
    maps = []
    for c in range(8):
        m = dict(shared)
        m["x"] = np.ascontiguousarray(x[c])
        m.update(consts)
        maps.append(m)
    return maps


def kernel(**inputs):
    nc = build_nc()
    res = run_bass_kernel_spmd(nc, make_in_maps(inputs), core_ids=list(range(8)))
    return np.stack([np.asarray(r["out"], dtype=np.float32) for r in res.results], axis=0)
```

```python
import math
from contextlib import ExitStack

import numpy as np
import concourse.bass as bass
import concourse.mybir as mybir
from concourse.bass_utils import run_bass_kernel_spmd
from concourse.alu_op_type import AluOpType as ALU

F32 = mybir.dt.float32
BF16 = mybir.dt.bfloat16
I32 = mybir.dt.int32
AF = mybir.ActivationFunctionType

D = 1024
SEQ = 2048
NMETA = 16
LP = 2176
NT = 17
FRONT = 112
DIN = 4864
DFF = 4096
NL = 2
EPS = 1e-6
OFF_K, OFF_V, OFF_SSM, OFF_POOL, OFF_GATE = 512, 640, 768, 1280, 1792
NCH = 272
TCH = [(0, 512), (512, 512), (1024, 512), (1536, 512), (2048, 128)]


class Sched:
    ENG = ("pe", "act", "dve", "pool", "sp")

    def __init__(self, nc, es, n_dma_sems=28):
        self.nc = nc
        self.lists = {e: [] for e in self.ENG}
        self.sem = {e: es.enter_context(nc.semaphore("s_" + e)) for e in self.ENG}
        self.count = {e: 0 for e in self.ENG}
        self.waited = {e: {} for e in self.ENG}
        self.dma_sems = [es.enter_context(nc.semaphore("s_dma%d" % i)) for i in range(n_dma_sems)]
        self.dma_tot = [0] * n_dma_sems
        self.dma_rr = 0
        self.dma_rr_sw = 0
        self.res = {}
        self.ninstr = 0

    def _semobj(self, k):
        return self.sem[k] if isinstance(k, str) else self.dma_sems[k[1]]

    def _wait(self, eng, tok, raw=False):
        k, v = tok
        if k == eng and (not raw or eng == "pe"):
            return
        cur = self.waited[eng].get(k, 0)
        if cur >= v:
            return
        self.waited[eng][k] = v
        so = self._semobj(k)
        self.lists[eng].append(lambda h, so=so, v=v: h.wait_ge(so, v))

    def _deps(self, eng, reads, writes):
        for r in reads:
            st = self.res.get(r)
            if st and st[0] is not None:
                self._wait(eng, st[0], raw=True)
        for w in writes:
            st = self.res.get(w)
            if st:
                if st[0] is not None:
                    self._wait(eng, st[0])
                for k, v in st[1].items():
                    self._wait(eng, (k, v))

    def _commit(self, tok, reads, writes):
        k, v = tok
        for r in reads:
            st = self.res.setdefault(r, [None, {}])
            if st[1].get(k, 0) < v:
                st[1][k] = v
        for w in writes:
            self.res[w] = [tok, {}]

    def op(self, eng, fn, reads=(), writes=()):
        self._deps(eng, reads, writes)
        self.count[eng] += 1
        v = self.count[eng]
        so = self.sem[eng]
        self.lists[eng].append(lambda h, fn=fn, so=so: fn(h).then_inc(so, 1))
        self._commit((eng, v), reads, writes)
        self.ninstr += 1

    def dma(self, eng, out, in_, reads=(), writes=(), **kw):
        self._deps(eng, reads, writes)
        if eng == "pool":
            i = 16 + self.dma_rr_sw
            self.dma_rr_sw = (self.dma_rr_sw + 1) % (len(self.dma_sems) - 16)
        else:
            i = self.dma_rr
            self.dma_rr = (self.dma_rr + 1) % 16
        if self.dma_tot[i] > 0:
            k = ("d", i)
            cur = self.waited[eng].get(k, 0)
            if cur < self.dma_tot[i]:
                self.waited[eng][k] = self.dma_tot[i]
                so0, v0 = self.dma_sems[i], self.dma_tot[i]
                self.lists[eng].append(lambda h, so0=so0, v0=v0: h.wait_ge(so0, v0))
        self.dma_tot[i] += 16
        v = self.dma_tot[i]
        so = self.dma_sems[i]
        self.lists[eng].append(
            lambda h, so=so, out=out, in_=in_, kw=kw: h.dma_start(out=out, in_=in_, **kw).then_inc(so, 16))
        self._commit((("d", i), v), reads, writes)
        self.ninstr += 1

    def barrier(self):
        for e in self.ENG:
            for f in self.ENG:
                if f != e and self.count[f] > 0:
                    self._wait(e, (f, self.count[f]))
            for i, t in enumerate(self.dma_tot):
                if t > 0:
                    self._wait(e, (("d", i), t))
        self.res = {}

    def final_wait(self, eng="sp"):
        for f in self.ENG:
            if f != eng and self.count[f] > 0:
                self._wait(eng, (f, self.count[f]))
        for i, t in enumerate(self.dma_tot):
            if t > 0:
                self._wait(eng, (("d", i), t))

    def emit(self, block):
        L = self.lists

        @block.sync
        def _(h):
            for f in L["sp"]:
                f(h)

        @block.scalar
        def _(h):
            for f in L["act"]:
                f(h)

        @block.vector
        def _(h):
            for f in L["dve"]:
                f(h)

        @block.gpsimd
        def _(h):
            for f in L["pool"]:
                f(h)

        @block.tensor
        def _(h):
            for f in L["pe"]:
                f(h)


def make_consts():
    c = {}
    c["c_ident"] = np.eye(128, dtype=np.float32)
    t = np.arange(LP, dtype=np.float32) - FRONT
    r = np.arange(128)
    j = r % 64
    invf = (10000.0 ** (-(np.arange(32, dtype=np.float32)) * 2.0 / 64)).astype(np.float32)
    ang = (t[None, :] * invf[j % 32][:, None]).astype(np.float32)
    c["c_cos"] = np.cos(ang).astype(np.float32)
    sgn = np.where(j < 32, -1.0, 1.0).astype(np.float32)
    c["c_sin"] = (np.sin(ang) * sgn[:, None]).astype(np.float32)
    kl = np.arange(128)[:, None]
    ql = np.arange(128)[None, :]
    mp = np.where(kl >= ql, 0.0, -30000.0).astype(np.float32)
    mn = np.where(kl <= ql, 0.0, -30000.0).astype(np.float32)
    c["c_maskp"] = np.tile(mp, (1, 4))
    c["c_maskn"] = np.tile(mn, (1, 4))
    s = np.arange(128)[:, None] // 16
    tt = np.arange(128)[None, :] // 16
    c["c_maskfb"] = np.concatenate([(tt >= s), (tt <= s)], axis=1).astype(np.float32)
    tau = np.zeros((128, 2, 16, 9), np.float32)
    tau[:, 0, :, 0:8] = np.arange(8, dtype=np.float32)[None, None, :]
    tau[:, 1, :, 0:8] = (7 - np.arange(8, dtype=np.float32))[None, None, :]
    tau[:, :, :, 8] = 8.0
    c["c_tau"] = tau.reshape(128, 288)
    rc = np.zeros((4, LP), np.float32)
    L = NMETA + SEQ
    idx = np.arange(L)
    for gi, w in enumerate((2, 4, 8, 16)):
        lo = np.clip(idx - w // 2, 0, L)
        hi = np.clip(idx + w // 2, 0, L)
        rc[gi, FRONT:] = 1.0 / (hi - lo).astype(np.float32)
    c["c_rcnt"] = rc
    return c


CONST_SHAPES = {"c_ident": [128, 128], "c_cos": [128, LP], "c_sin": [128, LP], "c_maskp": [128, 512],
                "c_maskn": [128, 512], "c_maskfb": [128, 256], "c_tau": [128, 288], "c_rcnt": [4, LP]}

IN_SHAPES = {
    "x": [SEQ, D], "meta_tokens": [NMETA, D], "norm_mix": [NL, D], "w_in": [NL, D, DIN], "attn_sink": [NL, 8],
    "ssm_lam_re": [NL, 2, 32, 64], "ssm_lam_im": [NL, 2, 32, 64], "ssm_log_dt": [NL, 2, 32],
    "ssm_b_re": [NL, 2, 32, 64, 16], "ssm_b_im": [NL, 2, 32, 64, 16], "ssm_c_re": [NL, 2, 32, 16, 64],
    "ssm_c_im": [NL, 2, 32, 16, 64], "ssm_d": [NL, 512], "ssm_glu_w": [NL, 512, 512], "ssm_glu_b": [NL, 512],
    "pool_w": [NL, 4, 128, 128], "pool_scale": [NL, 512], "w_branch": [NL, 3, 512, D], "w_out": [NL, D, D],
    "norm_mlp": [NL, D], "w_up": [NL, D, DFF], "w_down": [NL, DFF, D], "norm_final": [D],
}


def AP(t, offset, ap):
    return bass.AP(tensor=t, offset=offset, ap=[list(a) for a in ap])


def bc(ap, axis, n):
    a = ap.unsqueeze(axis)
    shp = list(a.shape)
    shp[axis] = n
    return a.to_broadcast(shp)


class K:
    def __init__(self, nlayers=NL, taps=(), stop_after=None):
        self.nlayers = nlayers
        self.stop_after = stop_after
        nc = self.nc = bass.Bass("TRN2", target_bir_lowering=False)
        self.T = {}
        for k, shp in IN_SHAPES.items():
            self.T[k] = nc.dram_tensor(k, shp, F32, kind="ExternalInput")
        for k, shp in CONST_SHAPES.items():
            self.T[k] = nc.dram_tensor(k, shp, F32, kind="ExternalInput")
        self.A = {k: v.ap() for k, v in self.T.items()}
        self.out = nc.dram_tensor("out", [SEQ, D], F32, kind="ExternalOutput").ap()
        self.h_d = nc.dram_tensor("h_scr", [LP, D], F32, kind="Internal").ap()
        self.u_t = nc.dram_tensor("u_scr", [LP, 512], BF16, kind="Internal")
        self.u_d = self.u_t.ap()
        self.z_t = nc.dram_tensor("z_scr", [LP, 512], BF16, kind="Internal")
        self.z_d = self.z_t.ap()
        self.tap_out = {}
        for name, shp in taps:
            self.tap_out[name] = nc.dram_tensor("tap_" + name, shp, F32, kind="ExternalOutput").ap()
        self.uid = 0
        self.rr = {"ps": 0, "pt": 0, "ev": 0, "ew": 0}

    def sbuf(self, st, name, shape, dt):
        self.uid += 1
        return st.enter_context(self.nc.sbuf_tensor("%s_%d" % (name, self.uid), shape, dt))

    def next_ps(self):
        i = self.rr["ps"]
        self.rr["ps"] = (i + 1) % len(self.ps)
        return self.ps[i], ("ps", i)

    def next_pt(self):
        i = self.rr["pt"]
        self.rr["pt"] = (i + 1) % len(self.pt)
        return self.pt[i], ("pt", i)

    def evac(self, out_ap, in_ap, reads, writes, eng=None):
        S = self.S
        if eng is None:
            self.rr["ev"] ^= 1
            eng = "act" if self.rr["ev"] else "dve"
        if eng == "act":
            S.op("act", lambda h: h.activation(out=out_ap, in_=in_ap, func=AF.Copy), reads=reads, writes=writes)
        else:
            S.op(eng, lambda h: h.tensor_copy(out=out_ap, in_=in_ap), reads=reads, writes=writes)

    def ew(self):
        self.rr["ew"] ^= 1
        return "dve" if self.rr["ew"] else "pool"

    def tap(self, name, src_ap, reads):
        if name in self.tap_out:
            self.S.dma("sp", self.tap_out[name], src_ap, reads=reads)

    def load_w(self, dst_ap, name, base, rows0, nk, ld, c0, ncols, writes, eng="pool"):
        src = AP(self.T[name], base + rows0 * ld + c0, [[ld, 128], [128 * ld, nk], [1, ncols]])
        self.S.dma(eng, dst_ap, src, writes=writes)

    def mm(self, out_ap, lhsT, rhs, start, stop, reads, writes):
        self.S.op("pe", lambda h: h.matmul(out_ap, lhsT=lhsT, rhs=rhs, start=start, stop=stop), reads=reads, writes=writes)

    def tr(self, out_ap, in_ap, idn, reads, writes):
        self.S.op("pe", lambda h: h.transpose(out=out_ap, in_=in_ap, identity=idn), reads=list(reads) + ["ident"], writes=writes)

    def build(self):
        nc = self.nc
        with ExitStack() as es:
            S = self.S = Sched(nc, es)
            self.ps = [es.enter_context(nc.psum_tensor("ps%d" % i, [128, 512], F32)) for i in range(6)]
            self.pt = [es.enter_context(nc.psum_tensor("pt%d" % i, [128, 1024], BF16)) for i in range(2)]
            self.identf = self.sbuf(es, "identf", [128, 128], F32)
            self.ident = self.sbuf(es, "ident", [128, 128], BF16)
            self.small = self.sbuf(es, "small", [128, 16], F32)
            self.zt = self.sbuf(es, "zt", [FRONT, 256], F32)
            S.dma("sp", self.identf[:], self.A["c_ident"], writes=["identf"])
            S.op("dve", lambda h: h.tensor_copy(out=self.ident[:], in_=self.identf[:]), reads=["identf"], writes=["ident"])
            S.op("pool", lambda h: h.memset(self.small[:, 0:1], EPS), writes=["eps"])
            self.init_h()
            for l in range(self.nlayers):
                self.layer(l)
            S.final_wait("sp")
            with nc.Block() as block:
                S.emit(block)
        return nc

    def init_h(self):
        S = self.S
        zt = self.zt
        S.op("pool", lambda h: h.memset(zt[:], 0.0), writes=["zt"])
        for c in range(4):
            S.dma("sp", self.h_d[0:FRONT, c * 256:(c + 1) * 256], zt[:], reads=["zt"], writes=[("h", 0)])
        S.dma("sp", self.h_d[FRONT:128, :], self.A["meta_tokens"], writes=[("hm", 0)])
        for b in range(1, NT):
            S.dma("sp", self.h_d[b * 128:(b + 1) * 128, :], self.A["x"][(b - 1) * 128:b * 128, :], writes=[("h", b)])

    def norm_transpose(self, st, gname, goff, hnT, src_tiles):
        S = self.S
        small = self.small
        gain = self.sbuf(st, "gain", [128, D], F32)
        S.dma("sp", gain[:], AP(self.T[gname], goff, [[0, 128], [1, D]]), writes=["gain"])
        junk = self.sbuf(st, "junk", [128, D], BF16)
        hnb = [self.sbuf(st, "hnb%d" % i, [128, D], BF16) for i in range(2)]
        ss = self.sbuf(st, "ss", [128, 3 * NT], F32)
        for b in range(NT):
            src, rd = src_tiles(b)
            S.op("act", lambda h, src=src, b=b: h.activation(out=junk[:], in_=src, func=AF.Square, accum_out=ss[:, b:b + 1]),
                 reads=rd, writes=["junk", ("ss", b)])
            S.op("act", lambda h, b=b: h.activation(out=ss[:, NT + b:NT + b + 1], in_=ss[:, b:b + 1], func=AF.Sqrt,
                                                    scale=1.0 / D, bias=small[:, 0:1]),
                 reads=[("ss", b), "eps"], writes=[("ss1", b)])
            S.op("dve", lambda h, b=b: h.reciprocal(out=ss[:, 2 * NT + b:2 * NT + b + 1], in_=ss[:, NT + b:NT + b + 1]),
                 reads=[("ss1", b)], writes=[("ss2", b)])
            i = b % 2
            S.op("dve", lambda h, src=src, b=b, i=i: h.scalar_tensor_tensor(
                out=hnb[i][:], in0=src, scalar=ss[:, 2 * NT + b:2 * NT + b + 1], in1=gain[:], op0=ALU.mult, op1=ALU.mult),
                reads=list(rd) + [("ss2", b), "gain"], writes=[("hnb", i)])
            bank, bk = self.next_pt()
            for k in range(8):
                self.tr(bank[:, k * 128:(k + 1) * 128], hnb[i][:, k * 128:(k + 1) * 128], self.ident[:, :], [("hnb", i)], [bk])
            self.evac(hnT[:, :, b * 128:(b + 1) * 128], bank[:, :].rearrange("p (k t) -> p k t", k=8), [bk], [("hnT", b)])

    def layer(self, l):
        S = self.S
        with ExitStack() as lst:
            hnT = self.sbuf(lst, "hnT", [128, 8, LP], BF16)
            if l == 0 and "hnT" in self.tap_out:
                with ExitStack() as st:
                    tmp = self.sbuf(st, "tmp", [128, 8 * LP], F32)
                    S.op("dve", lambda h: h.tensor_copy(out=tmp[:], in_=hnT[:].rearrange("p k t -> p (k t)")), reads=[("hnT", b) for b in range(NT)], writes=["tmp"])
                    self.tap("hnT", tmp[:], ["tmp"])
                    S.barrier()
            if self.stop_after == "n1":
                return
            self.phase_ssm_core(l, hnT)
            if self.stop_after in ("ssmA", "ssmB", "ssmC", "ssmD", "ssmD1", "ssmD2"):
                return
            yT = [self.sbuf(lst, "yT%d" % c, [128, 4, LP], BF16) for c in (1,)]
            self.phase_ssm_glu(l, yT[0])
            if self.stop_after == "ssm":
                return
            yTa = self.sbuf(lst, "yTa", [128, 4, LP], BF16)
            self.phase_attn(l, hnT, yTa)
            if self.stop_after == "attn":
                return
            yTp = self.sbuf(lst, "yTp", [128, 4, LP], BF16)
            self.phase_pool(l, hnT, yTp)
            if self.stop_after == "pool":
                return
            self.phase_merge(l, hnT, [yTa, yT[0], yTp])
        if self.stop_after == "merge":
            return
        self.phase_ffn(l)

    def tap_yT(self, name, yT, l):
        S = self.S
        if l == 0 and name in self.tap_out:
            with ExitStack() as st:
                tmp = self.sbuf(st, "tmp", [128, 4 * LP], F32)
                S.op("dve", lambda h: h.tensor_copy(out=tmp[:], in_=yT[:].rearrange("p k t -> p (k t)")), reads=["yT_all"], writes=["tmp"])
                self.tap(name, tmp[:], ["tmp"])
                S.barrier()

    def phase_pool(self, l, hnT, yTp):
        S = self.S
        WX, OFF = LP + 64, 32
        with ExitStack() as st:
            wp = [self.sbuf(st, "wp%d" % i, [128, 8, 128], BF16) for i in range(2)]
            pw = [self.sbuf(st, "pw%d" % i, [128, 128], BF16) for i in range(2)]
            ua = self.sbuf(st, "ua", [128, WX], F32)
            pb = [self.sbuf(st, "pb%d" % i, [128, WX], F32) for i in range(2)]
            rcb = self.sbuf(st, "rcb", [128, LP], F32)
            dd = self.sbuf(st, "dd", [128, LP], BF16)
            psc = self.sbuf(st, "psc", [128, 4], F32)
            S.dma("sp", psc[:], AP(self.T["pool_scale"], l * 512, [[1, 128], [128, 4]]), writes=["psc"], allow_slow_non_contiguous=True)
            S.op("pool", lambda h: h.memset(ua[:], 0.0), writes=["ua"])
            S.op("pool", lambda h: h.memset(pb[0][:], 0.0), writes=["pb0"])
            S.op("pool", lambda h: h.memset(pb[1][:], 0.0), writes=["pb1"])
            shifts = [(-1, 0), (-1, 1), (-2, 2), (-4, 4)]
            for g in range(4):
                i = g % 2
                self.load_w(wp[i][:], "w_in", l * D * DIN, 0, 8, DIN, OFF_POOL + 128 * g, 128, [("wp", i)])
                S.dma("pool", pw[i][:], AP(self.T["pool_w"], (l * 4 + g) * 16384, [[128, 128], [1, 128]]), writes=[("pw", i)])
                S.dma("sp", rcb[:], AP(self.T["c_rcnt"], g * LP, [[0, 128], [1, LP]]), writes=["rcb"])
                for (t0, n) in TCH:
                    bank, bk = self.next_ps()
                    for k in range(8):
                        self.mm(bank[:, 0:n], wp[i][:, k, :], hnT[:, k, t0:t0 + n], k == 0, k == 7, [("wp", i)], [bk])
                    self.evac(ua[:, OFF + t0:OFF + t0 + n], bank[:, 0:n], [bk], ["ua"])
                cur, curk = ua, "ua"
                lo, hi = OFF - 16, OFF + LP + 16
                for lev in range(g + 1):
                    s0, s1 = shifts[lev]
                    dst, dk = pb[lev % 2], "pb%d" % (lev % 2)
                    S.op(self.ew(), lambda h, dst=dst, cur=cur, s0=s0, s1=s1: h.tensor_tensor(
                        out=dst[:, lo:hi], in0=cur[:, lo + s0:hi + s0], in1=cur[:, lo + s1:hi + s1], op=ALU.add),
                        reads=[curk], writes=[dk])
                    cur, curk = dst, dk
                oth, ok = pb[(g + 1) % 2], "pb%d" % ((g + 1) % 2)
                S.op("dve", lambda h, cur=cur, oth=oth: h.tensor_tensor(out=oth[:, OFF:OFF + LP], in0=cur[:, OFF:OFF + LP], in1=rcb[:], op=ALU.mult),
                     reads=[curk, "rcb"], writes=[ok])
                S.op("pool", lambda h, oth=oth: h.tensor_tensor(out=dd[:], in0=oth[:, OFF:OFF + LP], in1=ua[:, OFF:OFF + LP], op=ALU.subtract),
                     reads=[ok, "ua"], writes=["dd"])
                for (t0, n) in TCH:
                    bank, bk = self.next_ps()
                    self.mm(bank[:, 0:n], pw[i][:], dd[:, t0:t0 + n], True, True, [("pw", i), "dd"], [bk])
                    S.op("act", lambda h, bank=bank, n=n, t0=t0, g=g: h.activation(out=yTp[:, g, t0:t0 + n], in_=bank[:, 0:n], func=AF.Copy,
                                                                                    scale=psc[:, g:g + 1]),
                         reads=[bk, "psc"], writes=["yT_all"])
            S.barrier()
        self.tap_yT("yTp", yTp, l)

    def phase_attn(self, l, hnT, yTa):
        S = self.S
        A = self.A
        with ExitStack() as st:
            qT = self.sbuf(st, "qT", [128, 4, LP], BF16)
            kT = self.sbuf(st, "kT", [128, 2, LP], BF16)
            Va = self.sbuf(st, "Va", [128, NT, 2, 65], BF16)
            Vm = self.sbuf(st, "Vm", [16, 2, 65], BF16)
            cosT = self.sbuf(st, "cosT", [128, LP], F32)
            sinT = self.sbuf(st, "sinT", [128, LP], F32)
            mkf = self.sbuf(st, "mkf", [128, 1024], F32)
            mk = self.sbuf(st, "mk", [128, 1024], BF16)
            ww = [self.sbuf(st, "ww%d" % i, [128, 8, 128], BF16) for i in range(2)]
            wr = [self.sbuf(st, "wr%d" % i, [128, 8, 128], BF16) for i in range(2)]
            t1 = [self.sbuf(st, "t1%d" % i, [128, 512], F32) for i in range(2)]
            t2 = [self.sbuf(st, "t2%d" % i, [128, 512], F32) for i in range(2)]
            PT = [self.sbuf(st, "PT%d" % i, [128, 512], BF16) for i in range(8)]
            den = self.sbuf(st, "den", [128, 16], F32)
            snk = self.sbuf(st, "snk", [128, 8], F32)
            S.dma("sp", cosT[:], A["c_cos"], writes=["cosT"])
            S.dma("sp", sinT[:], A["c_sin"], writes=["sinT"])
            S.dma("sp", mkf[:, 0:512], A["c_maskp"], writes=["mkf"])
            S.dma("sp", mkf[:, 512:1024], A["c_maskn"], writes=["mkf"])
            S.op("dve", lambda h: h.tensor_copy(out=mk[:], in_=mkf[:]), reads=["mkf"], writes=["mk"])
            S.dma("sp", snk[:], AP(self.T["attn_sink"], l * 8, [[0, 128], [1, 8]]), writes=["snk"])
            S.op("act", lambda h: h.activation(out=snk[:], in_=snk[:], func=AF.Exp), reads=["snk"], writes=["snk"])
            S.op("pool", lambda h: h.memset(Va[:, :, :, 64:65], 1.0), writes=["Va1"])
            S.op("pool", lambda h: h.memset(Vm[:, :, 64:65], 1.0), writes=["Vm1"])
            for ct in range(6):
                i = ct % 2
                if ct < 4:
                    self.load_w(ww[i][:], "w_in", l * D * DIN, 0, 8, DIN, 128 * ct, 128, [("ww", i)])
                    dst = lambda t0, n, ct=ct: qT[:, ct, t0:t0 + n]
                else:
                    kh = ct - 4
                    self.load_w(ww[i][:, :, 0:64], "w_in", l * D * DIN, 0, 8, DIN, OFF_K + 64 * kh, 64, [("ww", i)])
                    self.load_w(ww[i][:, :, 64:128], "w_in", l * D * DIN, 0, 8, DIN, OFF_K + 64 * kh, 64, [("ww", i)])
                    dst = lambda t0, n, kh=kh: kT[:, kh, t0:t0 + n]
                wv = ww[i][:].rearrange("p k (a two j) -> p k a two j", a=2, two=2, j=32)
                rv = wr[i][:].rearrange("p k (a two j) -> p k a two j", a=2, two=2, j=32)
                S.op("dve", lambda h, wv=wv, rv=rv: h.tensor_copy(out=rv[:, :, :, 0, :], in_=wv[:, :, :, 1, :]), reads=[("ww", i)], writes=[("wr", i)])
                S.op("act", lambda h, wv=wv, rv=rv: h.activation(out=rv[:, :, :, 1, :], in_=wv[:, :, :, 0, :], func=AF.Copy), reads=[("ww", i)], writes=[("wr", i)])
                for ti, (t0, n) in enumerate(TCH):
                    j = ti % 2
                    ba, bka = self.next_ps()
                    bb_, bkb = self.next_ps()
                    for k in range(8):
                        self.mm(ba[:, 0:n], ww[i][:, k, :], hnT[:, k, t0:t0 + n], k == 0, k == 7, [("ww", i)], [bka])
                    for k in range(8):
                        self.mm(bb_[:, 0:n], wr[i][:, k, :], hnT[:, k, t0:t0 + n], k == 0, k == 7, [("wr", i)], [bkb])
                    S.op("dve", lambda h, ba=ba, n=n, t0=t0, j=j: h.tensor_tensor(out=t1[j][:, 0:n], in0=ba[:, 0:n], in1=cosT[:, t0:t0 + n], op=ALU.mult),
                         reads=[bka, "cosT"], writes=[("t1", j)])
                    S.op("dve", lambda h, bb_=bb_, n=n, t0=t0, j=j: h.tensor_tensor(out=t2[j][:, 0:n], in0=bb_[:, 0:n], in1=sinT[:, t0:t0 + n], op=ALU.mult),
                         reads=[bkb, "sinT"], writes=[("t2", j)])
                    d_ap = dst(t0, n)
                    S.op("dve", lambda h, d_ap=d_ap, n=n, j=j: h.tensor_tensor(out=d_ap, in0=t1[j][:, 0:n], in1=t2[j][:, 0:n], op=ALU.add),
                         reads=[("t1", j), ("t2", j)], writes=["qk"])
            wvv = ww[0]
            self.load_w(wvv[:], "w_in", l * D * DIN, 0, 8, DIN, OFF_V, 128, [("ww", 0)])
            for b in range(NT):
                bank, bk = self.next_ps()
                for k in range(8):
                    self.mm(bank[:, 0:128], hnT[:, k, b * 128:(b + 1) * 128], wvv[:, k, :], k == 0, k == 7, [("ww", 0)], [bk])
                self.evac(Va[:, b, :, 0:64], bank[:, 0:128].rearrange("p (a d) -> p a d", a=2), [bk], [("Va", b)])
            bank, bk = self.next_ps()
            for k in range(8):
                self.mm(bank[0:16, 0:128], hnT[:, k, FRONT:128], wvv[:, k, :], k == 0, k == 7, [("ww", 0)], [bk])
            self.evac(Vm[:, :, 0:64], bank[0:16, 0:128].rearrange("p (a d) -> p a d", a=2), [bk], ["Vm"])
            if l == 0 and "qT" in self.tap_out:
                with ExitStack() as st2:
                    tmp = self.sbuf(st2, "tmp", [128, 6 * LP], F32)
                    S.op("dve", lambda h: h.tensor_copy(out=tmp[:, 0:4 * LP], in_=qT[:].rearrange("p k t -> p (k t)")), reads=["qk"], writes=["tmp"])
                    S.op("dve", lambda h: h.tensor_copy(out=tmp[:, 4 * LP:6 * LP], in_=kT[:].rearrange("p k t -> p (k t)")), reads=["qk"], writes=["tmp"])
                    self.tap("qT", tmp[:], ["tmp"])
                    S.barrier()
            ytm = self.sbuf(st, "ytmall", [128, NT, 512], BF16)
            for n_ in range(NT):
                for kh in range(2):
                    kbs = []
                    if n_ - 1 >= 1:
                        kbs.append((n_ - 1, 0))
                    if n_ >= 1:
                        kbs.append((n_, None))
                    if n_ + 1 <= NT - 1:
                        kbs.append((n_ + 1, 1))
                    kbs.append((-1, None))
                    slots = []
                    for idx, (kb, mi) in enumerate(kbs):
                        nk = 128 if kb >= 0 else 16
                        kc0 = kb * 128 if kb >= 0 else FRONT
                        sl = (kh * 4 + idx)
                        for half in range(2):
                            bank, bk = self.next_ps()
                            if mi is not None:
                                self.mm(bank[:, 0:256], self.ident[:], mk[:, mi * 512:mi * 512 + 256], True, False, ["ident", "mk"], [bk])
                            for ii in range(2):
                                i = 2 * ii + half
                                hq = 4 * kh + i
                                tile_ = hq // 2
                                r0 = 64 * half
                                self.mm(bank[0:nk, ii * 128:(ii + 1) * 128], kT[r0:r0 + 64, kh, kc0:kc0 + nk],
                                        qT[r0:r0 + 64, tile_, n_ * 128:(n_ + 1) * 128], mi is None, True, ["qk"], [bk])
                            S.op("act", lambda h, bank=bank, nk=nk, sl=sl, half=half: h.activation(out=PT[sl][0:nk, half * 256:(half + 1) * 256], in_=bank[0:nk, 0:256], func=AF.Exp, scale=0.125),
                                 reads=[bk], writes=[("PT", sl)])
                        slots.append((sl, kb, nk))
                    ob, obk = self.next_ps()
                    for i in range(4):
                        pc0 = (i % 2) * 256 + (i // 2) * 128
                        for idx, (sl, kb, nk) in enumerate(slots):
                            vsrc = Va[:, kb, kh, :] if kb >= 0 else Vm[:, kh, :]
                            rds = [("PT", sl)] + ([("Va", kb), "Va1"] if kb >= 0 else ["Vm", "Vm1"])
                            self.mm(ob[:, i * 65:(i + 1) * 65], PT[sl][0:nk, pc0:pc0 + 128], vsrc, idx == 0, idx == len(slots) - 1, rds, [obk])
                    ov = ob[:, 0:260].rearrange("p (i c) -> p i c", i=4)
                    dsl = den[:, kh * 8:kh * 8 + 4]
                    rsl = den[:, kh * 8 + 4:kh * 8 + 8]
                    S.op("dve", lambda h, ov=ov, dsl=dsl, kh=kh: h.tensor_tensor(out=dsl.unsqueeze(2), in0=ov[:, :, 64:65], in1=snk[:, 4 * kh:4 * kh + 4].unsqueeze(2), op=ALU.add),
                         reads=[obk, "snk"], writes=[("den", kh)])
                    S.op("dve", lambda h, dsl=dsl, rsl=rsl: h.reciprocal(out=rsl, in_=dsl), reads=[("den", kh)], writes=[("rden", kh)])
                    for i in range(4):
                        hq = 4 * kh + i
                        S.op("act", lambda h, ob=ob, i=i, hq=hq, n_=n_, rsl=rsl: h.activation(out=ytm[:, n_, hq * 64:(hq + 1) * 64], in_=ob[:, i * 65:i * 65 + 64],
                                                                                           func=AF.Copy, scale=rsl[:, i:i + 1]),
                             reads=[obk, ("rden", kh)], writes=["ytm"])
            S.barrier()
            for n_ in range(NT):
                bank, bk = self.next_pt()
                for j in range(4):
                    self.tr(bank[:, j * 128:(j + 1) * 128], ytm[:, n_, j * 128:(j + 1) * 128], self.ident[:, :], ["ytm"], [bk])
                self.evac(yTa[:, :, n_ * 128:(n_ + 1) * 128], bank[:, 0:512].rearrange("p (j t) -> p j t", j=4), [bk], ["yT_all"])
            S.barrier()
        self.tap_yT("yTa", yTa, l)

    def phase_merge(self, l, hnT, yTs):
        S = self.S
        with ExitStack() as st:
            wg = [self.sbuf(st, "wg%d" % i, [128, 8, 3, 128], BF16) for i in range(2)]
            wb = [self.sbuf(st, "wb%d" % i, [128, 12, 128], BF16) for i in range(2)]
            wo = self.sbuf(st, "wo", [128, 4, D], BF16)
            mT = self.sbuf(st, "mT", [128, 4, LP], BF16)
            sg = [self.sbuf(st, "sg%d" % i, [128, 512], F32) for i in range(2)]
            pa = [self.sbuf(st, "pa%d" % i, [128, 512], F32) for i in range(3)]
            ht = [self.sbuf(st, "ht%d" % i, [128, D], F32) for i in range(4)]
            cnt = 0
            for half in range(2):
                for dt in range(4):
                    dti = 4 * half + dt
                    i = dti % 2
                    for c in range(3):
                        self.load_w(wg[i][:, :, c, :], "w_in", l * D * DIN, 0, 8, DIN, OFF_GATE + c * D + dti * 128, 128, [("wg", i)])
                    self.load_w(wb[i][:], "w_branch", l * 3 * 512 * D, 0, 12, D, dti * 128, 128, [("wb", i)])
                    for (t0, n) in TCH:
                        for c in range(3):
                            bB, kB = self.next_ps()
                            bG, kG = self.next_ps()
                            for k in range(4):
                                self.mm(bB[:, 0:n], wb[i][:, 4 * c + k, :], yTs[c][:, k, t0:t0 + n], k == 0, k == 3, [("wb", i), "yT_all"], [kB])
                            for k in range(8):
                                self.mm(bG[:, 0:n], wg[i][:, k, c, :], hnT[:, k, t0:t0 + n], k == 0, k == 7, [("wg", i)], [kG])
                            j = cnt % 2
                            cnt += 1
                            S.op("act", lambda h, bG=bG, n=n, j=j: h.activation(out=sg[j][:, 0:n], in_=bG[:, 0:n], func=AF.Sigmoid), reads=[kG], writes=[("sg", j)])
                            S.op("dve", lambda h, bB=bB, n=n, j=j, c=c: h.tensor_tensor(out=pa[c][:, 0:n], in0=bB[:, 0:n], in1=sg[j][:, 0:n], op=ALU.mult),
                                 reads=[kB, ("sg", j)], writes=[("pa", c)])
                        S.op("dve", lambda h, n=n: h.tensor_tensor(out=pa[0][:, 0:n], in0=pa[0][:, 0:n], in1=pa[1][:, 0:n], op=ALU.add),
                             reads=[("pa", 0), ("pa", 1)], writes=[("pa", 0)])
                        S.op("dve", lambda h, n=n, t0=t0, dt=dt: h.tensor_tensor(out=mT[:, dt, t0:t0 + n], in0=pa[0][:, 0:n], in1=pa[2][:, 0:n], op=ALU.add),
                             reads=[("pa", 0), ("pa", 2)], writes=["mT"])
                self.load_w(wo[:], "w_out", l * D * D, half * 512, 4, D, 0, D, ["wo"])
                NB = 4

                def load_h(b):
                    rd = [("h", b)] + ([("hm", 0)] if b == 0 else [])
                    S.dma("sp", ht[b % NB][:], self.h_d[b * 128:(b + 1) * 128, :], reads=rd, writes=[("ht", b % NB)])

                for b in range(NB - 1):
                    load_h(b)
                for b in range(NT):
                    i = b % NB
                    if b + NB - 1 < NT:
                        load_h(b + NB - 1)
                    for ch in range(2):
                        bank, bk = self.next_ps()
                        for dt in range(4):
                            self.mm(bank[:, :], mT[:, dt, b * 128:(b + 1) * 128], wo[:, dt, ch * 512:(ch + 1) * 512], dt == 0, dt == 3, ["mT", "wo"], [bk])
                        S.op("dve", lambda h, bank=bank, i=i, ch=ch: h.tensor_tensor(out=ht[i][:, ch * 512:(ch + 1) * 512], in0=bank[:, :], in1=ht[i][:, ch * 512:(ch + 1) * 512], op=ALU.add),
                             reads=[bk, ("ht", i)], writes=[("ht", i)])
                    S.dma("sp", self.h_d[b * 128:(b + 1) * 128, :], ht[i][:], reads=[("ht", i)], writes=[("h", b), ("hm", 0)] if b == 0 else [("h", b)])
            S.barrier()
        if l == 0 and "h_mix" in self.tap_out:
            S.dma("sp", self.tap_out["h_mix"], self.h_d, reads=[("h", b) for b in range(NT)])
            S.barrier()

    def phase_ffn(self, l):
        S = self.S
        last = (l == NL - 1)
        with ExitStack() as st:
            hs = self.sbuf(st, "hs", [128, NT, D], F32)
            hn2T = self.sbuf(st, "hn2T", [128, 8, LP], BF16)
            for b in range(NT):
                rd = [("h", b)] + ([("hm", 0)] if b == 0 else [])
                S.dma("sp", hs[:, b, :], self.h_d[b * 128:(b + 1) * 128, :], reads=rd, writes=[("hs", b, 0), ("hs", b, 1)])
            with ExitStack() as st2:
                self.norm_transpose(st2, "norm_mlp", l * D, hn2T, lambda b: (hs[:, b, :], [("hs", b, 0), ("hs", b, 1)]))
                wu = [self.sbuf(st2, "wu%d" % i, [128, 8, 512], BF16) for i in range(2)]
                wd = [self.sbuf(st2, "wd%d" % i, [128, 4, D], BF16) for i in range(2)]
                aT = self.sbuf(st2, "aT", [128, 4, LP], BF16)
                rl = [self.sbuf(st2, "rl%d" % i, [128, 512], F32) for i in range(2)]
                cnt = 0
                for fc in range(8):
                    i = fc % 2
                    self.load_w(wu[i][:], "w_up", l * D * DFF, 0, 8, DFF, fc * 512, 512, [("wu", i)])
                    self.load_w(wd[i][:], "w_down", l * DFF * D, fc * 512, 4, D, 0, D, [("wd", i)])
                    for ft in range(4):
                        for (t0, n) in TCH:
                            bank, bk = self.next_ps()
                            for k in range(8):
                                self.mm(bank[:, 0:n], wu[i][:, k, ft * 128:(ft + 1) * 128], hn2T[:, k, t0:t0 + n], k == 0, k == 7,
                                        [("wu", i)] + [("hnT", bb2) for bb2 in range(t0 // 128, (t0 + n) // 128)], [bk])
                            j = cnt % 2
                            cnt += 1
                            S.op("act", lambda h, bank=bank, n=n, j=j: h.activation(out=rl[j][:, 0:n], in_=bank[:, 0:n], func=AF.Relu), reads=[bk], writes=[("rl", j)])
                            S.op("dve", lambda h, n=n, j=j, ft=ft, t0=t0: h.tensor_tensor(out=aT[:, ft, t0:t0 + n], in0=rl[j][:, 0:n], in1=rl[j][:, 0:n], op=ALU.mult),
                                 reads=[("rl", j)], writes=["aT"])
                    for b in range(NT):
                        for ch in range(2):
                            bank, bk = self.next_ps()
                            for ft in range(4):
                                self.mm(bank[:, :], aT[:, ft, b * 128:(b + 1) * 128], wd[i][:, ft, ch * 512:(ch + 1) * 512], ft == 0, ft == 3, ["aT", ("wd", i)], [bk])
                            S.op("dve", lambda h, bank=bank, b=b, ch=ch: h.tensor_tensor(out=hs[:, b, ch * 512:(ch + 1) * 512], in0=bank[:, :], in1=hs[:, b, ch * 512:(ch + 1) * 512], op=ALU.add),
                                 reads=[bk, ("hs", b, ch)], writes=[("hs", b, ch)])
                S.barrier()
            if l == 0 and "h_new" in self.tap_out:
                for b in range(NT):
                    S.dma("sp", self.tap_out["h_new"][b * 128:(b + 1) * 128, :], hs[:, b, :], reads=[("hs", b, 0), ("hs", b, 1)])
            if not last or self.nlayers < NL:
                for b in range(NT):
                    S.dma("sp", self.h_d[b * 128:(b + 1) * 128, :], hs[:, b, :], reads=[("hs", b, 0), ("hs", b, 1)], writes=[("h", b), ("hm", 0)] if b == 0 else [("h", b)])
            if l == self.nlayers - 1:
                with ExitStack() as st2:
                    small = self.small
                    gain = self.sbuf(st2, "gainf", [128, D], F32)
                    S.dma("sp", gain[:], AP(self.T["norm_final"], 0, [[0, 128], [1, D]]), writes=["gainf"])
                    junk = self.sbuf(st2, "junkf", [128, D], BF16)
                    ss = self.sbuf(st2, "ssf", [128, 3 * NT], F32)
                    ob = [self.sbuf(st2, "ob%d" % i, [128, D], F32) for i in range(2)]
                    for b in range(1, NT):
                        i = b % 2
                        S.op("act", lambda h, b=b: h.activation(out=junk[:], in_=hs[:, b, :], func=AF.Square, accum_out=ss[:, b:b + 1]),
                             reads=[("hs", b, 0), ("hs", b, 1)], writes=["junkf", ("ssf", b)])
                        S.op("act", lambda h, b=b: h.activation(out=ss[:, NT + b:NT + b + 1], in_=ss[:, b:b + 1], func=AF.Sqrt, scale=1.0 / D, bias=small[:, 0:1]),
                             reads=[("ssf", b), "eps"], writes=[("ssf1", b)])
                        S.op("dve", lambda h, b=b: h.reciprocal(out=ss[:, 2 * NT + b:2 * NT + b + 1], in_=ss[:, NT + b:NT + b + 1]), reads=[("ssf1", b)], writes=[("ssf2", b)])
                        S.op("dve", lambda h, b=b, i=i: h.scalar_tensor_tensor(out=ob[i][:], in0=hs[:, b, :], scalar=ss[:, 2 * NT + b:2 * NT + b + 1], in1=gain[:], op0=ALU.mult, op1=ALU.mult),
                             reads=[("hs", b, 0), ("hs", b, 1), ("ssf2", b), "gainf"], writes=[("ob", i)])
                        S.dma("sp", self.out[(b - 1) * 128:b * 128, :], ob[i][:], reads=[("ob", i)])
            S.barrier()

    def phase_ssm_core(self, l, hnT):
        S = self.S
        T = self.T
        TWO_PI = 2.0 * math.pi
        with ExitStack() as st:
            UT = self.sbuf(st, "UT", [128, 32, NCH], BF16)
            Kb = self.sbuf(st, "Kb", [128, 32, 128], BF16)
            Cp = self.sbuf(st, "Cp", [128, 2, 16, 2, 128], BF16)
            BpT = self.sbuf(st, "BpT", [128, 2, 16, 2, 128], BF16)
            a8 = self.sbuf(st, "a8", [128, 2, 2, 16, 2], F32)
            with ExitStack() as sa:
                ht = [self.sbuf(sa, "ht%d" % i, [128, D], F32) for i in range(3)]

                def src_tiles(b, l=l):
                    i = b % 3
                    if l == 0:
                        if b == 0:
                            S.op("pool", lambda h, i=i: h.memset(ht[i][0:FRONT, :], 0.0), writes=[("ht", i)])
                            S.dma("sp", ht[i][FRONT:128, :], self.A["meta_tokens"], writes=[("htm", i)])
                            return ht[i][:], [("ht", i), ("htm", i)]
                        S.dma("sp", ht[i][:], self.A["x"][(b - 1) * 128:b * 128, :], writes=[("ht", i), ("htm", i)])
                        return ht[i][:], [("ht", i), ("htm", i)]
                    rd = [("h", b)] + ([("hm", 0)] if b == 0 else [])
                    S.dma("sp", ht[i][:], self.h_d[b * 128:(b + 1) * 128, :], reads=rd, writes=[("ht", i), ("htm", i)])
                    return ht[i][:], [("ht", i), ("htm", i)]

                self.norm_transpose(sa, "norm_mix", l * D, hnT, src_tiles)
                wss = self.sbuf(sa, "wss", [128, 8, 512], BF16)
                utm = [self.sbuf(sa, "utm%d" % i, [128, 512], BF16) for i in range(2)]
                ucm = [self.sbuf(sa, "ucm%d" % i, [128, 4096], BF16) for i in range(3)]
                self.load_w(wss[:], "w_in", l * D * DIN, 0, 8, DIN, OFF_SSM, 512, ["wss"])
                for b in range(NT):
                    i = b % 2
                    bank, bk = self.next_ps()
                    for k in range(8):
                        self.mm(bank[:, :], hnT[:, k, b * 128:(b + 1) * 128], wss[:, k, :], k == 0, k == 7, ["wss", ("hnT", b)], [bk])
                    self.evac(utm[i][:], bank[:, :], [bk], [("utm", i)])
                    S.dma("sp", self.u_d[b * 128:(b + 1) * 128, :], utm[i][:], reads=[("utm", i)], writes=[("ud", b)])
                for ct in range(3):
                    n = 128 if ct < 2 else 16
                    S.dma("sp", ucm[ct][0:n, :], self.u_d.rearrange("(c s) x -> c (s x)", s=8)[ct * 128:ct * 128 + n, :],
                          reads=[("ud", b) for b in range(NT)], writes=[("ucm", ct)])
                ucg = [self.sbuf(sa, "ucg%d" % i, [128, 4096], BF16) for i in range(3)]
                for ct in range(3):
                    n = 128 if ct < 2 else 16
                    engs = ("dve", "pool", "act")
                    src = ucm[ct][0:n, :].rearrange("c (s g q) -> c g s q", s=8, g=32, q=16)
                    dstv = ucg[ct][0:n, :].rearrange("c (g s q) -> c g s q", s=8, g=32, q=16)
                    for g4 in range(4):
                        self.evac(dstv[:, 8 * g4:8 * g4 + 8], src[:, 8 * g4:8 * g4 + 8], [("ucm", ct)], [("ucg", ct)], eng=engs[(ct + g4) % 3])
                for ct in range(2):
                    uv = ucg[ct][:, :].rearrange("c (g x) -> c g x", g=32)
                    for gg in range(4):
                        bank, bk = self.next_pt()
                        for gi in range(8):
                            self.tr(bank[:, gi * 128:(gi + 1) * 128], uv[:, 8 * gg + gi], self.ident[:, :], [("ucg", ct)], [bk])
                        self.evac(UT[:, 8 * gg:8 * gg + 8, ct * 128:(ct + 1) * 128], bank[:, :].rearrange("p (g c) -> p g c", g=8), [bk], ["UT"])
                uv = ucg[2][0:16, :].rearrange("c (g x) -> c g x", g=32)
                bank, bk = self.next_pt()
                for g in range(32):
                    self.tr(bank[:, g * 16:(g + 1) * 16], uv[:, g], self.ident[0:16, 0:16], [("ucg", 2)], [bk])
                self.evac(UT[:, :, 256:272], bank[:, 0:512].rearrange("p (g c) -> p g c", g=32), [bk], ["UT"])
                S.barrier()
            if self.stop_after == "ssmA":
                return
            with ExitStack() as sb_:
                Bp = self.sbuf(sb_, "Bp", [128, 2, 16, 2, 128], BF16)
                dcol = self.sbuf(sb_, "dcol", [128, 32], F32)
                mfb = self.sbuf(sb_, "mfb", [128, 256], F32)
                tau = self.sbuf(sb_, "tau", [128, 2, 16, 9], F32)
                S.dma("sp", mfb[:], self.A["c_maskfb"], writes=["mfb"])
                S.dma("sp", tau[:].rearrange("p d m t -> p (d m t)"), self.A["c_tau"], writes=["tau"])
                for s in range(8):
                    S.dma("sp", dcol[16 * s:16 * s + 16, :], AP(T["ssm_d"], l * 512, [[1, 16], [16, 32]]), writes=["dcol"], allow_slow_non_contiguous=True)
                prm = self.sbuf(sb_, "prm", [128, 12, 2, 16], F32)
                LR, LI, DT, X, TH, DEN, NRE, FR, FI, ABR, ABI, TMP = range(12)
                for idx_, nm_ in ((LR, "ssm_lam_re"), (LI, "ssm_lam_im")):
                    for d in range(2):
                        for m4 in range(4):
                            S.dma("sp", prm[:, idx_, d, 4 * m4:4 * m4 + 4], AP(T[nm_], l * 4096 + d * 2048 + m4 * 512, [[1, 128], [128, 4]]),
                                  writes=["prm"], allow_slow_non_contiguous=True)
                ldt = self.sbuf(sb_, "ldt", [128, 2, 16, 2], F32)
                S.dma("sp", ldt[:].rearrange("p d m j -> p (d m j)"), AP(T["ssm_log_dt"], l * 64, [[0, 128], [1, 64]]), writes=["ldt"])
                for j in range(2):
                    S.op("dve", lambda h, j=j: h.tensor_copy(out=prm[64 * j:64 * j + 64, DT], in_=ldt[64 * j:64 * j + 64, :, :, j]), reads=["ldt"], writes=["prm"])
                braw = self.sbuf(sb_, "braw", [128, 2, 2, 16, 16], F32)
                craw = self.sbuf(sb_, "craw", [128, 2, 2, 16, 16], F32)
                for ri, nm in enumerate(("ssm_b_re", "ssm_b_im")):
                    for d in range(2):
                        for m4 in range(4):
                            S.dma("sp", braw[:, ri, d, 4 * m4:4 * m4 + 4, :], AP(T[nm], l * 65536 + d * 32768 + m4 * 8192, [[16, 128], [2048, 4], [1, 16]]), writes=["braw"])
                ctmp = [self.sbuf(sb_, "ctmp%d" % i, [128, 2, 64], F32) for i in range(2)]
                cc = 0
                for ri, nm in enumerate(("ssm_c_re", "ssm_c_im")):
                    for d in range(2):
                        for i4 in range(4):
                            ci = cc % 2
                            cc += 1
                            S.dma("sp", ctmp[ci][:], AP(T[nm], l * 65536 + d * 32768 + i4 * 8192, [[64, 128], [0, 2], [1, 64]]), writes=[("ctmp", ci)])
                            bank, bk = self.next_ps()
                            S.op("pe", lambda h, bank=bank, ci=ci: h.transpose(out=bank[:, 0:128], in_=ctmp[ci][:].rearrange("p a n -> p (a n)"), identity=self.identf[:]),
                                 reads=[("ctmp", ci), "identf"], writes=[bk])
                            for j in range(2):
                                src = bank[64 * j:64 * j + 64, 0:128].rearrange("p (ml jj q) -> p ml jj q", ml=4, jj=2, q=16)[:, :, j, :]
                                self.evac(craw[64 * j:64 * j + 64, ri, d, 4 * i4:4 * i4 + 4, :], src, [bk], ["craw"], eng="dve")
                P = prm
                v = lambda idx: P[:, idx].rearrange("p d m -> p (d m)")
                S.op("act", lambda h: h.activation(out=v(DT), in_=v(DT), func=AF.Exp), reads=["prm"], writes=["prm"])
                S.op("dve", lambda h: h.tensor_tensor(out=v(X), in0=v(LR), in1=v(DT), op=ALU.mult), reads=["prm"], writes=["prm"])
                S.op("dve", lambda h: h.tensor_tensor(out=v(TH), in0=v(LI), in1=v(DT), op=ALU.mult), reads=["prm"], writes=["prm"])
                pw = self.sbuf(sb_, "pw", [128, 10, 288], F32)
                XE, THE, MC, MB, SN, CS, KI, PCR, PCI, PBR = range(10)
                pbi = self.sbuf(sb_, "pbi", [128, 288], F32)
                ki = self.sbuf(sb_, "ki", [128, 288], I32)
                w3 = lambda a: a.rearrange("p (dm t) -> p dm t", t=9)
                tau3 = tau[:].rearrange("p d m t -> p (d m) t")
                S.op("dve", lambda h: h.tensor_tensor(out=w3(pw[:, XE]), in0=bc(v(X), 2, 9), in1=tau3, op=ALU.mult), reads=["prm", "tau"], writes=["pw"])
                S.op("dve", lambda h: h.tensor_tensor(out=w3(pw[:, THE]), in0=bc(v(TH), 2, 9), in1=tau3, op=ALU.mult), reads=["prm", "tau"], writes=["pw"])
                S.op("act", lambda h: h.activation(out=pw[:, MC], in_=pw[:, XE], func=AF.Exp), reads=["pw"], writes=["pw"])
                S.op("act", lambda h: h.activation(out=pw[:, MB], in_=pw[:, XE], func=AF.Exp, scale=-1.0), reads=["pw"], writes=["pw"])

                def sin_of(dst, shift):
                    S.op("dve", lambda h: h.tensor_scalar(out=ki[:], in0=pw[:, THE], scalar1=shift, scalar2=1.0 / TWO_PI, op0=ALU.add, op1=ALU.mult), reads=["pw"], writes=["ki"])
                    S.op("dve", lambda h: h.tensor_copy(out=pw[:, KI], in_=ki[:]), reads=["ki"], writes=["pw"])
                    S.op("dve", lambda h: h.scalar_tensor_tensor(out=pw[:, KI], in0=pw[:, KI], scalar=-TWO_PI, in1=pw[:, THE], op0=ALU.mult, op1=ALU.add), reads=["pw"], writes=["pw"])
                    S.op("dve", lambda h: h.tensor_scalar(out=pw[:, KI], in0=pw[:, KI], scalar1=shift, scalar2=math.pi, op0=ALU.add, op1=ALU.min), reads=["pw"], writes=["pw"])
                    S.op("dve", lambda h: h.tensor_scalar(out=pw[:, KI], in0=pw[:, KI], scalar1=-math.pi, scalar2=None, op0=ALU.max), reads=["pw"], writes=["pw"])
                    S.op("act", lambda h: h.activation(out=dst, in_=pw[:, KI], func=AF.Sin), reads=["pw"], writes=["pw"])

                sin_of(pw[:, SN], 0.0)
                sin_of(pw[:, CS], math.pi / 2)
                S.op("dve", lambda h: h.tensor_tensor(out=pw[:, PCR], in0=pw[:, MC], in1=pw[:, CS], op=ALU.mult), reads=["pw"], writes=["pw"])
                S.op("dve", lambda h: h.tensor_tensor(out=pw[:, PCI], in0=pw[:, MC], in1=pw[:, SN], op=ALU.mult), reads=["pw"], writes=["pw"])
                S.op("dve", lambda h: h.tensor_tensor(out=pw[:, PBR], in0=pw[:, MB], in1=pw[:, CS], op=ALU.mult), reads=["pw"], writes=["pw"])
                S.op("dve", lambda h: h.scalar_tensor_tensor(out=pbi[:], in0=pw[:, MB], scalar=-1.0, in1=pw[:, SN], op0=ALU.mult, op1=ALU.mult), reads=["pw"], writes=["pbi"])
                pcr4 = pw[:, PCR].rearrange("p (d m t) -> p d m t", d=2, m=16)
                pci4 = pw[:, PCI].rearrange("p (d m t) -> p d m t", d=2, m=16)
                for d, ti in ((0, 1), (1, 6)):
                    S.op("dve", lambda h, d=d, ti=ti: h.tensor_copy(out=P[:, ABR, d, :], in_=pcr4[:, d, :, ti]), reads=["pw"], writes=["prm"])
                    S.op("dve", lambda h, d=d, ti=ti: h.tensor_copy(out=P[:, ABI, d, :], in_=pci4[:, d, :, ti]), reads=["pw"], writes=["prm"])
                for d in range(2):
                    S.op("dve", lambda h, d=d: h.tensor_copy(out=a8[:, 0, d, :, 0], in_=pcr4[:, d, :, 8]), reads=["pw"], writes=["a8"])
                    S.op("dve", lambda h, d=d: h.tensor_scalar(out=a8[:, 0, d, :, 1], in0=pci4[:, d, :, 8], scalar1=-1.0, scalar2=None, op0=ALU.mult), reads=["pw"], writes=["a8"])
                    S.op("dve", lambda h, d=d: h.tensor_copy(out=a8[:, 1, d, :, 0], in_=pcr4[:, d, :, 8]), reads=["pw"], writes=["a8"])
                    S.op("dve", lambda h, d=d: h.tensor_copy(out=a8[:, 1, d, :, 1], in_=pci4[:, d, :, 8]), reads=["pw"], writes=["a8"])
                tt = lambda o, a, b, op: S.op("dve", lambda h: h.tensor_tensor(out=v(o), in0=v(a), in1=v(b), op=op), reads=["prm"], writes=["prm"])
                tt(DEN, LR, LR, ALU.mult)
                tt(TMP, LI, LI, ALU.mult)
                tt(DEN, DEN, TMP, ALU.add)
                S.op("dve", lambda h: h.reciprocal(out=v(DEN), in_=v(DEN)), reads=["prm"], writes=["prm"])
                S.op("dve", lambda h: h.tensor_scalar(out=v(NRE), in0=v(ABR), scalar1=-1.0, scalar2=None, op0=ALU.add), reads=["prm"], writes=["prm"])
                tt(FR, NRE, LR, ALU.mult)
                tt(TMP, ABI, LI, ALU.mult)
                tt(FR, FR, TMP, ALU.add)
                tt(FR, FR, DEN, ALU.mult)
                tt(FI, ABI, LR, ALU.mult)
                tt(TMP, NRE, LI, ALU.mult)
                tt(FI, FI, TMP, ALU.subtract)
                tt(FI, FI, DEN, ALU.mult)
                bb = self.sbuf(sb_, "bb", [128, 2, 32, 16], F32)
                m1 = self.sbuf(sb_, "m1", [128, 4096], F32)
                m2 = self.sbuf(sb_, "m2", [128, 4096], F32)
                b3 = lambda ri: braw[:, ri].rearrange("p d m q -> p (d m) q")
                c3 = lambda ri: craw[:, ri].rearrange("p d m q -> p (d m) q")
                m13 = m1[:, 0:512].rearrange("p (a q) -> p a q", q=16)
                m23 = m2[:, 0:512].rearrange("p (a q) -> p a q", q=16)
                fr3, fi3 = bc(v(FR), 2, 16), bc(v(FI), 2, 16)
                S.op("dve", lambda h: h.tensor_tensor(out=m13, in0=fr3, in1=b3(0), op=ALU.mult), reads=["prm", "braw"], writes=["m1"])
                S.op("pool", lambda h: h.tensor_tensor(out=m23, in0=fi3, in1=b3(1), op=ALU.mult), reads=["prm", "braw"], writes=["m2"])
                S.op("dve", lambda h: h.tensor_tensor(out=bb[:, 0], in0=m13, in1=m23, op=ALU.subtract), reads=["m1", "m2"], writes=["bb"])
                S.op("dve", lambda h: h.tensor_tensor(out=m13, in0=fr3, in1=b3(1), op=ALU.mult), reads=["prm", "braw", "bb"], writes=["m1"])
                S.op("pool", lambda h: h.tensor_tensor(out=m23, in0=fi3, in1=b3(0), op=ALU.mult), reads=["prm", "braw", "bb"], writes=["m2"])
                S.op("dve", lambda h: h.tensor_tensor(out=bb[:, 1], in0=m13, in1=m23, op=ALU.add), reads=["m1", "m2"], writes=["bb"])
                pw3 = lambda idx: w3(pw[:, idx])[:, :, 0:8]
                pbi3 = w3(pbi[:])[:, :, 0:8]
                m14 = m1[:].rearrange("p (a s q) -> p a s q", s=8, q=16)
                m24 = m2[:].rearrange("p (a s q) -> p a s q", s=8, q=16)

                def cplx(dst, outkey, ar, ai, br, bi, neg_im):
                    A_r, A_i = bc(ar, 3, 16), bc(ai, 3, 16)
                    B_r, B_i = bc(br, 2, 8), bc(bi, 2, 8)
                    o_re = dst[:, :, :, 0, :].rearrange("p d m (s q) -> p (d m) s q", s=8)
                    o_im = dst[:, :, :, 1, :].rearrange("p d m (s q) -> p (d m) s q", s=8)
                    S.op("dve", lambda h: h.tensor_tensor(out=m14, in0=A_r, in1=B_r, op=ALU.mult), reads=["pw", "pbi", "bb", "craw", outkey], writes=["m1"])
                    S.op("pool", lambda h: h.tensor_tensor(out=m24, in0=A_i, in1=B_i, op=ALU.mult), reads=["pw", "pbi", "bb", "craw", outkey], writes=["m2"])
                    S.op("dve", lambda h: h.tensor_tensor(out=o_re, in0=m14, in1=m24, op=ALU.subtract), reads=["m1", "m2"], writes=[outkey])
                    S.op("dve", lambda h: h.tensor_tensor(out=m14, in0=A_r, in1=B_i, op=ALU.mult), reads=["pw", "pbi", "bb", "craw", outkey], writes=["m1"])
                    S.op("pool", lambda h: h.tensor_tensor(out=m24, in0=A_i, in1=B_r, op=ALU.mult), reads=["pw", "pbi", "bb", "craw", outkey], writes=["m2"])
                    if neg_im:
                        S.op("dve", lambda h: h.scalar_tensor_tensor(out=o_im, in0=m14, scalar=-1.0, in1=m24, op0=ALU.mult, op1=ALU.subtract), reads=["m1", "m2"], writes=[outkey])
                    else:
                        S.op("dve", lambda h: h.tensor_tensor(out=o_im, in0=m14, in1=m24, op=ALU.add), reads=["m1", "m2"], writes=[outkey])

                cplx(Bp, "Bp", pw3(PBR), pbi3, bb[:, 0], bb[:, 1], False)
                cplx(Cp, "Cp", pw3(PCR), pw3(PCI), c3(0), c3(1), True)
                if l == 0 and "BpCp" in self.tap_out:
                    S.op("dve", lambda h: h.tensor_copy(out=m1[:], in_=Bp[:].rearrange("p d m c x -> p (d m c x)")[:, 0:4096]), reads=["Bp", "m1"], writes=["m1"])
                    S.op("dve", lambda h: h.tensor_copy(out=m2[:], in_=Cp[:].rearrange("p d m c x -> p (d m c x)")[:, 0:4096]), reads=["Cp", "m2"], writes=["m2"])
                    S.dma("sp", self.tap_out["BpCp"][:, 0:4096], m1[:], reads=["m1"])
                    S.dma("sp", self.tap_out["BpCp"][:, 4096:8192], m2[:], reads=["m2"])
                    S.barrier()
                tk = [self.sbuf(sb_, "tk%d" % i, [128, 256], F32) for i in range(2)]
                t2k = [self.sbuf(sb_, "t2k%d" % i, [128, 128], F32) for i in range(2)]
                for g in range(32):
                    m, j = g // 2, g % 2
                    r0 = 64 * j
                    i = g % 2
                    bank, bk = self.next_ps()
                    for d in range(2):
                        for c2 in range(2):
                            self.mm(bank[:, d * 128:(d + 1) * 128], Bp[r0:r0 + 64, d, m, c2, :], Cp[r0:r0 + 64, d, m, c2, :], c2 == 0, c2 == 1, ["Bp", "Cp"], [bk])
                    S.op("dve", lambda h, bank=bank, i=i: h.tensor_tensor(out=tk[i][:], in0=bank[:, 0:256], in1=mfb[:], op=ALU.mult), reads=[bk, "mfb"], writes=[("tk", i)])
                    S.op("pool", lambda h, i=i: h.tensor_tensor(out=t2k[i][:], in0=tk[i][:, 0:128], in1=tk[i][:, 128:256], op=ALU.add), reads=[("tk", i)], writes=[("t2k", i)])
                    S.op("dve", lambda h, i=i, g=g: h.scalar_tensor_tensor(out=Kb[:, g, :], in0=self.identf[:], scalar=dcol[:, g:g + 1], in1=t2k[i][:], op0=ALU.mult, op1=ALU.add),
                         reads=[("t2k", i), "dcol", "identf"], writes=["Kb"])
                for d in range(2):
                    for mm4 in range(4):
                        bank, bk = self.next_pt()
                        for mi in range(4):
                            m = 4 * mm4 + mi
                            for c2 in range(2):
                                o0 = (mi * 2 + c2) * 128
                                self.tr(bank[:, o0:o0 + 128], Bp[:, d, m, c2, :], self.ident[:, :], ["Bp"], [bk])
                        self.evac(BpT[:, d, 4 * mm4:4 * mm4 + 4].rearrange("p m c x -> p (m c x)"), bank[:, :], [bk], ["BpT"])
                S.barrier()
            if self.stop_after == "ssmB":
                return
            with ExitStack() as sc:
                W_ = NCH + 2
                Rbf = self.sbuf(sc, "Rbf", [128, 2, 16, 2, W_], BF16)
                with ExitStack() as sc2:
                    RZ = self.sbuf(sc2, "RZ", [128, 2, 16, 2, W_], F32)
                    Wst = self.sbuf(sc2, "Wst", [128, 32, 2], F32)
                    P3 = self.sbuf(sc2, "P3", [128, 32, 2, 3], F32)
                    ztmp = [self.sbuf(sc2, "ztmp%d" % i, [128, 2, NCH], F32) for i in range(2)]
                    S.op("pool", lambda h: h.memset(P3[:], 0.0), writes=["P3"])
                    plane = 16 * 2 * W_

                    def rz_cols(colf, colb):
                        a = RZ[:, :, :, :, colf]
                        return bass.AP(tensor=a.tensor, offset=a.offset, ap=[list(a.ap[0]), [plane + colb - colf, 2], list(a.ap[2]), list(a.ap[3])])

                    S.op("pool", lambda h: h.memset(RZ[:, 0, :, :, 0:1], 0.0), writes=["RZ"])
                    S.op("pool", lambda h: h.memset(RZ[:, 1, :, :, W_ - 1:W_], 0.0), writes=["RZ"])
                    S.op("pool", lambda h: h.memset(Wst[:], 0.0), writes=["W"])
                    for d in range(2):
                        for m in range(16):
                            bre, kre = self.next_ps()
                            bim, kim = self.next_ps()
                            for j in range(2):
                                g = 2 * m + j
                                self.mm(bre[64 * j:64 * j + 64, 0:NCH], BpT[:, d, m, 0, 64 * j:64 * j + 64], UT[:, g, :], True, True, ["BpT", "UT"], [kre])
                                self.mm(bim[64 * j:64 * j + 64, 0:NCH], BpT[:, d, m, 1, 64 * j:64 * j + 64], UT[:, g, :], True, True, ["BpT", "UT"], [kim])
                            ar_ap = a8[:, 1, d, m, 0:1]
                            ai_ap = a8[:, 1, d, m, 1:2]
                            zi = (d * 16 + m) % 2
                            S.op("dve", lambda h, bim=bim, ai_ap=ai_ap, zi=zi: h.tensor_scalar(out=ztmp[zi][:, 0, :], in0=bim[:, 0:NCH], scalar1=ai_ap, scalar2=None, op0=ALU.mult),
                                 reads=[kim, "a8"], writes=[("ztmp", zi)])
                            S.op("dve", lambda h, bre=bre, ai_ap=ai_ap, zi=zi: h.tensor_scalar(out=ztmp[zi][:, 1, :], in0=bre[:, 0:NCH], scalar1=ai_ap, scalar2=None, op0=ALU.mult),
                                 reads=[kre, "a8"], writes=[("ztmp", zi)])
                            S.op("dve", lambda h, bre=bre, ar_ap=ar_ap, zi=zi, d=d, m=m: h.scalar_tensor_tensor(out=RZ[:, d, m, 0, 1:1 + NCH], in0=bre[:, 0:NCH], scalar=ar_ap, in1=ztmp[zi][:, 0, :],
                                                                                                          op0=ALU.mult, op1=ALU.subtract),
                                 reads=[kre, "a8", ("ztmp", zi)], writes=["RZ"])
                            S.op("dve", lambda h, bim=bim, ar_ap=ar_ap, zi=zi, d=d, m=m: h.scalar_tensor_tensor(out=RZ[:, d, m, 1, 1:1 + NCH], in0=bim[:, 0:NCH], scalar=ar_ap, in1=ztmp[zi][:, 1, :],
                                                                                                          op0=ALU.mult, op1=ALU.add),
                                 reads=[kim, "a8", ("ztmp", zi)], writes=["RZ"])
                    CA = a8[:, 0].rearrange("p d m c -> p (d m) c")
                    CB = a8[:, 1].rearrange("p d m c -> p (d m) c")
                    wnat = Wst[:]
                    wrev = bass.AP(tensor=wnat.tensor, offset=wnat.offset + 1, ap=[list(wnat.ap[0]), list(wnat.ap[1]), [-1, 2]])
                    wst3 = Wst[:].rearrange("p (d m) c -> p d m c", d=2)
                    p3z = P3[:, :, :, 2].rearrange("p (d m) c -> p d m c", d=2)
                    AXX = mybir.AxisListType.X
                    prev_out = None
                    for k in range(NCH - 1):
                        zc = rz_cols(k + 1, NCH - k)
                        S.op("dve", lambda h, zc=zc: h.tensor_copy(out=p3z, in_=zc), reads=["RZ"], writes=["P3z"])
                        S.op("dve", lambda h: h.tensor_tensor(out=P3[:, :, 0, 0:2], in0=wnat, in1=CA, op=ALU.mult), reads=["W", "a8", "P3"], writes=["P3a"])
                        S.op("dve", lambda h: h.tensor_tensor(out=P3[:, :, 1, 0:2], in0=wrev, in1=CB, op=ALU.mult), reads=["W", "a8", "P3"], writes=["P3a"])
                        if prev_out is not None:
                            S.op("dve", lambda h, po=prev_out: h.tensor_copy(out=po, in_=wst3), reads=["W"], writes=["RZo"])
                        S.op("dve", lambda h: h.tensor_reduce(out=Wst[:], in_=P3[:], axis=AXX, op=ALU.add), reads=["P3a", "P3z"], writes=["W"])
                        prev_out = zc
                    S.op("dve", lambda h, po=prev_out: h.tensor_copy(out=po, in_=wst3), reads=["W"], writes=["RZo"])
                    cengs = ("act", "dve", "pool", "act")
                    for d in range(2):
                        for mh in range(2):
                            self.evac(Rbf[:, d, 8 * mh:8 * mh + 8].rearrange("p m c x -> p (m c x)"), RZ[:, d, 8 * mh:8 * mh + 8].rearrange("p m c x -> p (m c x)"),
                                      ["RZ", "RZo"], ["Rbf"], eng=cengs[2 * d + mh])
                    S.barrier()
                if self.stop_after == "ssmC" and "Rbf" not in self.tap_out:
                    return
                if l == 0 and "Rbf" in self.tap_out:
                    with ExitStack() as sc3:
                        tmp = self.sbuf(sc3, "tmp", [128, 2 * 16 * 2 * (NCH + 2)], F32)
                        S.op("dve", lambda h: h.tensor_copy(out=tmp[:], in_=Rbf[:].rearrange("p d m c x -> p (d m c x)")), reads=["Rbf"], writes=["tmp"])
                        self.tap("Rbf", tmp[:], ["tmp"])
                        S.barrier()
                    if self.stop_after == "ssmC":
                        return
                with ExitStack() as sd:
                    zcm = [self.sbuf(sd, "zcm%d" % i, [128, 4096], BF16) for i in range(3)]
                    for ct in range(3):
                        n = 128 if ct < 2 else 16
                        c0 = ct * 128
                        for g4 in range(8):
                            bank, bk = self.next_ps()
                            for gi in range(4):
                                g = 4 * g4 + gi
                                m, j = g // 2, g % 2
                                r0 = 64 * j
                                o_ap = bank[0:n, gi * 128:(gi + 1) * 128]
                                self.mm(o_ap, UT[:, g, c0:c0 + n], Kb[:, g, :], True, False, ["Kb", "UT"], [bk])
                                for d in range(2):
                                    off = 0 if d == 0 else 2
                                    for c2 in range(2):
                                        self.mm(o_ap, Rbf[r0:r0 + 64, d, m, c2, off + c0:off + c0 + n], Cp[r0:r0 + 64, d, m, c2, :], False, (d == 1 and c2 == 1), ["Cp", "Rbf"], [bk])
                            dst = zcm[ct][0:n, :].rearrange("c (t g p) -> c g t p", t=8, g=32, p=16)[:, 4 * g4:4 * g4 + 4]
                            src = bank[0:n, :].rearrange("c (g t p) -> c g t p", g=4, t=8, p=16)
                            S.op("act", lambda h, dst=dst, src=src: h.activation(out=dst, in_=src, func=AF.Gelu), reads=[bk], writes=[("zcm", ct)])
                    for ct in range(3):
                        if self.stop_after in ("ssmD1", "ssmD2"):
                            continue
                        n = 128 if ct < 2 else 16
                        S.dma("sp", self.z_d.rearrange("(c s) x -> c (s x)", s=8)[ct * 128:ct * 128 + n, :], zcm[ct][0:n, :], reads=[("zcm", ct)], writes=["zd"])
                    S.barrier()

    def phase_ssm_glu(self, l, yTs):
        S = self.S
        with ExitStack() as st:
            zT = self.sbuf(st, "zT", [128, 4, LP], BF16)
            ztm = [self.sbuf(st, "ztm%d" % i, [128, 512], BF16) for i in range(2)]
            wgl = self.sbuf(st, "wgl", [128, 4, 512], BF16)
            glb = self.sbuf(st, "glb", [128, 4], F32)
            sgs = [self.sbuf(st, "sgs%d" % i, [128, 512], BF16) for i in range(2)]
            self.load_w(wgl[:], "ssm_glu_w", l * 512 * 512, 0, 4, 512, 0, 512, ["wgl"])
            S.dma("sp", glb[:], AP(self.T["ssm_glu_b"], l * 512, [[1, 128], [128, 4]]), writes=["glb"], allow_slow_non_contiguous=True)
            for b in range(NT):
                i = b % 2
                S.dma("sp", ztm[i][:], self.z_d[b * 128:(b + 1) * 128, :], reads=["zd"], writes=[("ztm", i)])
                bank, bk = self.next_pt()
                for j in range(4):
                    self.tr(bank[:, j * 128:(j + 1) * 128], ztm[i][:, j * 128:(j + 1) * 128], self.ident[:, :], [("ztm", i)], [bk])
                self.evac(zT[:, :, b * 128:(b + 1) * 128], bank[:, 0:512].rearrange("p (j t) -> p j t", j=4), [bk], ["zT"])
            cnt = 0
            for co in range(4):
                for (t0, n) in TCH:
                    bank, bk = self.next_ps()
                    for k in range(4):
                        self.mm(bank[:, 0:n], wgl[:, k, co * 128:(co + 1) * 128], zT[:, k, t0:t0 + n], k == 0, k == 3, ["wgl", "zT"], [bk])
                    i = cnt % 2
                    cnt += 1
                    S.op("act", lambda h, bank=bank, n=n, i=i, co=co: h.activation(out=sgs[i][:, 0:n], in_=bank[:, 0:n], func=AF.Sigmoid, bias=glb[:, co:co + 1]),
                         reads=[bk, "glb"], writes=[("sgs", i)])
                    S.op(self.ew(), lambda h, n=n, i=i, co=co, t0=t0: h.tensor_tensor(out=yTs[:, co, t0:t0 + n], in0=zT[:, co, t0:t0 + n], in1=sgs[i][:, 0:n], op=ALU.mult),
                         reads=[("sgs", i), "zT"], writes=["yT_all"])
            S.barrier()
        self.tap_yT("yTs", yTs, l)


def build_nc(nlayers=NL, taps=(), stop_after=None):
    return K(nlayers=nlayers, taps=taps, stop_after=stop_after).build()


def make_in_maps(inputs):
    consts = make_consts()
    shared = {k: np.ascontiguousarray(np.asarray(v), dtype=np.float32) for k, v in inputs.items() if k != "x"}
    x = np.asarray(inputs["x"], dtype=np.float32)
    maps = []
    for c in range(8):
        m = dict(shared)
        m["x"] = np.ascontiguousarray(x[c])
        m.update(consts)
        maps.append(m)
    return maps


def kernel(**inputs):
    nc = build_nc()
    res = run_bass_kernel_spmd(nc, make_in_maps(inputs), core_ids=list(range(8)))
    return np.stack([np.asarray(r["out"], dtype=np.float32) for r in res.results], axis=0)
```
